# Optimizing a Trainium2 kernel written in Bass

```python
import math
import jax, jax.numpy as jnp
from jax import lax
import numpy as np

D_MODEL = 2048
BATCH = 4
SEQ = 2048
DEPTH = 1
DEC_BATCH = 128
DEC_SEQ = 4
PAST_LEN = 16384
PAGE_SIZE = 128

EXPAND = 2
D_MIX = EXPAND * D_MODEL
D_RET = D_MIX // 2
D_RWKV = D_MIX - D_RET
RET_HEADS = 8
RET_HEAD_DIM = D_RET // RET_HEADS
RWKV_HEAD_DIM = 64
RWKV_HEADS = D_RWKV // RWKV_HEAD_DIM
DECAY_LORA = max(32, int(round(1.8 * math.sqrt(D_RWKV) / 32)) * 32)
A_LORA = DECAY_LORA
RET_CHUNK = 128
ROPE_THETA = 10000.0
NORM_EPS = 1e-6
GN_EPS = 1e-5
RWKV_GN_EPS = 64e-5

RET_Q = 0
RET_K = RET_Q + D_RET
RET_V = RET_K + D_RET
RET_G = RET_V + D_RET
RW_R = RET_G + D_RET
RW_K = RW_R + D_RWKV
RW_V = RW_K + D_RWKV
RW_G = RW_V + D_RWKV
RW_WD = RW_G + D_RWKV
RW_AD = RW_WD + DECAY_LORA
N_IN = RW_AD + A_LORA
N_SHIFT = 3 * D_RWKV + DECAY_LORA + A_LORA

kernel_name = 'hymba_retention_rwkv7_adaln_step'


def rms_norm(x, w):
    xf = x.astype(jnp.float32)
    return xf * lax.rsqrt(jnp.mean(xf * xf, axis=-1, keepdims=True) + NORM_EPS) * w.astype(jnp.float32)


def group_norm(x, eps):
    mean = jnp.mean(x, axis=-1, keepdims=True)
    xc = x - mean
    var = jnp.mean(xc * xc, axis=-1, keepdims=True)
    return xc * lax.rsqrt(var + eps)


def rotary(x, pos):
    half = x.shape[-1] // 2
    inv_freq = ROPE_THETA ** (-jnp.arange(half, dtype=jnp.float32) / half)
    ang = pos[:, None] * inv_freq[None, :]
    cos = jnp.cos(ang)[None, :, None, :]
    sin = jnp.sin(ang)[None, :, None, :]
    x1, x2 = x[..., :half], x[..., half:]
    return jnp.concatenate([x1 * cos - x2 * sin, x1 * sin + x2 * cos], axis=-1)


def retention(q, k, v, s0):
    b, l, h, d = q.shape
    chunk = math.gcd(l, RET_CHUNK)
    n = l // chunk
    lg = jnp.log1p(-jnp.exp2(-5.0 - jnp.arange(h, dtype=jnp.float32)))
    idx = jnp.arange(chunk, dtype=jnp.float32)
    diff = idx[:, None] - idx[None, :]
    dmask = jnp.where(diff[None] >= 0, jnp.exp(jnp.maximum(diff, 0.0)[None] * lg[:, None, None]), 0.0)
    cross_decay = jnp.exp((idx[None, :] + 1.0) * lg[:, None])
    key_decay = jnp.exp((chunk - 1.0 - idx[None, :]) * lg[:, None])
    chunk_decay = jnp.exp(chunk * lg)

    def to_chunks(t):
        return t.reshape(b, n, chunk, h, d).transpose(1, 0, 3, 2, 4)

    def step(s, inp):
        qc, kc, vc = inp
        scores = jnp.einsum('bhid,bhjd->bhij', qc, kc) * dmask[None]
        inner = jnp.einsum('bhij,bhjd->bhid', scores, vc)
        cross = jnp.einsum('bhid,bhde->bhie', qc, s) * cross_decay[None, :, :, None]
        s = s * chunk_decay[None, :, None, None] + jnp.einsum('bhjd,bhje->bhde', kc * key_decay[None, :, :, None], vc)
        return s, inner + cross

    s, out = lax.scan(step, s0, (to_chunks(q), to_chunks(k), to_chunks(v)))
    out = out.transpose(1, 0, 3, 2, 4).reshape(b, l, h, d)
    return out, s


def rwkv7_recurrence(r, w, k, v, kk, a, s0):
    def step(s, inp):
        r_t, w_t, k_t, v_t, kk_t, a_t = inp
        sa = jnp.einsum('bhvk,bhk->bhv', s, -kk_t)
        s = (s * w_t[:, :, None, :] + sa[..., None] * (kk_t * a_t)[:, :, None, :]
             + v_t[..., None] * k_t[:, :, None, :])
        return s, jnp.einsum('bhvk,bhk->bhv', s, r_t)

    xs = (jnp.swapaxes(r, 0, 1), jnp.swapaxes(w, 0, 1), jnp.swapaxes(k, 0, 1),
          jnp.swapaxes(v, 0, 1), jnp.swapaxes(kk, 0, 1), jnp.swapaxes(a, 0, 1))
    s, out = lax.scan(step, s0, xs)
    return jnp.swapaxes(out, 0, 1), s


def shift_cols(t):
    return jnp.concatenate([t[..., RW_R:RW_G], t[..., RW_WD:N_IN]], axis=-1)


def hybrid_layer(x, c, s_ret, s_rwkv, h_prev, pos0, norm_w, w_ada, b_ada, w_in, mu_shift,
                 w0_decay, w2_decay, a0, a2, k_k, k_a, r_k, ln_x_w, ln_x_b, w_out):
    f32 = jnp.float32
    b, l, _ = x.shape
    mod = jax.nn.silu(c.astype(f32)) @ w_ada.astype(f32) + b_ada.astype(f32)
    shift, scale, gate = jnp.split(mod, 3, axis=-1)
    h = (rms_norm(x, norm_w) * (1.0 + scale[:, None]) + shift[:, None]).astype(x.dtype)
    proj = (h @ w_in).astype(f32)

    pos = pos0 + jnp.arange(l, dtype=f32)
    q = rotary(proj[..., RET_Q:RET_K].reshape(b, l, RET_HEADS, RET_HEAD_DIM), pos)
    kr = rotary(proj[..., RET_K:RET_V].reshape(b, l, RET_HEADS, RET_HEAD_DIM), pos) * (RET_HEAD_DIM ** -0.5)
    vr = proj[..., RET_V:RET_G].reshape(b, l, RET_HEADS, RET_HEAD_DIM)
    o_ret, s_ret_new = retention(q, kr, vr, s_ret.astype(f32))
    y_ret = group_norm(o_ret, GN_EPS).reshape(b, l, D_RET) * jax.nn.silu(proj[..., RET_G:RW_R])

    cur = shift_cols(proj)
    prev_first = shift_cols((h_prev.astype(x.dtype) @ w_in).astype(f32))[:, None]
    prev = jnp.concatenate([prev_first, cur[:, :-1]], axis=1)
    mix = cur + (prev - cur) * mu_shift.astype(f32)
    r = mix[..., 0:D_RWKV]
    kw = mix[..., D_RWKV:2 * D_RWKV]
    vw = mix[..., 2 * D_RWKV:3 * D_RWKV]
    wd = mix[..., 3 * D_RWKV:3 * D_RWKV + DECAY_LORA]
    ad = mix[..., 3 * D_RWKV + DECAY_LORA:]
    w_log = -jax.nn.softplus(-(w0_decay.astype(f32) + jnp.tanh(wd) @ w2_decay.astype(f32))) - 0.5
    decay = jnp.exp(-jnp.exp(w_log))
    a = jax.nn.sigmoid(a0.astype(f32) + ad @ a2.astype(f32))

    def heads(t):
        return t.reshape(b, l, RWKV_HEADS, RWKV_HEAD_DIM)

    kk = heads(kw * k_k.astype(f32))
    kk = kk / jnp.maximum(jnp.sqrt(jnp.sum(kk * kk, axis=-1, keepdims=True)), 1e-12)
    kw = kw * (1.0 + (a - 1.0) * k_a.astype(f32))
    r_h, k_h, v_h = heads(r), heads(kw), heads(vw)
    o_rw, s_rwkv_new = rwkv7_recurrence(r_h, heads(decay), k_h, v_h, kk, heads(a), s_rwkv.astype(f32))
    o_rw = group_norm(o_rw, RWKV_GN_EPS) * heads(ln_x_w.astype(f32))[0, 0] + heads(ln_x_b.astype(f32))[0, 0] \
        if False else group_norm(o_rw, RWKV_GN_EPS) * ln_x_w.astype(f32).reshape(RWKV_HEADS, RWKV_HEAD_DIM) \
        + ln_x_b.astype(f32).reshape(RWKV_HEADS, RWKV_HEAD_DIM)
    o_rw = o_rw + jnp.sum(r_h * k_h * r_k.astype(f32), axis=-1, keepdims=True) * v_h
    y_rw = o_rw.reshape(b, l, D_RWKV) * jax.nn.silu(proj[..., RW_G:RW_WD])

    y = (jnp.concatenate([y_ret, y_rw], axis=-1).astype(x.dtype) @ w_out).astype(f32)
    x_new = (x.astype(f32) + gate[:, None] * y).astype(x.dtype)
    return x_new, s_ret_new, s_rwkv_new, h[:, -1]


def setup_inputs(seed: int = 0) -> dict:
    key = jax.random.key(seed)
    ks = jax.random.split(key, 24)
    f32 = jnp.float32

    def nrm(k, shape, s):
        return jax.random.normal(k, shape, f32) * s

    ramp = (jnp.arange(D_RWKV, dtype=f32) / (D_RWKV - 1)) ** 0.9
    return {
        'x_prompt': nrm(ks[0], (BATCH, SEQ, D_MODEL), 1.0),
        'x_sample': nrm(ks[1], (DEC_BATCH, DEC_SEQ, D_MODEL), 1.0),
        'c_prompt': nrm(ks[2], (BATCH, D_MODEL), 1.0),
        'c_sample': nrm(ks[3], (DEC_BATCH, D_MODEL), 1.0),
        'state_ret': nrm(ks[4], (DEPTH, DEC_BATCH, RET_HEADS, RET_HEAD_DIM, RET_HEAD_DIM), 0.3),
        'state_rwkv': nrm(ks[5], (DEPTH, DEC_BATCH, RWKV_HEADS, RWKV_HEAD_DIM, RWKV_HEAD_DIM), 0.3),
        'state_shift': nrm(ks[6], (DEPTH, DEC_BATCH, D_MODEL), 1.0),
        'norm_w': 1.0 + nrm(ks[7], (DEPTH, D_MODEL), 0.02),
        'w_ada': nrm(ks[8], (DEPTH, D_MODEL, 3 * D_MODEL), 0.3 * D_MODEL ** -0.5),
        'b_ada': nrm(ks[9], (DEPTH, 3 * D_MODEL), 0.02),
        'w_in': nrm(ks[10], (DEPTH, D_MODEL, N_IN), D_MODEL ** -0.5),
        'mu_shift': jax.random.uniform(ks[11], (DEPTH, N_SHIFT), f32),
        'w0_decay': -5.5 + 5.0 * ramp + nrm(ks[12], (DEPTH, D_RWKV), 0.1),
        'w2_decay': nrm(ks[13], (DEPTH, DECAY_LORA, D_RWKV), 0.5 * DECAY_LORA ** -0.5),
        'a0': nrm(ks[14], (DEPTH, D_RWKV), 0.1),
        'a2': nrm(ks[15], (DEPTH, A_LORA, D_RWKV), A_LORA ** -0.5),
        'k_k': 0.85 + nrm(ks[16], (DEPTH, D_RWKV), 0.02),
        'k_a': 1.0 + nrm(ks[17], (DEPTH, D_RWKV), 0.02),
        'r_k': nrm(ks[18], (DEPTH, RWKV_HEADS, RWKV_HEAD_DIM), 0.1),
        'ln_x_w': 1.0 + nrm(ks[19], (DEPTH, D_RWKV), 0.02),
        'ln_x_b': nrm(ks[20], (DEPTH, D_RWKV), 0.02),
        'w_out': nrm(ks[21], (DEPTH, D_MIX, D_MODEL), D_MIX ** -0.5),
        'final_norm_w': 1.0 + nrm(ks[22], (D_MODEL,), 0.02),
    }


def reference(x_prompt, x_sample, c_prompt, c_sample, state_ret, state_rwkv, state_shift,
              norm_w, w_ada, b_ada, w_in, mu_shift, w0_decay, w2_decay, a0, a2, k_k, k_a,
              r_k, ln_x_w, ln_x_b, w_out, final_norm_w):
    bp = x_prompt.shape[0]
    zero_ret = jnp.zeros((bp, RET_HEADS, RET_HEAD_DIM, RET_HEAD_DIM), jnp.float32)
    zero_rwkv = jnp.zeros((bp, RWKV_HEADS, RWKV_HEAD_DIM, RWKV_HEAD_DIM), jnp.float32)
    zero_shift = jnp.zeros((bp, D_MODEL), x_prompt.dtype)
    hp, hs = x_prompt, x_sample
    ret_p, rwkv_p, shift_p, ret_s, rwkv_s, shift_s = [], [], [], [], [], []
    for layer in range(DEPTH):
        params = (norm_w[layer], w_ada[layer], b_ada[layer], w_in[layer], mu_shift[layer],
                  w0_decay[layer], w2_decay[layer], a0[layer], a2[layer], k_k[layer], k_a[layer],
                  r_k[layer], ln_x_w[layer], ln_x_b[layer], w_out[layer])
        hp, sr, sw, sh = hybrid_layer(hp, c_prompt, zero_ret, zero_rwkv, zero_shift, 0.0, *params)
        ret_p.append(sr.astype(state_ret.dtype))
        rwkv_p.append(sw.astype(state_rwkv.dtype))
        shift_p.append(sh.astype(state_shift.dtype))
        hs, sr, sw, sh = hybrid_layer(hs, c_sample, state_ret[layer], state_rwkv[layer],
                                      state_shift[layer], float(PAST_LEN), *params)
        ret_s.append(sr.astype(state_ret.dtype))
        rwkv_s.append(sw.astype(state_rwkv.dtype))
        shift_s.append(sh.astype(state_shift.dtype))
    y_prompt = rms_norm(hp, final_norm_w).astype(x_prompt.dtype)
    y_sample = rms_norm(hs, final_norm_w).astype(x_sample.dtype)
    return (y_prompt, y_sample, jnp.stack(ret_p), jnp.stack(rwkv_p), jnp.stack(shift_p),
            jnp.stack(ret_s), jnp.stack(rwkv_s), jnp.stack(shift_s))
```

```python
import math
import numpy as np
from contextlib import ExitStack
import concourse.bass as bass
import concourse.mybir as mybir
from concourse.bass_utils import run_bass_kernel_spmd

F32 = mybir.dt.float32
BF16 = mybir.dt.bfloat16
AF = mybir.ActivationFunctionType
ALU = mybir.AluOpType
AX = mybir.AxisListType

D = 2048
NIN = 16576
T = 2048
NS = 16
TSM = 64
KC = 16
STAGE = 99
DEBUG = False
RW_PAIRS = 16
RW_SAMPLE = True


class Buf:
    __slots__ = ("w", "r")

    def __init__(self):
        self.w = None
        self.r = {}


class Eng:
    def __init__(self, nc, eng, key, sems, is_pe=False):
        self.eng = eng
        self.key = key
        self.sem = sems[key]
        self.cnt = 0
        self.waited = {}
        self.is_pe = is_pe
        self.slots = []
        self.uses = []
        self.n = 0


class K:
    def __init__(self, nc, es):
        self.nc = nc
        self.es = es
        self.sems = {}
        for k in ["pe", "act", "dve", "pool", "sp"]:
            self.sems[k] = es.enter_context(nc.semaphore("s_" + k))
        self.pe = Eng(nc, nc.tensor, "pe", self.sems, True)
        self.act = Eng(nc, nc.scalar, "act", self.sems)
        self.dve = Eng(nc, nc.vector, "dve", self.sems)
        self.pool = Eng(nc, nc.gpsimd, "pool", self.sems)
        self.sp = Eng(nc, nc.sync, "sp", self.sems)
        for q, n in ((self.sp, 8), (self.pool, 8), (self.act, 4)):
            for i in range(n):
                k = "%s_d%d" % (q.key, i)
                self.sems[k] = es.enter_context(nc.semaphore("s_" + k))
                q.slots.append(k)
                q.uses.append(0)
        self.engs = [self.pe, self.act, self.dve, self.pool, self.sp]
        self.nps = 0
        self.ps = []
        self.psb = []
        self.evq = 0
        self.held = set()

    def _deps(self, en, rd, wr, extra=None):
        deps = dict(extra or {})
        for b in rd:
            if b.w:
                k, v = b.w
                deps[k] = max(deps.get(k, 0), v)
        for b in wr:
            if b.w:
                k, v = b.w
                deps[k] = max(deps.get(k, 0), v)
            for k, v in b.r.items():
                deps[k] = max(deps.get(k, 0), v)
        for k, v in deps.items():
            if en.is_pe and k == en.key:
                continue
            if en.waited.get(k, 0) < v:
                en.eng.wait_ge(self.sems[k], v)
                en.waited[k] = v

    def _post(self, ev, rd, wr):
        k, v = ev
        for b in rd:
            b.r[k] = max(b.r.get(k, 0), v)
        for b in wr:
            b.w = ev
            b.r = {}

    def op(self, en, fn, rd=(), wr=()):
        self._deps(en, rd, wr)
        ins = fn(en.eng)
        en.cnt += 1
        ins.then_inc(en.sem, 1)
        self._post((en.key, en.cnt), rd, wr)

    def dma(self, q, out, in_, rd=(), wr=()):
        s = q.n % len(q.slots)
        q.n += 1
        k = q.slots[s]
        q.uses[s] += 1
        u = q.uses[s]
        extra = {k: 16 * (u - 1)} if u > 1 else None
        self._deps(q, rd, wr, extra)
        ins = q.eng.dma_start(out=out, in_=in_)
        ins.then_inc(self.sems[k], 16)
        self._post((k, 16 * u), rd, wr)

    def barrier(self):
        tgt = {}
        for e in self.engs:
            if e.cnt:
                tgt[e.key] = e.cnt
            for k, u in zip(e.slots, e.uses):
                if u:
                    tgt[k] = 16 * u
        for e in self.engs:
            for k, v in tgt.items():
                if k == e.key:
                    continue
                if e.waited.get(k, 0) < v:
                    e.eng.wait_ge(self.sems[k], v)
                    e.waited[k] = v

    def finish(self):
        self.barrier()

    def psum(self, exclude=()):
        while True:
            i = self.nps % 8
            self.nps += 1
            if self.psb[i] not in exclude and self.psb[i] not in self.held:
                return self.ps[i], self.psb[i]

    def ev_eng(self):
        self.evq += 1
        return self.dve if self.evq % 2 else self.act


def build(stage=STAGE, debug=DEBUG):
    nc = bass.Bass("TRN2", target_bir_lowering=False)

    def din(name, shape, dt=F32):
        return nc.dram_tensor(name, list(shape), dt, kind="ExternalInput").ap()

    def dout(name, shape, dt=F32):
        return nc.dram_tensor(name, list(shape), dt, kind="ExternalOutput").ap()

    xT = din("xT", [128, KC, T])
    x = din("x", [T, D])
    xsT = din("xsT", [128, KC, TSM])
    xs = din("xs", [TSM, D])
    hpT = din("hpT", [128, KC, NS])
    cT = din("cT", [128, KC, 17])
    sret = din("sret", [NS, 8, 256, 256])
    srwT = din("srwT", [NS, 32, 64, 64])
    w_ada = din("w_ada", [D, 3 * D])
    b_adaT = din("b_adaT", [128, 48])
    w_in = din("w_in", [D, NIN])
    w_out = din("w_out", [2 * D, D])
    norm_wT = din("norm_wT", [128, KC])
    fnw_b = din("fnw_b", [128, D])
    rwp2 = din("rwp2", [128, 10, 16])
    bones = din("bones", [128, 128])
    rwp = din("rwp", [64, 10, 32])
    mu_lora = din("mu_lora", [96, 2])
    w2d = din("w2d", [96, D])
    a2 = din("a2", [96, D])
    ident = din("ident", [128, 128])
    cmask = din("cmask", [128, 6, 128])
    seqmask = din("seqmask", [64, NS])
    seqmaskT = din("seqmaskT", [128, NS, 64])
    rope = din("rope", [128, 4, T])
    ropes = din("ropes", [128, 2, TSM])
    rdec = din("rdec", [128, 8, 4, 128])
    ones = din("ones", [128, 128])
    selp = din("selp", [17, 128])
    sels = din("sels", [17, 64])

    y = dout("y", [T, D])
    ys = dout("ys", [TSM, D])
    nret_p = dout("nret_p", [8, 256, 256])
    nrw_pT = dout("nrw_pT", [32, 64, 64])
    nshift_p = dout("nshift_p", [128, KC])
    nret_s = dout("nret_s", [NS, 8, 256, 256])
    nrw_sT = dout("nrw_sT", [NS, 32, 64, 64])
    nshift_s = dout("nshift_s", [128, KC, NS])
    xnew_d = nc.dram_tensor("xnew_d", [T + TSM, D], F32, kind="Internal").ap()
    ymix_d = nc.dram_tensor("ymix_d", [32, 128, T + TSM], BF16, kind="Internal").ap()
    dbg = {}
    if debug:
        dbg["hT"] = dout("dbg_hT", [128, KC, T + 80])
        dbg["modT"] = dout("dbg_modT", [128, 48, 17])

    with ExitStack() as es:
        k = K(nc, es)

        _cnt = [0]

        def sb(name, shape, dt=F32, stack=es):
            _cnt[0] += 1
            return stack.enter_context(nc.sbuf_tensor("%s_%d" % (name, _cnt[0]), list(shape), dt))

        for i in range(8):
            k.ps.append(es.enter_context(nc.psum_tensor("ps%d" % i, [128, 512], F32)))
            k.psb.append(Buf())

        identf = sb("identf", [128, 128]); identb = sb("identb", [128, 128], BF16)
        onesf = sb("onesf", [128, 128])
        cm = sb("cm", [128, 6, 128])
        smk = sb("smk", [64, NS]); smkT = sb("smkT", [128, NS, 64])
        nwT = sb("nwT", [128, KC]); baT = sb("baT", [128, 48])
        rw = sb("rw", [64, 10, 32]); mul = sb("mul", [96, 2])
        cTs = sb("cTs", [128, KC, 17])
        modT = sb("modT", [128, 48, 17])
        gT = sb("gT", [128, KC, 17]); shT = sb("shT", [128, KC, 17])
        hst = ExitStack()
        hlast = sb("hlast", [128, KC], F32, hst); hlast_s = sb("hlast_s", [128, KC, NS], F32, hst)
        hT = sb("hT", [128, KC, T], BF16, hst)
        hTs = sb("hTs", [128, KC, 80], BF16, hst)
        B = {n: Buf() for n in ["const", "cTs", "modT", "gsh", "hT", "hTs", "misc"]}
        for dst, src in ((identf, ident), (onesf, ones), (cm, cmask), (smk, seqmask), (smkT, seqmaskT),
                         (nwT, norm_wT), (baT, b_adaT), (rw, rwp), (mul, mu_lora)):
            k.dma(k.sp, dst[:], src, wr=[B["const"]])
        k.dma(k.sp, cTs[:], cT, wr=[B["cTs"]])
        k.op(k.dve, lambda e: e.tensor_copy(out=identb[:], in_=identf[:]), rd=[B["const"]], wr=[B["const"]])
        k.op(k.act, lambda e: e.activation(out=cTs[:], in_=cTs[:], func=AF.Silu), rd=[B["cTs"]], wr=[B["cTs"]])

        with ExitStack() as ph:
            wa = [sb("wa%d" % i, [128, KC, 512], F32, ph) for i in range(2)]
            wab = [Buf(), Buf()]
            for g in range(12):
                i = g % 2
                k.dma(k.act if g % 2 else k.sp, wa[i][:], w_ada[:, g * 512:(g + 1) * 512].rearrange("(c p) n -> p c n", p=128), wr=[wab[i]])
                for f in range(4):
                    ps, pb = k.psum()
                    for c in range(KC):
                        k.op(k.pe, lambda e, c=c, f=f, i=i, ps=ps: e.matmul(ps[:, 0:17], lhsT=wa[i][:, c, f * 128:(f + 1) * 128],
                                                                           rhs=cTs[:, c, :], start=(c == 0), stop=(c == KC - 1)),
                             rd=[wab[i], B["cTs"]], wr=[pb])
                    fc = g * 4 + f
                    k.op(k.dve, lambda e, ps=ps, fc=fc: e.tensor_scalar(out=modT[:, fc, :], in0=ps[:, 0:17], scalar1=baT[:, fc:fc + 1],
                                                                      scalar2=None, op0=ALU.add), rd=[pb, B["const"]], wr=[B["modT"]])
            k.op(k.dve, lambda e: e.tensor_scalar(out=gT[:], in0=modT[:, 16:32, :], scalar1=1.0, scalar2=None, op0=ALU.add),
                 rd=[B["modT"]], wr=[B["gsh"]])
            k.op(k.dve, lambda e: e.tensor_tensor(out=gT[:], in0=gT[:], in1=nwT[:].unsqueeze(2).to_broadcast([128, KC, 17]), op=ALU.mult),
                 rd=[B["gsh"], B["const"]], wr=[B["gsh"]])
            k.op(k.dve, lambda e: e.tensor_copy(out=shT[:], in_=modT[:, 0:16, :]), rd=[B["modT"]], wr=[B["gsh"]])
            k.barrier()
        if debug:
            k.dma(k.pool, dbg["modT"], modT[:], rd=[B["modT"]])

        with ExitStack() as ph:
            xt = [sb("xt%d" % i, [128, KC, 512], F32, ph) for i in range(2)]
            xtb = [Buf(), Buf()]
            sq = sb("sq", [128, 512], F32, ph); sqb = Buf()
            rstd = sb("rstd", [128, 512], F32, ph); rsb = Buf()
            tmp = [sb("tmpn%d" % i, [128, 512], F32, ph) for i in range(2)]
            tmpb = [Buf(), Buf()]
            groups = [(xT, tg * 512, 512, tg) for tg in range(4)] + [(xsT, 0, TSM, 4)]
            for (src, t0, n, gi) in groups:
                i = gi % 2
                k.dma(k.sp, xt[i][:, :, 0:n], src[:, :, t0:t0 + n], wr=[xtb[i]])
                ps, pb = k.psum()
                for c in range(KC):
                    j2 = c % 2
                    k.op(k.act, lambda e, c=c, i=i, n=n, j2=j2: e.activation(out=tmp[j2][:, 0:n], in_=xt[i][:, c, 0:n], func=AF.Square),
                         rd=[xtb[i]], wr=[tmpb[j2]])
                    if c == 0:
                        k.op(k.dve, lambda e, n=n, j2=j2: e.tensor_copy(out=sq[:, 0:n], in_=tmp[j2][:, 0:n]), rd=[tmpb[j2]], wr=[sqb])
                    else:
                        k.op(k.dve, lambda e, n=n, j2=j2: e.tensor_tensor(out=sq[:, 0:n], in0=sq[:, 0:n], in1=tmp[j2][:, 0:n], op=ALU.add), rd=[tmpb[j2], sqb], wr=[sqb])
                k.op(k.pe, lambda e, ps=ps, n=n: e.matmul(ps[:, 0:n], lhsT=onesf[:, :], rhs=sq[:, 0:n], start=True, stop=True),
                     rd=[sqb, B["const"]], wr=[pb])
                k.op(k.dve, lambda e, ps=ps, n=n: e.tensor_scalar(out=rstd[:, 0:n], in0=ps[:, 0:n], scalar1=1.0 / D, scalar2=1e-6,
                                                              op0=ALU.mult, op1=ALU.add), rd=[pb], wr=[rsb])
                k.op(k.act, lambda e, n=n: e.activation(out=rstd[:, 0:n], in_=rstd[:, 0:n], func=AF.Sqrt), rd=[rsb], wr=[rsb])
                k.op(k.dve, lambda e, n=n: e.reciprocal(out=rstd[:, 0:n], in_=rstd[:, 0:n]), rd=[rsb], wr=[rsb])
                for c in range(KC):
                    j = c % 2
                    k.op(k.dve, lambda e, c=c, i=i, j=j, n=n: e.tensor_tensor(out=tmp[j][:, 0:n], in0=xt[i][:, c, 0:n], in1=rstd[:, 0:n], op=ALU.mult),
                         rd=[xtb[i], rsb], wr=[tmpb[j]])
                    if gi < 4:
                        k.op(k.act, lambda e, c=c, j=j, t0=t0, n=n: e.activation(out=hT[:, c, t0:t0 + n], in_=tmp[j][:, 0:n], func=AF.Identity,
                                                                             scale=gT[:, c, 0:1], bias=shT[:, c, 0:1]),
                             rd=[tmpb[j], B["gsh"]], wr=[B["hT"]])
                        if gi == 3:
                            k.op(k.act, lambda e, c=c, j=j: e.activation(out=hlast[:, c:c + 1], in_=tmp[j][:, 511:512], func=AF.Identity,
                                                                         scale=gT[:, c, 0:1], bias=shT[:, c, 0:1]),
                                 rd=[tmpb[j], B["gsh"]], wr=[B["misc"]])
                    else:
                        v3 = tmp[j][:, 0:TSM].rearrange("p (s j) -> p s j", j=4)
                        k.op(k.dve, lambda e, c=c, v3=v3: e.tensor_tensor(out=v3, in0=v3, in1=gT[:, c, 1:17].unsqueeze(2).to_broadcast([128, NS, 4]), op=ALU.mult),
                             rd=[tmpb[j], B["gsh"]], wr=[tmpb[j]])
                        k.op(k.dve, lambda e, c=c, v3=v3: e.tensor_tensor(out=v3, in0=v3, in1=shT[:, c, 1:17].unsqueeze(2).to_broadcast([128, NS, 4]), op=ALU.add),
                             rd=[tmpb[j], B["gsh"]], wr=[tmpb[j]])
                        k.op(k.act, lambda e, c=c, j=j: e.activation(out=hTs[:, c, 0:TSM], in_=tmp[j][:, 0:TSM], func=AF.Copy),
                             rd=[tmpb[j]], wr=[B["hTs"]])
                        k.op(k.dve, lambda e, c=c, v3=v3: e.tensor_copy(out=hlast_s[:, c, :], in_=v3[:, :, 3]), rd=[tmpb[j]], wr=[B["misc"]])
            k.dma(k.sp, xt[0][:, :, 0:NS], hpT, wr=[xtb[0]])
            k.op(k.dve, lambda e: e.tensor_copy(out=hTs[:, :, TSM:TSM + NS], in_=xt[0][:, :, 0:NS]), rd=[xtb[0]], wr=[B["hTs"]])
            k.dma(k.pool, nshift_p, hlast[:], rd=[B["misc"]])
            k.dma(k.pool, nshift_s, hlast_s[:], rd=[B["misc"]])
            k.barrier()
        if debug:
            with ExitStack() as ph:
                d32 = sb("d32", [128, KC, 512], F32, ph); db = Buf()
                for tg in range(4):
                    k.op(k.dve, lambda e, tg=tg: e.tensor_copy(out=d32[:], in_=hT[:, :, tg * 512:(tg + 1) * 512]), rd=[B["hT"]], wr=[db])
                    k.dma(k.pool, dbg["hT"][:, :, tg * 512:(tg + 1) * 512], d32[:], rd=[db])
                k.op(k.dve, lambda e: e.tensor_copy(out=d32[:, :, 0:80], in_=hTs[:]), rd=[B["hTs"]], wr=[db])
                k.dma(k.pool, dbg["hT"][:, :, T:T + 80], d32[:, :, 0:80], rd=[db])
                k.barrier()
        if stage >= 2:
            phase_ret(nc, k, sb, locals())
        if stage >= 3:
            phase_rwkv(nc, k, sb, locals())
        k.barrier()
        hst.close()
        if stage >= 4:
            phase_out(nc, k, sb, locals())
        k.finish()
    return nc


def _bf(a):
    return np.ascontiguousarray(a, dtype=np.float32)


def host_inputs(inp, core):
    pb = core % 4
    s0 = core * NS
    xp = np.asarray(inp["x_prompt"][pb], np.float32)
    xsm = np.asarray(inp["x_sample"][s0:s0 + NS], np.float32).reshape(TSM, D)
    cc = np.concatenate([np.asarray(inp["c_prompt"][pb:pb + 1]), np.asarray(inp["c_sample"][s0:s0 + NS])], 0).astype(np.float32)

    def tr(a):
        return _bf(a.reshape(a.shape[0], KC, 128).transpose(2, 1, 0))

    def colv(v, nch):
        return _bf(np.asarray(v, np.float32).reshape(nch, 128).T)

    m = {}
    m["xT"] = tr(xp); m["x"] = _bf(xp); m["xsT"] = tr(xsm); m["xs"] = _bf(xsm)
    m["hpT"] = tr(np.asarray(inp["state_shift"][0, s0:s0 + NS], np.float32))
    m["cT"] = tr(cc)
    m["sret"] = _bf(inp["state_ret"][0, s0:s0 + NS])
    m["srwT"] = _bf(np.asarray(inp["state_rwkv"][0, s0:s0 + NS]).transpose(0, 1, 3, 2))
    m["w_ada"] = _bf(inp["w_ada"][0]); m["b_adaT"] = colv(inp["b_ada"][0], 48)
    m["w_in"] = _bf(inp["w_in"][0]); m["w_out"] = _bf(inp["w_out"][0])
    m["norm_wT"] = colv(inp["norm_w"][0], KC)
    m["fnw_b"] = _bf(np.broadcast_to(np.asarray(inp["final_norm_w"], np.float32)[None, :], (128, D)))
    mu = np.asarray(inp["mu_shift"][0], np.float32)

    def hv(v):
        return np.asarray(v, np.float32).reshape(32, 64).T

    rwp = np.stack([hv(mu[0:D]), hv(mu[D:2 * D]), hv(mu[2 * D:3 * D]), hv(inp["w0_decay"][0]), hv(inp["a0"][0]),
                    hv(inp["k_k"][0]), hv(inp["k_a"][0]), hv(np.asarray(inp["r_k"][0]).reshape(-1)),
                    hv(inp["ln_x_w"][0]), hv(inp["ln_x_b"][0])], 1)
    m["rwp"] = _bf(rwp)
    m["rwp2"] = _bf(np.concatenate([rwp[:, :, 0::2], rwp[:, :, 1::2]], 0))
    m["mu_lora"] = _bf(np.stack([mu[3 * D:3 * D + 96], mu[3 * D + 96:3 * D + 192]], 1))
    m["w2d"] = _bf(inp["w2_decay"][0]); m["a2"] = _bf(inp["a2"][0])
    m.update(CONSTS)
    return m


def make_consts():
    c = {}
    c["ident"] = np.eye(128, dtype=np.float32)
    c["ones"] = np.ones((128, 128), np.float32)
    i = np.arange(128)
    cm = np.zeros((128, 6, 128), np.float32)
    cm[:, 0, :] = (i[None, :] >= i[:, None])
    cm[:, 1, :] = (i[None, :] < i[:, None])
    cm[:, 2, :] = (i[None, :] > i[:, None])
    blk = (i[:, None] // 4 == i[None, :] // 4) & (i[:, None] < 64) & (i[None, :] < 64)
    for q in range(3):
        cm[:, 3 + q, :] = cm[:, q, :] * blk
    c["cmask"] = cm
    sm = np.zeros((64, NS), np.float32)
    sm[np.arange(64), np.arange(64) // 4] = 1
    c["seqmask"] = sm
    c["seqmaskT"] = np.ascontiguousarray(np.broadcast_to(sm.T[None], (128, NS, 64))).astype(np.float32)
    half = 128
    inv = (10000.0 ** (-np.arange(half, dtype=np.float32) / half)).astype(np.float32)
    pos = np.arange(T, dtype=np.float32)
    ang = (pos[None, :] * inv[:, None]).astype(np.float32)
    rope = np.zeros((128, 4, T), np.float32)
    rope[:, 0] = np.cos(ang); rope[:, 1] = np.sin(ang)
    c["rope"] = rope
    poss = (16384.0 + (np.arange(TSM) % 4)).astype(np.float32)
    angs = (poss[None, :] * inv[:, None]).astype(np.float32)
    c["ropes"] = np.stack([np.cos(angs), np.sin(angs)], 1).astype(np.float32)
    rdec = np.zeros((128, 8, 4, 128), np.float32)
    for h in range(8):
        lg = np.log1p(-np.exp2(-5.0 - h))
        ip = np.arange(128)
        rdec[:, h, 0, :] = np.exp((ip + 1) * lg)[None]
        rdec[:, h, 1, :] = (np.exp(-(ip + 1) * lg) * 256 ** -0.5)[None]
        isq = np.arange(64) % 4
        rdec[:, h, 2, :64] = np.exp((isq + 1) * lg)[None]
        rdec[:, h, 3, :64] = (np.exp(-(isq + 1) * lg) * 256 ** -0.5)[None]
    c["rdec"] = rdec
    bo = np.zeros((128, 128), np.float32); bo[:64, :64] = 1; bo[64:, 64:] = 1
    c["bones"] = bo
    sp_ = np.zeros((17, 128), np.float32); sp_[0] = 1
    ss_ = np.zeros((17, 64), np.float32); ss_[1:, :] = sm.T
    c["selp"] = sp_; c["sels"] = ss_
    return c


CONSTS = make_consts()
_NC = {}


def run(inputs, stage=STAGE, debug=DEBUG):
    key = (stage, debug)
    if key not in _NC:
        _NC[key] = build(stage, debug)
    nc = _NC[key]
    in_maps = [host_inputs(inputs, c) for c in range(8)]
    res = run_bass_kernel_spmd(nc, in_maps, core_ids=list(range(8)))
    return res.results


def kernel(**inputs):
    r = run(inputs)
    y_prompt = np.stack([r[c]["y"] for c in range(4)], 0)
    y_sample = np.concatenate([r[c]["ys"].reshape(NS, 4, D) for c in range(8)], 0)
    nret_p = np.stack([r[c]["nret_p"] for c in range(4)], 0)[None]
    nrw_p = np.stack([r[c]["nrw_pT"].transpose(0, 2, 1) for c in range(4)], 0)[None]
    nsh_p = np.stack([r[c]["nshift_p"].T.reshape(D) for c in range(4)], 0)[None]
    nret_s = np.concatenate([r[c]["nret_s"] for c in range(8)], 0)[None]
    nrw_s = np.concatenate([r[c]["nrw_sT"].transpose(0, 1, 3, 2) for c in range(8)], 0)[None]
    nsh_s = np.concatenate([r[c]["nshift_s"].transpose(2, 1, 0).reshape(NS, D) for c in range(8)], 0)[None]
    f = lambda a: np.ascontiguousarray(a, dtype=np.float32)
    return (f(y_prompt), f(y_sample), f(nret_p), f(nrw_p), f(nsh_p), f(nret_s), f(nrw_s), f(nsh_s))


class WStream:
    def __init__(self, nc, k, sb, stack, w_in, hT, hTs, Bh, Bhs, alloc=True):
        self.nc, self.k, self.w_in, self.hT, self.hTs, self.Bh, self.Bhs = nc, k, w_in, hT, hTs, Bh, Bhs
        self.n = 0
        if not alloc:
            return
        self.stg = sb("wstg", [128, KC, 256], F32, stack); self.stgb = Buf()
        self.wbig = sb("wbig", [128, KC, 512], BF16, stack)
        self.wb = [self.wbig[:, :, 0:256], self.wbig[:, :, 256:512]]
        self.wbb = [Buf(), Buf()]
        self.n = 0

    def tm2(self, t0, n, sample=False):
        k = self.k
        ps, pb = k.psum()
        src, sbf = (self.hTs, self.Bhs) if sample else (self.hT, self.Bh)
        for c in range(KC):
            k.op(k.pe, lambda e, c=c: e.matmul(ps[0:n, 0:512], lhsT=src[:, c, t0:t0 + n], rhs=self.wbig[:, c, :],
                                              start=(c == 0), stop=(c == KC - 1)), rd=[self.wbb[0], self.wbb[1], sbf], wr=[pb])
        return ps, pb

    def load(self, col0, ncol):
        k = self.k
        i = self.n % 2
        self.n += 1
        k.dma(k.sp, self.stg[:, :, 0:ncol], self.w_in[:, col0:col0 + ncol].rearrange("(c p) n -> p c n", p=128), wr=[self.stgb])
        k.op(k.pool, lambda e: e.tensor_copy(out=self.wb[i][:, :, 0:ncol], in_=self.stg[:, :, 0:ncol]), rd=[self.stgb], wr=[self.wbb[i]])
        return self.wb[i], self.wbb[i]

    def fm(self, wt, wtb, c0, m, t0, n, sample=False):
        k = self.k
        ps, pb = k.psum()
        src, sbf = (self.hTs, self.Bhs) if sample else (self.hT, self.Bh)
        for c in range(KC):
            k.op(k.pe, lambda e, c=c: e.matmul(ps[0:m, 0:n], lhsT=wt[:, c, c0:c0 + m], rhs=src[:, c, t0:t0 + n],
                                              start=(c == 0), stop=(c == KC - 1)), rd=[wtb, sbf], wr=[pb])
        return ps, pb

    def fm_gen(self, wt, wtb, c0, m, t0, n, sample=False, every=4):
        k = self.k
        ps, pb = k.psum()
        k.held.add(pb)
        src, sbf = (self.hTs, self.Bhs) if sample else (self.hT, self.Bh)
        for c in range(KC):
            k.op(k.pe, lambda e, c=c: e.matmul(ps[0:m, 0:n], lhsT=wt[:, c, c0:c0 + m], rhs=src[:, c, t0:t0 + n],
                                              start=(c == 0), stop=(c == KC - 1)), rd=[wtb, sbf], wr=[pb])
            if c % every == every - 1 and c < KC - 1:
                yield
        k.held.discard(pb)
        return ps, pb

    def tm(self, wt, wtb, ncol, t0, n, sample=False):
        k = self.k
        ps, pb = k.psum()
        src, sbf = (self.hTs, self.Bhs) if sample else (self.hT, self.Bh)
        for c in range(KC):
            k.op(k.pe, lambda e, c=c: e.matmul(ps[0:n, 0:ncol], lhsT=src[:, c, t0:t0 + n], rhs=wt[:, c, 0:ncol],
                                              start=(c == 0), stop=(c == KC - 1)), rd=[wtb, sbf], wr=[pb])
        return ps, pb


def phase_ret(nc, k, sb, L):
    hT, hTs, B = L["hT"], L["hTs"], L["B"]
    cm, smk, smkT, identb = L["cm"], L["smk"], L["smkT"], L["identb"]
    TA = T + TSM
    with ExitStack() as ph:
        ws = WStream(nc, k, sb, ph, L["w_in"], hT, hTs, B["hT"], B["hTs"])
        cosT = sb("cosT", [128, TA], F32, ph); sinT = sb("sinT", [128, TA], F32, ph); rb = Buf()
        k.dma(k.sp, cosT[:, 0:T], L["rope"][:, 0, :], wr=[rb]); k.dma(k.sp, sinT[:, 0:T], L["rope"][:, 1, :], wr=[rb])
        k.dma(k.sp, cosT[:, T:TA], L["ropes"][:, 0, :], wr=[rb]); k.dma(k.sp, sinT[:, T:TA], L["ropes"][:, 1, :], wr=[rb])
        dec = sb("dec", [128, 4, 128], F32, ph); decb = Buf()
        qT = sb("qT", [128, 2, TA], BF16, ph); kT = sb("kT", [128, 2, TA], BF16, ph); qb = Buf(); kb = Buf()
        vtm = sb("vtm", [128, 17, 256], BF16, ph); sgt = sb("sgt", [128, 17, 256], BF16, ph); vb = Buf(); gb = Buf()
        t1 = sb("rt1", [128, 512], F32, ph); t2 = sb("rt2", [128, 512], F32, ph); t1b = Buf(); t2b = Buf()
        S = sb("Sret", [128, 2, 256], F32, ph); Sb = sb("Sretb", [128, 2, 256], BF16, ph); Sbuf = Buf()
        Ss = [sb("Ss%d" % i, [128, 2, 256], F32, ph) for i in range(4)]; Ssb = [Buf() for _ in range(4)]
        Qm = sb("Qm", [128, 2, NS, 64], F32, ph); Qmb = Buf()
        Vblk = sb("Vblk", [64, NS, 256], BF16, ph); Vbb = Buf()
        scm = sb("scm", [128, 128], BF16, ph); scb = Buf()
        ktm = sb("ktm", [128, 256], BF16, ph); ktb = Buf()
        st6 = sb("st6", [128, 6], F32, ph); mv = sb("mv", [128, 2], F32, ph); rs = sb("rsg", [128, 1], F32, ph); stb = Buf()
        yf = sb("yf", [128, 256], F32, ph); ybf = sb("ybf", [128, 256], BF16, ph); yfb = Buf(); ybb = Buf()
        yTt = [sb("yTt%d" % i, [128, 2, 128], BF16, ph) for i in range(2)]; yTb = [Buf(), Buf()]
        groups = [(tg * 512, 512, False) for tg in range(4)] + [(0, TSM, True)]
        tiles = [(i * 128, 128, False, i) for i in range(16)] + [(T, 64, True, 16)]
        nsd = 0
        for hd in range(8):
            lg = math.log1p(-2.0 ** (-5.0 - hd))
            gC = math.exp(128 * lg); g4 = math.exp(4 * lg)
            k.dma(k.sp, dec[:], L["rdec"][:, hd], wr=[decb])
            for which, dst, dstb in ((0, qT, qb), (1, kT, kb)):
                wt, wtb = ws.load(which * D + hd * 256, 256)
                for (t0, n, smp) in groups:
                    p1, p1b = ws.fm(wt, wtb, 0, 128, t0, n, smp)
                    p2, p2b = ws.fm(wt, wtb, 128, 128, t0, n, smp)
                    o0 = T if smp else t0
                    cs = cosT[:, o0:o0 + n]; sn = sinT[:, o0:o0 + n]
                    if smp:
                        dv = dec[:, 2 + which, 0:64]; shp = None
                    else:
                        dv = dec[:, which, :].unsqueeze(1).to_broadcast([128, 4, 128])
                    for half in range(2):
                        a, ab, b_, bb = (p1, p1b, p2, p2b)
                        ca, cb = (cs, sn) if half == 0 else (sn, cs)
                        op2 = ALU.subtract if half == 0 else ALU.add
                        k.op(k.dve, lambda e: e.tensor_tensor(out=t1[:, 0:n], in0=a[:, 0:n], in1=ca, op=ALU.mult), rd=[ab, rb], wr=[t1b])
                        k.op(k.dve, lambda e: e.tensor_tensor(out=t2[:, 0:n], in0=b_[:, 0:n], in1=cb, op=ALU.mult), rd=[bb, rb], wr=[t2b])
                        k.op(k.dve, lambda e: e.tensor_tensor(out=t1[:, 0:n], in0=t1[:, 0:n], in1=t2[:, 0:n], op=op2), rd=[t1b, t2b], wr=[t1b])
                        if smp:
                            k.op(k.dve, lambda e: e.tensor_tensor(out=dst[:, half, T:T + 64], in0=t1[:, 0:64], in1=dv, op=ALU.mult),
                                 rd=[t1b, decb], wr=[dstb])
                        else:
                            k.op(k.dve, lambda e: e.tensor_tensor(out=dst[:, half, t0:t0 + 512].rearrange("p (a b) -> p a b", b=128),
                                                                  in0=t1[:, 0:512].rearrange("p (a b) -> p a b", b=128), in1=dv, op=ALU.mult),
                                 rd=[t1b, decb], wr=[dstb])
            assert ws.n % 2 == 0
            ws.load(2 * D + hd * 256, 256)
            ws.load(3 * D + hd * 256, 256)
            for (t0, n, smp, ti) in tiles:
                ps, pb = ws.tm2(0 if smp else t0, n, smp)
                k.op(k.act, lambda e: e.activation(out=vtm[0:n, ti, :], in_=ps[0:n, 0:256], func=AF.Copy), rd=[pb], wr=[vb])
                k.op(k.act, lambda e: e.activation(out=sgt[0:n, ti, :], in_=ps[0:n, 256:512], func=AF.Silu), rd=[pb], wr=[gb])
            k.op(k.dve, lambda e: e.memset(S[:], 0.0), wr=[Sbuf])
            k.op(k.dve, lambda e: e.memset(Sb[:], 0.0), wr=[Sbuf])
            for h in range(2):
                k.op(k.dve, lambda e: e.tensor_tensor(out=Qm[:, h], in0=qT[:, h, T:TA].unsqueeze(1).to_broadcast([128, NS, 64]),
                                                      in1=smkT[:], op=ALU.mult), rd=[qb, B["const"]], wr=[Qmb])
            k.op(k.dve, lambda e: e.tensor_tensor(out=Vblk[:], in0=vtm[0:64, 16, :].unsqueeze(1).to_broadcast([64, NS, 256]),
                                                  in1=smk[:].unsqueeze(2).to_broadcast([64, NS, 256]), op=ALU.mult),
                 rd=[vb, B["const"]], wr=[Vbb])
            for (t0, n, smp, ti) in tiles:
                mi = 3 if smp else 0
                ps, pb = k.psum()
                for h in range(2):
                    k.op(k.pe, lambda e: e.matmul(ps[0:n, 0:n], lhsT=kT[:, h, t0:t0 + n], rhs=qT[:, h, t0:t0 + n], start=(h == 0), stop=(h == 1)),
                         rd=[kb, qb], wr=[pb])
                k.op(k.dve, lambda e: e.tensor_tensor(out=scm[0:n, 0:n], in0=ps[0:n, 0:n], in1=cm[0:n, mi, 0:n], op=ALU.mult),
                     rd=[pb, B["const"]], wr=[scb])
                po, pob = k.psum()
                k.op(k.pe, lambda e: e.matmul(po[0:n, 0:256], lhsT=scm[0:n, 0:n], rhs=vtm[0:n, ti, :], start=True, stop=False), rd=[scb, vb], wr=[pob])
                if not smp:
                    for h in range(2):
                        k.op(k.pe, lambda e: e.matmul(po[0:n, 0:256], lhsT=qT[:, h, t0:t0 + n], rhs=Sb[:, h, :], start=False, stop=(h == 1)),
                             rd=[qb, Sbuf], wr=[pob])
                else:
                    for b in range(NS):
                        j = nsd % 4
                        nsd += 1
                        k.dma(k.sp, Ss[j][:], L["sret"][b, hd].rearrange("(h p) e -> p h e", p=128), wr=[Ssb[j]])
                        for h in range(2):
                            k.op(k.pe, lambda e: e.matmul(po[0:n, 0:256], lhsT=Qm[:, h, b, :], rhs=Ss[j][:, h, :], start=False,
                                                          stop=(b == NS - 1 and h == 1)), rd=[Qmb, Ssb[j]], wr=[pob])
                        pu, pub = k.psum(exclude=[pob])
                        if b == 0:
                            pt, ptb = k.psum(exclude=[pob, pub])
                            ptv = pt[:].bitcast(BF16)
                            for h in range(2):
                                k.op(k.pe, lambda e: e.transpose(ptv[0:64, h * 128:(h + 1) * 128], kT[:, h, T:TA], identb[:]), rd=[kb, B["const"]], wr=[ptb])
                            k.op(k.dve, lambda e: e.tensor_copy(out=ktm[0:64, :], in_=ptv[0:64, 0:256]), rd=[ptb], wr=[ktb])
                        for h in range(2):
                            k.op(k.pe, lambda e: e.matmul(pu[:, h * 256:(h + 1) * 256], lhsT=ktm[0:64, h * 128:(h + 1) * 128], rhs=Vblk[:, b, :],
                                                          start=True, stop=True), rd=[ktb, Vbb], wr=[pub])
                        sv = Ss[j][:].rearrange("p h e -> p (h e)")
                        k.op(k.dve, lambda e: e.tensor_scalar(out=sv, in0=sv, scalar1=g4, scalar2=None, op0=ALU.mult), rd=[Ssb[j]], wr=[Ssb[j]])
                        k.op(k.dve, lambda e: e.scalar_tensor_tensor(out=sv, in0=pu[:, 0:512], scalar=g4, in1=sv, op0=ALU.mult, op1=ALU.add),
                             rd=[pub, Ssb[j]], wr=[Ssb[j]])
                        k.dma(k.pool, L["nret_s"][b, hd].rearrange("(h p) e -> p h e", p=128), Ss[j][:], rd=[Ssb[j]])
                k.op(k.dve, lambda e: e.bn_stats(out=st6[0:n, :], in_=po[0:n, 0:256]), rd=[pob], wr=[stb])
                k.op(k.dve, lambda e: e.bn_aggr(out=mv[0:n, :], in_=st6[0:n, :]), rd=[stb], wr=[stb])
                k.op(k.dve, lambda e: e.tensor_scalar(out=rs[0:n, :], in0=mv[0:n, 1:2], scalar1=1e-5, scalar2=None, op0=ALU.add), rd=[stb], wr=[stb])
                k.op(k.act, lambda e: e.activation(out=rs[0:n, :], in_=rs[0:n, :], func=AF.Sqrt), rd=[stb], wr=[stb])
                k.op(k.dve, lambda e: e.reciprocal(out=rs[0:n, :], in_=rs[0:n, :]), rd=[stb], wr=[stb])
                k.op(k.dve, lambda e: e.tensor_scalar(out=yf[0:n, :], in0=po[0:n, 0:256], scalar1=mv[0:n, 0:1], scalar2=rs[0:n, 0:1],
                                                      op0=ALU.subtract, op1=ALU.mult), rd=[pob, stb], wr=[yfb])
                k.op(k.dve, lambda e: e.tensor_tensor(out=ybf[0:n, :], in0=yf[0:n, :], in1=sgt[0:n, ti, :], op=ALU.mult), rd=[yfb, gb], wr=[ybb])
                pt, ptb = k.psum()
                ptv = pt[:].bitcast(BF16)
                for h in range(2):
                    k.op(k.pe, lambda e: e.transpose(ptv[:, h * 128:h * 128 + n], ybf[0:n, h * 128:(h + 1) * 128], identb[0:n, 0:n]),
                         rd=[ybb, B["const"]], wr=[ptb])
                yi = ti % 2
                k.op(k.act, lambda e: e.activation(out=yTt[yi][:, :, 0:n], in_=ptv[:, 0:256].rearrange("p (h t) -> p h t", h=2)[:, :, 0:n], func=AF.Copy),
                     rd=[ptb], wr=[yTb[yi]])
                k.dma(k.pool, L["ymix_d"][2 * hd:2 * hd + 2, :, t0:t0 + n].rearrange("h p t -> p h t"), yTt[yi][:, :, 0:n], rd=[yTb[yi]])
                if not smp:
                    pt2, pt2b = k.psum()
                    ptv2 = pt2[:].bitcast(BF16)
                    for h in range(2):
                        k.op(k.pe, lambda e: e.transpose(ptv2[:, h * 128:(h + 1) * 128], kT[:, h, t0:t0 + 128], identb[:]), rd=[kb, B["const"]], wr=[pt2b])
                    k.op(k.act, lambda e: e.activation(out=ktm[:, :], in_=ptv2[:, 0:256], func=AF.Copy), rd=[pt2b], wr=[ktb])
                    pu, pub = k.psum()
                    for h in range(2):
                        k.op(k.pe, lambda e: e.matmul(pu[:, h * 256:(h + 1) * 256], lhsT=ktm[:, h * 128:(h + 1) * 128], rhs=vtm[:, ti, :], start=True, stop=True),
                             rd=[ktb, vb], wr=[pub])
                    sv = S[:].rearrange("p h e -> p (h e)")
                    k.op(k.dve, lambda e: e.tensor_scalar(out=sv, in0=sv, scalar1=gC, scalar2=None, op0=ALU.mult), rd=[Sbuf], wr=[Sbuf])
                    k.op(k.dve, lambda e: e.scalar_tensor_tensor(out=sv, in0=pu[:, 0:512], scalar=gC, in1=sv, op0=ALU.mult, op1=ALU.add),
                         rd=[pub, Sbuf], wr=[Sbuf])
                    k.op(k.act, lambda e: e.activation(out=Sb[:], in_=S[:], func=AF.Copy), rd=[Sbuf], wr=[Sbuf])
                    if ti == 15:
                        k.dma(k.pool, L["nret_p"][hd].rearrange("(h p) e -> p h e", p=128), S[:], rd=[Sbuf])
        k.barrier()


def phase_out(nc, k, sb, L):
    B, modT, identf = L["B"], L["modT"], L["identf"]
    with ExitStack() as ph:
        grow = sb("grow", [17, D], F32, ph); grb = Buf()
        selp = sb("selp_s", [17, 128], F32, ph); sels = sb("sels_s", [17, 64], F32, ph); fnw = sb("fnw", [128, D], F32, ph); cb = Buf()
        k.dma(k.sp, selp[:], L["selp"], wr=[cb]); k.dma(k.sp, sels[:], L["sels"], wr=[cb]); k.dma(k.sp, fnw[:], L["fnw_b"], wr=[cb])
        for c in range(KC):
            ps, pb = k.psum()
            k.op(k.pe, lambda e: e.transpose(ps[0:17, 0:128], modT[:, 32 + c, :], identf[:]), rd=[B["modT"], B["const"]], wr=[pb])
            k.op(k.dve, lambda e: e.tensor_copy(out=grow[:, c * 128:(c + 1) * 128], in_=ps[0:17, 0:128]), rd=[pb], wr=[grb])
        wo = sb("wo", [128, 32, D], BF16, ph); wob = [Buf() for _ in range(32)]
        wst = [sb("wost%d" % i, [128, 4, 256], F32, ph) for i in range(2)]; wstb = [Buf(), Buf()]
        nld = 0
        for fc4 in range(8):
            for cg in range(8):
                i = nld % 2
                nld += 1
                k.dma(k.act if i else k.sp, wst[i][:], L["w_out"][fc4 * 512:(fc4 + 1) * 512, cg * 256:(cg + 1) * 256].rearrange("(c p) n -> p c n", p=128), wr=[wstb[i]])
                dstw = wo[:, fc4 * 4:(fc4 + 1) * 4, cg * 256:(cg + 1) * 256]
                wbufs = [wob[fc4 * 4 + q] for q in range(4)]
                if nld % 2:
                    k.op(k.pool, lambda e: e.tensor_copy(out=dstw, in_=wst[i][:]), rd=[wstb[i]], wr=wbufs)
                else:
                    k.op(k.dve, lambda e: e.tensor_copy(out=dstw, in_=wst[i][:]), rd=[wstb[i]], wr=wbufs)
        ym = sb("ym", [128, 32, 128], BF16, ph); ymb = Buf()
        xr = [sb("xr%d" % i, [128, D], F32, ph) for i in range(2)]; xrb = [Buf(), Buf()]
        G = [sb("Gt%d" % i, [128, 512], F32, ph) for i in range(2)]; Gb = [Buf(), Buf()]
        junk = sb("junk", [128, D], BF16, ph); jb = Buf()
        ssq = sb("ssq", [128, 1], F32, ph); sqb = Buf()
        tiles = [(i * 128, 128, False) for i in range(16)] + [(T, 64, True)]
        ng = 0
        for ti, (t0, n, smp) in enumerate(tiles):
            i = ti % 2
            k.dma(k.sp, ym[:, :, 0:n], L["ymix_d"][:, :, t0:t0 + n].rearrange("c p t -> p c t"), wr=[ymb])
            k.dma(k.sp, xr[i][0:n, :], L["xs"][0:n, :] if smp else L["x"][t0:t0 + n, :], wr=[xrb[i]])
            sel = sels if smp else selp
            for cg in range(4):
                c0 = cg * 512
                gi = ng % 2
                ng += 1
                pg, pgb = k.psum()
                k.op(k.pe, lambda e: e.matmul(pg[0:n, 0:512], lhsT=sel[:, 0:n], rhs=grow[:, c0:c0 + 512], start=True, stop=True), rd=[cb, grb], wr=[pgb])
                k.op(k.act, lambda e: e.activation(out=G[gi][0:n, :], in_=pg[0:n, 0:512], func=AF.Copy), rd=[pgb], wr=[Gb[gi]])
                ps, pb = k.psum()
                for fc in range(32):
                    k.op(k.pe, lambda e: e.matmul(ps[0:n, 0:512], lhsT=ym[:, fc, 0:n], rhs=wo[:, fc, c0:c0 + 512], start=(fc == 0), stop=(fc == 31)),
                         rd=[ymb, wob[fc]], wr=[pb])
                k.op(k.dve, lambda e: e.tensor_tensor(out=G[gi][0:n, :], in0=ps[0:n, 0:512], in1=G[gi][0:n, :], op=ALU.mult), rd=[pb, Gb[gi]], wr=[Gb[gi]])
                k.op(k.dve, lambda e: e.tensor_tensor(out=xr[i][0:n, c0:c0 + 512], in0=G[gi][0:n, :], in1=xr[i][0:n, c0:c0 + 512], op=ALU.add), rd=[Gb[gi], xrb[i]], wr=[xrb[i]])
            k.op(k.act, lambda e: e.activation(out=junk[0:n, :], in_=xr[i][0:n, :], func=AF.Square, accum_out=ssq[0:n, :]), rd=[xrb[i]], wr=[jb, sqb])
            k.op(k.dve, lambda e: e.tensor_scalar(out=ssq[0:n, :], in0=ssq[0:n, :], scalar1=1.0 / D, scalar2=1e-6, op0=ALU.mult, op1=ALU.add), rd=[sqb], wr=[sqb])
            k.op(k.act, lambda e: e.activation(out=ssq[0:n, :], in_=ssq[0:n, :], func=AF.Sqrt), rd=[sqb], wr=[sqb])
            k.op(k.dve, lambda e: e.reciprocal(out=ssq[0:n, :], in_=ssq[0:n, :]), rd=[sqb], wr=[sqb])
            k.op(k.dve, lambda e: e.scalar_tensor_tensor(out=xr[i][0:n, :], in0=xr[i][0:n, :], scalar=ssq[0:n, 0:1], in1=fnw[0:n, :], op0=ALU.mult, op1=ALU.mult),
                 rd=[xrb[i], sqb, cb], wr=[xrb[i]])
            dst = L["ys"][0:n, :] if smp else L["y"][t0:t0 + n, :]
            k.dma(k.pool, dst, xr[i][0:n, :], rd=[xrb[i]])
        k.barrier()


def phase_rwkv(nc, k, sb, L):
    hT, hTs, B = L["hT"], L["hTs"], L["B"]
    cm, smk, smkT, identb, onesf, rw, mul = L["cm"], L["smk"], L["smkT"], L["identb"], L["onesf"], L["rw"], L["mul"]
    TA = T + TSM
    EM = math.exp(-0.5)
    with ExitStack() as ph:
        ws = WStream(nc, k, sb, ph, L["w_in"], hT, hTs, B["hT"], B["hTs"], alloc=False)
        cB = B["const"]
        rw2 = sb("rw2", [128, 10, 16], F32, ph); bones = sb("bones", [128, 128], F32, ph)
        k.dma(k.sp, rw2[:], L["rwp2"], wr=[cB]); k.dma(k.sp, bones[:], L["bones"], wr=[cB])
        omm = sb("omm", [128, 3, 16], F32, ph); omml = sb("omml", [96, 2], F32, ph)
        k.op(k.dve, lambda e: e.tensor_scalar(out=omm[:], in0=rw2[:, 0:3, :], scalar1=-1.0, scalar2=1.0, op0=ALU.mult, op1=ALU.add), rd=[cB], wr=[cB])
        k.op(k.dve, lambda e: e.tensor_scalar(out=omml[:], in0=mul[:], scalar1=-1.0, scalar2=1.0, op0=ALU.mult, op1=ALU.add), rd=[cB], wr=[cB])
        w2b = sb("w2b", [96, D], BF16, ph); a2b = sb("a2b", [96, D], BF16, ph)
        with ExitStack() as p2:
            st = sb("lst", [96, D], F32, p2); stb = Buf()
            for src, dst in ((L["w2d"], w2b), (L["a2"], a2b)):
                k.dma(k.sp, st[:], src, wr=[stb])
                k.op(k.dve, lambda e: e.tensor_copy(out=dst[:], in_=st[:]), rd=[stb], wr=[cB])
            k.barrier()
        tanhwd = sb("tanhwd", [96, TA], BF16, ph); adm = sb("adm", [96, TA], BF16, ph); lb = Buf()
        with ExitStack() as p2:
            cur = sb("lcur", [96, 2, 1 + T], F32, p2); curs = sb("lcurs", [96, 2, 80], F32, p2); prv = sb("lprv", [96, 64], F32, p2)
            tmpl = sb("ltmp", [96, T], F32, p2); cb_ = Buf(); tb_ = Buf()
            lst_ = sb("lwst", [128, KC, 192], F32, p2); wt = sb("lwb", [128, KC, 192], BF16, p2); wtb = Buf()
            k.dma(k.sp, lst_[:], L["w_in"][:, 16384:16576].rearrange("(c p) n -> p c n", p=128), wr=[wtb])
            k.op(k.pool, lambda e: e.tensor_copy(out=wt[:], in_=lst_[:]), rd=[wtb], wr=[wtb])
            k.op(k.dve, lambda e: e.memset(cur[:, :, 0:1], 0.0), wr=[cb_])
            for which in range(2):
                for tg in range(4):
                    ps, pb = ws.fm(wt, wtb, which * 96, 96, tg * 512, 512)
                    k.op(k.act, lambda e: e.activation(out=cur[:, which, 1 + tg * 512:1 + (tg + 1) * 512], in_=ps[0:96, 0:512], func=AF.Copy), rd=[pb], wr=[cb_])
                ps, pb = ws.fm(wt, wtb, which * 96, 96, 0, 80, True)
                k.op(k.act, lambda e: e.activation(out=curs[:, which, :], in_=ps[0:96, 0:80], func=AF.Copy), rd=[pb], wr=[cb_])
                dst = tanhwd if which == 0 else adm
                fn = AF.Tanh if which == 0 else AF.Copy
                k.op(k.dve, lambda e: e.tensor_scalar(out=tmpl[:, 0:T], in0=cur[:, which, 0:T], scalar1=mul[:, which:which + 1], scalar2=None, op0=ALU.mult), rd=[cb_, cB], wr=[tb_])
                k.op(k.dve, lambda e: e.scalar_tensor_tensor(out=tmpl[:, 0:T], in0=cur[:, which, 1:1 + T], scalar=omml[:, which:which + 1], in1=tmpl[:, 0:T], op0=ALU.mult, op1=ALU.add),
                     rd=[cb_, cB, tb_], wr=[tb_])
                k.op(k.act, lambda e: e.activation(out=dst[:, 0:T], in_=tmpl[:, 0:T], func=fn), rd=[tb_], wr=[lb])
                c3 = curs[:, which, 0:64].rearrange("p (s j) -> p s j", j=4); p3 = prv[:].rearrange("p (s j) -> p s j", j=4)
                k.op(k.dve, lambda e: e.tensor_copy(out=p3[:, :, 1:4], in_=c3[:, :, 0:3]), rd=[cb_], wr=[tb_])
                k.op(k.dve, lambda e: e.tensor_copy(out=p3[:, :, 0], in_=curs[:, which, 64:80]), rd=[cb_], wr=[tb_])
                k.op(k.dve, lambda e: e.tensor_scalar(out=prv[:], in0=prv[:], scalar1=mul[:, which:which + 1], scalar2=None, op0=ALU.mult), rd=[tb_, cB], wr=[tb_])
                k.op(k.dve, lambda e: e.scalar_tensor_tensor(out=prv[:], in0=curs[:, which, 0:64], scalar=omml[:, which:which + 1], in1=prv[:], op0=ALU.mult, op1=ALU.add),
                     rd=[cb_, cB, tb_], wr=[tb_])
                k.op(k.act, lambda e: e.activation(out=dst[:, T:TA], in_=prv[:], func=fn), rd=[tb_], wr=[lb])
            k.barrier()
        NB = 256
        F = {n: sb("rf_" + n, [128, 1 + NB], F32, ph) for n in ["cr", "ck", "cv", "rm", "km", "vm", "t7", "W", "Wp"]}
        Z = {n: Buf() for n in ["cr", "ck", "cv", "rm", "km", "vm", "t7", "W", "Wp", "al", "be", "kt", "vb", "mp", "pwA", "pwB", "car", "wb",
                                "Ubp", "oT", "t7c", "y", "S", "S0s", "msk"]}
        for a_, b_ in {"w": "cr", "a": "ck", "kk": "cv"}.items():
            F[a_] = F[b_]; Z[a_] = Z[b_]
        HS = []
        for par2 in range(2):
            hb_ = {n: sb("rb_" + n, [128, NB], BF16, ph) for n in ["al", "be", "kt", "vb"]}
            hbp_ = {n: [sb("rbp_%s%d" % (n, e_), [128, NB], BF16, ph) for e_ in range(2)] for n in ["al", "be", "kt"]}
            hz_ = {n: Buf() for n in ["al", "be", "kt", "vb"]}
            for n_ in ["al", "be", "kt"]:
                for e_ in range(2):
                    hz_[n_ + "P%d" % e_] = Buf()
                    k.op(k.pool, lambda e: e.memset(hbp_[n_][e_][:], 0.0), wr=[hz_[n_ + "P%d" % e_]])
            HS.append(dict(Hb=hb_, HbP=hbp_, Z=hz_))
        mp = sb("mats_pre", [128, 3, 4, 128], BF16, ph)
        pw = [sb("pw%d" % i, [128, 2, 4, 128], BF16, ph) for i in range(2)]
        car = sb("car", [128, 3], F32, ph)
        wb = sb("wbp", [128, KC, 512], BF16, ph)
        stg4 = [sb("stg4_%d" % i, [128, KC, 64], F32, ph) for i in range(2)]; stg4b = [Buf(), Buf()]
        CS = []
        for par in range(3):
            d = dict(P=sb("Pm", [128, 4, 128], BF16, ph), m34=sb("m34", [128, 2, 4, 128], BF16, ph), tp=sb("tmpad", [128, 2, 4, 192], BF16, ph),
                     ah=sb("ah", [128, 2, 128], BF16, ph), Yb=sb("Yb", [128, 2, 2, 64], BF16, ph), wc=sb("wc", [128, 16], F32, ph),
                     bv=sb("bv", [128, NB], F32, ph), sg=sb("sg", [128, NB], BF16, ph), rt=sb("rt", [128, NB], BF16, ph))
            d["Z"] = {n: Buf() for n in ["P", "m34", "tp", "ah", "Yb", "wc", "bv", "sg", "rt"]}
            k.op(k.dve, lambda e: e.memset(d["tp"][:], 0.0), wr=[d["Z"]["tp"]])
            CS.append(d)
        Ubp = sb("Ubp", [128, 192], BF16, ph)
        k.op(k.dve, lambda e: e.memset(Ubp[:], 0.0), wr=[Z["Ubp"]])
        oT = sb("oT", [128, NB], F32, ph); t7c = sb("t7c", [128, NB], F32, ph); yb = sb("ybf", [128, NB], BF16, ph)
        S = sb("Srw", [128, 64], F32, ph); Sbd = sb("Sbd", [128, 128], BF16, ph)
        S0s = sb("S0s", [128, NS, 64], F32, ph); Sbds = sb("Sbds", [128, NS, 128], BF16, ph)
        ahm = sb("ahm", [128, NS, 64], BF16, ph); rtm = sb("rtm", [128, NS, 64], BF16, ph)
        Ublk = sb("Ublk", [64, 2, NS, 64], BF16, ph); Vblk = sb("Vblk2", [64, 2, NS, 64], BF16, ph)
        k.op(k.dve, lambda e: e.memset(Sbds[:], 0.0), wr=[Z["S0s"]])
        nstg = [0]

        def load_w(pi):
            for q in range(8):
                si = nstg[0] % 2
                nstg[0] += 1
                c0 = 8192 + (q // 2) * D + pi * 128 + (q % 2) * 64
                k.dma(k.sp, stg4[si][:], L["w_in"][:, c0:c0 + 64].rearrange("(c p) n -> p c n", p=128), wr=[stg4b[si]])
                k.op(k.pool, lambda e: e.tensor_copy(out=wb[:, :, q * 64:(q + 1) * 64], in_=stg4[si][:]), rd=[stg4b[si]], wr=[Z["wb"]])

        def prepA(pi, t0, n, smp, par, par2):
            C = CS[par]; CZ = C["Z"]; H = HS[par2]; Hb = H["Hb"]; HbP = H["HbP"]; HZ = H["Z"]
            TT = 64 if smp else 128
            ntile = n // TT
            NM = 2 * ntile
            pc = lambda j: rw2[:, j, pi:pi + 1]
            lc = T if smp else t0
            if t0 == 0 and not smp:
                load_w(pi)
                k.op(k.dve, lambda e: e.memset(car[:], 0.0), wr=[Z["car"]])
            for qi, cn in enumerate(("cr", "ck", "cv")):
                mn = ("rm", "km", "vm")[qi]
                if smp:
                    ps, pb = ws.fm(wb, Z["wb"], qi * 128, 128, 0, 80, True)
                    k.op(k.act, lambda e: e.activation(out=F[cn][:, 1:81], in_=ps[:, 0:80], func=AF.Copy), rd=[pb], wr=[Z[cn]])
                    c3 = F[cn][:, 1:65].rearrange("p (s j) -> p s j", j=4); p3 = F["t7"][:, 0:64].rearrange("p (s j) -> p s j", j=4)
                    k.op(k.dve, lambda e: e.tensor_copy(out=p3[:, :, 1:4], in_=c3[:, :, 0:3]), rd=[Z[cn]], wr=[Z["t7"]])
                    k.op(k.dve, lambda e: e.tensor_copy(out=p3[:, :, 0], in_=F[cn][:, 65:81]), rd=[Z[cn]], wr=[Z["t7"]])
                    k.op(k.dve, lambda e: e.tensor_scalar(out=F["t7"][:, 0:n], in0=F["t7"][:, 0:n], scalar1=pc(qi), scalar2=None, op0=ALU.mult), rd=[Z["t7"], cB], wr=[Z["t7"]])
                else:
                    k.op(k.dve, lambda e: e.tensor_copy(out=F[cn][:, 0:1], in_=car[:, qi:qi + 1]), rd=[Z["car"]], wr=[Z[cn]])
                    ps, pb = yield from ws.fm_gen(wb, Z["wb"], qi * 128, 128, t0, n)
                    k.op(k.act, lambda e: e.activation(out=F[cn][:, 1:1 + n], in_=ps[:, 0:n], func=AF.Copy), rd=[pb], wr=[Z[cn]])
                    k.op(k.dve, lambda e: e.tensor_copy(out=car[:, qi:qi + 1], in_=F[cn][:, n:n + 1]), rd=[Z[cn]], wr=[Z["car"]])
                    k.op(k.dve, lambda e: e.tensor_scalar(out=F["t7"][:, 0:n], in0=F[cn][:, 0:n], scalar1=pc(qi), scalar2=None, op0=ALU.mult), rd=[Z[cn], cB], wr=[Z["t7"]])
                k.op(k.dve, lambda e: e.scalar_tensor_tensor(out=F[mn][:, 0:n], in0=F[cn][:, 1:1 + n], scalar=omm[:, qi, pi:pi + 1], in1=F["t7"][:, 0:n], op0=ALU.mult, op1=ALU.add),
                     rd=[Z[cn], Z["t7"], cB], wr=[Z[mn]])
                yield
            ps, pb = yield from ws.fm_gen(wb, Z["wb"], 384, 128, 0 if smp else t0, n, smp)
            k.op(k.act, lambda e: e.activation(out=C["sg"][:, 0:n], in_=ps[:, 0:n], func=AF.Silu), rd=[pb], wr=[CZ["sg"]])
            ps, pb = k.psum()
            k.op(k.pe, lambda e: e.matmul(ps[:, 0:n], lhsT=w2b[:, pi * 128:(pi + 1) * 128], rhs=tanhwd[:, lc:lc + n], start=True, stop=True), rd=[cB, lb], wr=[pb])
            k.op(k.act, lambda e: e.activation(out=F["w"][:, 0:n], in_=ps[:, 0:n], func=AF.Sigmoid, bias=pc(3)), rd=[pb, cB], wr=[Z["w"]])
            ps, pb = k.psum()
            k.op(k.pe, lambda e: e.matmul(ps[:, 0:n], lhsT=a2b[:, pi * 128:(pi + 1) * 128], rhs=adm[:, lc:lc + n], start=True, stop=True), rd=[cB, lb], wr=[pb])
            k.op(k.act, lambda e: e.activation(out=F["a"][:, 0:n], in_=ps[:, 0:n], func=AF.Sigmoid, bias=pc(4)), rd=[pb, cB], wr=[Z["a"]])
            k.op(k.act, lambda e: e.activation(out=F["w"][:, 0:n], in_=F["w"][:, 0:n], func=AF.Exp, scale=-EM), rd=[Z["w"]], wr=[Z["w"]])
            yield
            k.op(k.dve, lambda e: e.tensor_scalar(out=F["kk"][:, 0:n], in0=F["km"][:, 0:n], scalar1=pc(5), scalar2=None, op0=ALU.mult), rd=[Z["km"], cB], wr=[Z["kk"]])
            k.op(k.dve, lambda e: e.tensor_tensor(out=F["t7"][:, 0:n], in0=F["kk"][:, 0:n], in1=F["kk"][:, 0:n], op=ALU.mult), rd=[Z["kk"]], wr=[Z["t7"]])
            ps, pb = k.psum()
            k.op(k.pe, lambda e: e.matmul(ps[:, 0:n], lhsT=bones[:, :], rhs=F["t7"][:, 0:n], start=True, stop=True), rd=[cB, Z["t7"]], wr=[pb])
            k.op(k.act, lambda e: e.activation(out=F["Wp"][:, 0:n], in_=ps[:, 0:n], func=AF.Sqrt), rd=[pb], wr=[Z["Wp"]])
            k.op(k.dve, lambda e: e.tensor_scalar(out=F["Wp"][:, 0:n], in0=F["Wp"][:, 0:n], scalar1=1e-12, scalar2=None, op0=ALU.max), rd=[Z["Wp"]], wr=[Z["Wp"]])
            k.op(k.dve, lambda e: e.reciprocal(out=F["Wp"][:, 0:n], in_=F["Wp"][:, 0:n]), rd=[Z["Wp"]], wr=[Z["Wp"]])
            k.op(k.dve, lambda e: e.tensor_tensor(out=F["kk"][:, 0:n], in0=F["kk"][:, 0:n], in1=F["Wp"][:, 0:n], op=ALU.mult), rd=[Z["kk"], Z["Wp"]], wr=[Z["kk"]])
            yield
            k.op(k.dve, lambda e: e.tensor_scalar(out=F["t7"][:, 0:n], in0=F["a"][:, 0:n], scalar1=pc(6), scalar2=pc(6), op0=ALU.mult, op1=ALU.subtract), rd=[Z["a"], cB], wr=[Z["t7"]])
            k.op(k.dve, lambda e: e.scalar_tensor_tensor(out=F["km"][:, 0:n], in0=F["t7"][:, 0:n], scalar=1.0, in1=F["km"][:, 0:n], op0=ALU.add, op1=ALU.mult),
                 rd=[Z["t7"], Z["km"]], wr=[Z["km"]])
            CL = 4 if smp else 128
            for c0 in range(0, n, CL):
                k.op(k.dve, lambda e: e.tensor_tensor_scan(out=F["W"][:, c0:c0 + CL], data0=F["w"][:, c0:c0 + CL], data1=F["w"][:, c0:c0 + CL], initial=1.0, op0=ALU.mult, op1=ALU.bypass),
                     rd=[Z["w"]], wr=[Z["W"]])
            W3 = F["W"][:, 0:n].rearrange("p (c j) -> p c j", j=CL); Wp3 = F["Wp"][:, 0:n].rearrange("p (c j) -> p c j", j=CL)
            k.op(k.dve, lambda e: e.tensor_copy(out=Wp3[:, :, 1:CL], in_=W3[:, :, 0:CL - 1]), rd=[Z["W"]], wr=[Z["Wp"]])
            k.op(k.dve, lambda e: e.memset(Wp3[:, :, 0:1], 1.0), wr=[Z["Wp"]])
            k.op(k.dve, lambda e: e.tensor_copy(out=C["wc"][:, 0:n // CL], in_=W3[:, :, CL - 1]), rd=[Z["W"]], wr=[CZ["wc"]])
            yield
            k.op(k.dve, lambda e: e.scalar_tensor_tensor(out=Hb["al"][:, 0:n], in0=F["kk"][:, 0:n], scalar=-1.0, in1=F["Wp"][:, 0:n], op0=ALU.mult, op1=ALU.mult), rd=[Z["kk"], Z["Wp"]], wr=[HZ["al"]])
            k.op(k.dve, lambda e: e.tensor_tensor(out=C["rt"][:, 0:n], in0=F["rm"][:, 0:n], in1=F["W"][:, 0:n], op=ALU.mult), rd=[Z["rm"], Z["W"]], wr=[CZ["rt"]])
            k.op(k.dve, lambda e: e.tensor_tensor(out=F["t7"][:, 0:n], in0=F["kk"][:, 0:n], in1=F["a"][:, 0:n], op=ALU.mult), rd=[Z["kk"], Z["a"]], wr=[Z["t7"]])
            k.op(k.dve, lambda e: e.reciprocal(out=F["W"][:, 0:n], in_=F["W"][:, 0:n]), rd=[Z["W"]], wr=[Z["W"]])
            k.op(k.dve, lambda e: e.tensor_tensor(out=Hb["be"][:, 0:n], in0=F["t7"][:, 0:n], in1=F["W"][:, 0:n], op=ALU.mult), rd=[Z["t7"], Z["W"]], wr=[HZ["be"]])
            k.op(k.dve, lambda e: e.tensor_tensor(out=Hb["kt"][:, 0:n], in0=F["km"][:, 0:n], in1=F["W"][:, 0:n], op=ALU.mult), rd=[Z["km"], Z["W"]], wr=[HZ["kt"]])
            k.op(k.act, lambda e: e.activation(out=Hb["vb"][:, 0:n], in_=F["vm"][:, 0:n], func=AF.Copy), rd=[Z["vm"]], wr=[HZ["vb"]])
            yield
            k.op(k.dve, lambda e: e.scalar_tensor_tensor(out=F["t7"][:, 0:n], in0=F["rm"][:, 0:n], scalar=pc(7), in1=F["km"][:, 0:n], op0=ALU.mult, op1=ALU.mult), rd=[Z["rm"], Z["km"], cB], wr=[Z["t7"]])
            ps, pb = k.psum()
            k.op(k.pe, lambda e: e.matmul(ps[:, 0:n], lhsT=bones[:, :], rhs=F["t7"][:, 0:n], start=True, stop=True), rd=[cB, Z["t7"]], wr=[pb])
            k.op(k.dve, lambda e: e.tensor_tensor(out=C["bv"][:, 0:n], in0=ps[:, 0:n], in1=F["vm"][:, 0:n], op=ALU.mult), rd=[pb, Z["vm"]], wr=[CZ["bv"]])
            yield
            for n_ in ("al", "be", "kt"):
                k.op(k.act, lambda e: e.activation(out=HbP[n_][0][0:64, 0:n], in_=Hb[n_][0:64, 0:n], func=AF.Copy), rd=[HZ[n_]], wr=[HZ[n_ + "P0"]])
                k.op(k.act, lambda e: e.activation(out=HbP[n_][1][64:128, 0:n], in_=Hb[n_][64:128, 0:n], func=AF.Copy), rd=[HZ[n_]], wr=[HZ[n_ + "P1"]])
            yield
        def prepB(pi, t0, n, smp, par, par2):
            C = CS[par]; CZ = C["Z"]; H = HS[par2]; Hb = H["Hb"]; HbP = H["HbP"]; HZ = H["Z"]
            TT = 64 if smp else 128
            ntile = n // TT
            NM = 2 * ntile
            for ti in range(ntile):
                sl = slice(ti * TT, (ti + 1) * TT)
                pt, ptb = k.psum()
                ptv = pt[:].bitcast(BF16)
                for qi, nm in enumerate(("al", "be", "kt", "vb")):
                    k.op(k.pe, lambda e: e.transpose(ptv[0:TT, qi * 128:(qi + 1) * 128], Hb[nm][:, sl], identb[:, :]), rd=[HZ[nm], cB], wr=[ptb])
                pv = ptv[0:TT, 0:512].rearrange("p (q c) -> p q c", c=128)
                k.op(k.dve, lambda e: e.tensor_copy(out=C["tp"][0:TT, ti, :, 0:64], in_=pv[:, :, 0:64]), rd=[ptb], wr=[CZ["tp"]])
                k.op(k.act, lambda e: e.activation(out=C["tp"][0:TT, ti, :, 128:192], in_=pv[:, :, 64:128], func=AF.Copy), rd=[ptb], wr=[CZ["tp"]])
            yield
            m1, m2, m0 = (4, 5, 3) if smp else (1, 2, 0)
            specs = [("al", "be", m1), ("be", "al", m2), ("kt", "al", m2), ("be", "rt", m0), ("kt", "rt", m0)]
            banks = [k.psum() for _ in range(5)]
            for ti in range(ntile):
                sl = slice(ti * TT, (ti + 1) * TT)
                for e_ in range(2):
                    m = ti * 2 + e_
                    for si_, (ln, rn, mk) in enumerate(specs):
                        lt = HbP[ln][e_][:, sl]
                        rtn = C["rt"][:, sl] if rn == "rt" else Hb[rn][:, sl]
                        rb_ = CZ["rt"] if rn == "rt" else HZ[rn]
                        k.op(k.pe, lambda e: e.matmul(banks[si_][0][0:TT, m * 128:m * 128 + TT], lhsT=lt, rhs=rtn, start=True, stop=True), rd=[HZ[ln + "P%d" % e_], rb_], wr=[banks[si_][1]])
            for si_, (ln, rn, mk) in enumerate(specs):
                dst = mp[0:TT, si_, 0:NM, 0:TT] if si_ < 3 else C["m34"][0:TT, si_ - 3, 0:NM, 0:TT]
                db_ = Z["mp"] if si_ < 3 else CZ["m34"]
                src_ = banks[si_][0][0:TT, 0:NM * 128].rearrange("p (m c) -> p m c", c=128)[:, :, 0:TT]
                k.op(k.dve, lambda e: e.tensor_tensor(out=dst, in0=src_, in1=cm[0:TT, mk, 0:TT].unsqueeze(1).to_broadcast([TT, NM, TT]), op=ALU.mult), rd=[banks[si_][1], cB], wr=[db_])
            yield
            P = C["P"]
            k.op(k.dve, lambda e: e.tensor_tensor(out=P[0:TT, 0:NM, 0:TT], in0=mp[0:TT, 1, 0:NM, 0:TT], in1=identb[0:TT, 0:TT].unsqueeze(1).to_broadcast([TT, NM, TT]), op=ALU.add),
                 rd=[Z["mp"], cB], wr=[CZ["P"]])
            cA = lambda m: mp[0:TT, 0, m, 0:TT]
            cBm = lambda m: mp[0:TT, 1, m, 0:TT]
            nlev = 2 if smp else 6
            for lv in range(nlev):
                pqa, pqab = k.psum(); pqb, pqbb = k.psum()
                for m in range(NM):
                    k.op(k.pe, lambda e: e.matmul(pqb[0:TT, m * 128:m * 128 + TT], lhsT=cA(m), rhs=cBm(m), start=True, stop=True), rd=[Z["mp"], Z["pwA"], Z["pwB"]], wr=[pqbb])
                    k.op(k.pe, lambda e: e.matmul(pqa[0:TT, m * 128:m * 128 + TT], lhsT=cBm(m), rhs=cA(m), start=True, stop=True), rd=[Z["mp"], Z["pwA"], Z["pwB"]], wr=[pqab])
                nw = pw[lv % 2]
                va = pqa[0:TT, 0:NM * 128].rearrange("p (m c) -> p m c", c=128)[:, :, 0:TT]
                vb_ = pqb[0:TT, 0:NM * 128].rearrange("p (m c) -> p m c", c=128)[:, :, 0:TT]
                k.op(k.act, lambda e: e.activation(out=nw[0:TT, 0, 0:NM, 0:TT], in_=va, func=AF.Copy), rd=[pqab], wr=[Z["pwA"]])
                k.op(k.dve, lambda e: e.tensor_copy(out=nw[0:TT, 1, 0:NM, 0:TT], in_=vb_), rd=[pqbb], wr=[Z["pwB"]])
                cA = lambda m, nw=nw: nw[0:TT, 0, m, 0:TT]
                cBm = lambda m, nw=nw: nw[0:TT, 1, m, 0:TT]
                yield
                pr, prb = k.psum()
                for m in range(NM):
                    k.op(k.pe, lambda e: e.matmul(pr[0:TT, m * 128:m * 128 + TT], lhsT=cA(m), rhs=P[0:TT, m, 0:TT], start=True, stop=True), rd=[Z["pwA"], CZ["P"]], wr=[prb])
                vp = pr[0:TT, 0:NM * 128].rearrange("p (m c) -> p m c", c=128)[:, :, 0:TT]
                k.op(k.dve, lambda e: e.tensor_tensor(out=P[0:TT, 0:NM, 0:TT], in0=vp, in1=P[0:TT, 0:NM, 0:TT], op=ALU.add), rd=[prb, CZ["P"]], wr=[CZ["P"]])
                yield
            pa, pab = k.psum(); py, pyb = k.psum()
            for ti in range(ntile):
                for e_ in range(2):
                    m = ti * 2 + e_
                    k.op(k.pe, lambda e: e.matmul(pa[:, ti * 128:ti * 128 + TT], lhsT=C["tp"][0:TT, ti, 0, e_ * 64:e_ * 64 + 128], rhs=P[0:TT, m, 0:TT], start=(e_ == 0), stop=(e_ == 1)),
                         rd=[CZ["tp"], CZ["P"]], wr=[pab])
                    k.op(k.pe, lambda e: e.matmul(py[0:TT, m * 64:(m + 1) * 64], lhsT=mp[0:TT, 2, m, 0:TT], rhs=C["tp"][0:TT, ti, 3, e_ * 128:e_ * 128 + 64], start=True, stop=True),
                         rd=[Z["mp"], CZ["tp"]], wr=[pyb])
            k.op(k.act, lambda e: e.activation(out=C["ah"][:, 0:ntile, 0:TT], in_=pa[:, 0:ntile * 128].rearrange("p (t c) -> p t c", c=128)[:, :, 0:TT], func=AF.Copy), rd=[pab], wr=[CZ["ah"]])
            k.op(k.dve, lambda e: e.tensor_copy(out=C["Yb"][0:TT, 0:ntile].rearrange("p t e v -> p (t e v)"), in_=py[0:TT, 0:NM * 64]), rd=[pyb], wr=[CZ["Yb"]])
            yield

        def chain(pi, t0, n, smp, par):
            C = CS[par]; CZ = C["Z"]
            TT = 64 if smp else 128
            ntile = n // TT
            P = C["P"]
            pc = lambda j: rw2[:, j, pi:pi + 1]
            if t0 == 0 and not smp:
                k.op(k.dve, lambda e: e.memset(S[:], 0.0), wr=[Z["S"]])
                k.op(k.dve, lambda e: e.memset(Sbd[:], 0.0), wr=[Z["S"]])
            if smp:
                for e_ in range(2):
                    k.dma(k.sp, S0s[64 * e_:64 * e_ + 64], L["srwT"][:, 2 * pi + e_].rearrange("b k v -> k b v"), wr=[Z["S0s"]])
                k.op(k.act, lambda e: e.activation(out=Sbds[0:64, :, 0:64], in_=S0s[0:64, :, :], func=AF.Copy), rd=[Z["S0s"]], wr=[Z["S0s"]])
                k.op(k.dve, lambda e: e.tensor_copy(out=Sbds[64:128, :, 64:128], in_=S0s[64:128, :, :]), rd=[Z["S0s"]], wr=[Z["S0s"]])
                k.op(k.dve, lambda e: e.tensor_tensor(out=ahm[:], in0=C["ah"][:, 0, 0:64].unsqueeze(1).to_broadcast([128, NS, 64]), in1=smkT[:], op=ALU.mult), rd=[CZ["ah"], cB], wr=[Z["msk"]])
                k.op(k.dve, lambda e: e.tensor_tensor(out=rtm[:], in0=C["rt"][:, 0:64].unsqueeze(1).to_broadcast([128, NS, 64]), in1=smkT[:], op=ALU.mult), rd=[CZ["rt"], cB], wr=[Z["msk"]])
            po, pob = k.psum()
            k.held.add(pob)
            for ti in range(ntile):
                sl = slice(ti * TT, (ti + 1) * TT)
                pu, pub = k.psum(exclude=[pob])
                for e_ in range(2):
                    k.op(k.pe, lambda e: e.matmul(pu[0:TT, e_ * 64:(e_ + 1) * 64], lhsT=P[0:TT, ti * 2 + e_, 0:TT], rhs=C["Yb"][0:TT, ti, e_, :], start=(e_ == 0), stop=False,
                                                  skip_group_check=True),
                         rd=[CZ["P"], CZ["Yb"]], wr=[pub])
                if not smp:
                    k.op(k.pe, lambda e: e.matmul(pu[0:TT, 0:128], lhsT=C["ah"][:, ti, 0:TT], rhs=Sbd[:, :], start=False, stop=True, skip_group_check=True), rd=[CZ["ah"], Z["S"]], wr=[pub])
                else:
                    for b in range(NS):
                        k.op(k.pe, lambda e: e.matmul(pu[0:TT, 0:128], lhsT=ahm[:, b, :], rhs=Sbds[:, b, :], start=False, stop=(b == NS - 1), skip_group_check=True), rd=[Z["msk"], Z["S0s"]], wr=[pub])
                k.op(k.dve, lambda e: e.tensor_copy(out=Ubp[0:TT, 0:64], in_=pu[0:TT, 0:64]), rd=[pub], wr=[Z["Ubp"]])
                k.op(k.act, lambda e: e.activation(out=Ubp[0:TT, 128:192], in_=pu[0:TT, 64:128], func=AF.Copy), rd=[pub], wr=[Z["Ubp"]])
                if smp:
                    yield
                oc = slice(ti * TT, (ti + 1) * TT)
                if not smp:
                    k.op(k.pe, lambda e: e.matmul(po[:, oc], lhsT=Sbd[:, :], rhs=C["rt"][:, sl], start=True, stop=False), rd=[Z["S"], CZ["rt"]], wr=[pob])
                else:
                    for b in range(NS):
                        k.op(k.pe, lambda e: e.matmul(po[:, oc], lhsT=Sbds[:, b, :], rhs=rtm[:, b, :], start=(b == 0), stop=False), rd=[Z["S0s"], Z["msk"]], wr=[pob])
                for e_ in range(2):
                    m = ti * 2 + e_
                    k.op(k.pe, lambda e: e.matmul(po[:, oc], lhsT=Ubp[0:TT, e_ * 64:e_ * 64 + 128], rhs=C["m34"][0:TT, 0, m, 0:TT], start=False, stop=False), rd=[Z["Ubp"], CZ["m34"]], wr=[pob])
                    k.op(k.pe, lambda e: e.matmul(po[:, oc], lhsT=C["tp"][0:TT, ti, 3, e_ * 64:e_ * 64 + 128], rhs=C["m34"][0:TT, 1, m, 0:TT], start=False, stop=(e_ == 1)), rd=[CZ["tp"], CZ["m34"]], wr=[pob])
                if smp:
                    yield
                if not smp:
                    pS, pSb = k.psum(exclude=[pob])
                    for e_ in range(2):
                        k.op(k.pe, lambda e: e.matmul(pS[:, 0:64], lhsT=C["tp"][0:TT, ti, 1, e_ * 64:e_ * 64 + 128], rhs=Ubp[0:TT, e_ * 128:e_ * 128 + 64], start=(e_ == 0), stop=False), rd=[CZ["tp"], Z["Ubp"]], wr=[pSb])
                        k.op(k.pe, lambda e: e.matmul(pS[:, 0:64], lhsT=C["tp"][0:TT, ti, 2, e_ * 64:e_ * 64 + 128], rhs=C["tp"][0:TT, ti, 3, e_ * 128:e_ * 128 + 64], start=False, stop=(e_ == 1)), rd=[CZ["tp"]], wr=[pSb])
                    k.op(k.dve, lambda e: e.tensor_tensor(out=S[:, :], in0=pS[:, 0:64], in1=S[:, :], op=ALU.add), rd=[pSb, Z["S"]], wr=[Z["S"]])
                    k.op(k.dve, lambda e: e.tensor_scalar(out=S[:, :], in0=S[:, :], scalar1=C["wc"][:, ti:ti + 1], scalar2=None, op0=ALU.mult), rd=[Z["S"], CZ["wc"]], wr=[Z["S"]])
                    k.op(k.act, lambda e: e.activation(out=Sbd[0:64, 0:64], in_=S[0:64, :], func=AF.Copy), rd=[Z["S"]], wr=[Z["S"]])
                    k.op(k.dve, lambda e: e.tensor_copy(out=Sbd[64:128, 64:128], in_=S[64:128, :]), rd=[Z["S"]], wr=[Z["S"]])
                    yield
                else:
                    for e_ in range(2):
                        k.op(k.dve, lambda e: e.tensor_tensor(out=Ublk[:, e_], in0=Ubp[0:64, e_ * 128:e_ * 128 + 64].unsqueeze(1).to_broadcast([64, NS, 64]), in1=smk[:].unsqueeze(2).to_broadcast([64, NS, 64]), op=ALU.mult), rd=[Z["Ubp"], cB], wr=[Z["msk"]])
                        k.op(k.dve, lambda e: e.tensor_tensor(out=Vblk[:, e_], in0=C["tp"][0:64, 0, 3, e_ * 128:e_ * 128 + 64].unsqueeze(1).to_broadcast([64, NS, 64]), in1=smk[:].unsqueeze(2).to_broadcast([64, NS, 64]), op=ALU.mult), rd=[CZ["tp"], cB], wr=[Z["msk"]])
                    for half in range(2):
                        pS, pSb = k.psum(exclude=[pob])
                        for b8 in range(8):
                            b = half * 8 + b8
                            for e_ in range(2):
                                k.op(k.pe, lambda e: e.matmul(pS[:, b8 * 64:(b8 + 1) * 64], lhsT=C["tp"][0:64, 0, 1, e_ * 64:e_ * 64 + 128], rhs=Ublk[:, e_, b, :], start=(e_ == 0), stop=False), rd=[CZ["tp"], Z["msk"]], wr=[pSb])
                                k.op(k.pe, lambda e: e.matmul(pS[:, b8 * 64:(b8 + 1) * 64], lhsT=C["tp"][0:64, 0, 2, e_ * 64:e_ * 64 + 128], rhs=Vblk[:, e_, b, :], start=False, stop=(e_ == 1)), rd=[CZ["tp"], Z["msk"]], wr=[pSb])
                        sv = S0s[:, half * 8:(half + 1) * 8, :]
                        k.op(k.dve, lambda e: e.tensor_tensor(out=sv, in0=pS[:, 0:512].rearrange("p (b v) -> p b v", v=64), in1=sv, op=ALU.add), rd=[pSb, Z["S0s"]], wr=[Z["S0s"]])
                        k.op(k.dve, lambda e: e.tensor_tensor(out=sv, in0=sv, in1=C["wc"][:, half * 8:(half + 1) * 8].unsqueeze(2).to_broadcast([128, 8, 64]), op=ALU.mult), rd=[Z["S0s"], CZ["wc"]], wr=[Z["S0s"]])
                        yield
                    for e_ in range(2):
                        k.dma(k.pool, L["nrw_sT"][:, 2 * pi + e_].rearrange("b k v -> k b v"), S0s[64 * e_:64 * e_ + 64], rd=[Z["S0s"]])
            k.op(k.act, lambda e: e.activation(out=oT[:, 0:n], in_=po[:, 0:n], func=AF.Copy), rd=[pob], wr=[Z["oT"]])
            k.held.discard(pob)
            ps, pb = k.psum()
            k.op(k.pe, lambda e: e.matmul(ps[:, 0:n], lhsT=bones[:, :], rhs=oT[:, 0:n], start=True, stop=True), rd=[cB, Z["oT"]], wr=[pb])
            k.op(k.dve, lambda e: e.scalar_tensor_tensor(out=oT[:, 0:n], in0=ps[:, 0:n], scalar=-1.0 / 64, in1=oT[:, 0:n], op0=ALU.mult, op1=ALU.add), rd=[pb, Z["oT"]], wr=[Z["oT"]])
            k.op(k.dve, lambda e: e.tensor_tensor(out=t7c[:, 0:n], in0=oT[:, 0:n], in1=oT[:, 0:n], op=ALU.mult), rd=[Z["oT"]], wr=[Z["t7c"]])
            if not smp:
                yield
            ps, pb = k.psum()
            k.op(k.pe, lambda e: e.matmul(ps[:, 0:n], lhsT=bones[:, :], rhs=t7c[:, 0:n], start=True, stop=True), rd=[cB, Z["t7c"]], wr=[pb])
            k.op(k.dve, lambda e: e.tensor_scalar(out=t7c[:, 0:n], in0=ps[:, 0:n], scalar1=1.0 / 64, scalar2=64e-5, op0=ALU.mult, op1=ALU.add), rd=[pb], wr=[Z["t7c"]])
            k.op(k.act, lambda e: e.activation(out=t7c[:, 0:n], in_=t7c[:, 0:n], func=AF.Ln), rd=[Z["t7c"]], wr=[Z["t7c"]])
            k.op(k.act, lambda e: e.activation(out=t7c[:, 0:n], in_=t7c[:, 0:n], func=AF.Exp, scale=-0.5), rd=[Z["t7c"]], wr=[Z["t7c"]])
            k.op(k.dve, lambda e: e.tensor_tensor(out=oT[:, 0:n], in0=oT[:, 0:n], in1=t7c[:, 0:n], op=ALU.mult), rd=[Z["oT"], Z["t7c"]], wr=[Z["oT"]])
            k.op(k.dve, lambda e: e.tensor_scalar(out=oT[:, 0:n], in0=oT[:, 0:n], scalar1=pc(8), scalar2=pc(9), op0=ALU.mult, op1=ALU.add), rd=[Z["oT"], cB], wr=[Z["oT"]])
            k.op(k.dve, lambda e: e.tensor_tensor(out=oT[:, 0:n], in0=oT[:, 0:n], in1=C["bv"][:, 0:n], op=ALU.add), rd=[Z["oT"], CZ["bv"]], wr=[Z["oT"]])
            k.op(k.dve, lambda e: e.tensor_tensor(out=yb[:, 0:n], in0=oT[:, 0:n], in1=C["sg"][:, 0:n], op=ALU.mult), rd=[Z["oT"], CZ["sg"]], wr=[Z["y"]])
            k.dma(k.pool, L["ymix_d"][16 + pi, :, t0:t0 + n], yb[:, 0:n], rd=[Z["y"]])
            if (not smp) and t0 + n == T:
                k.dma(k.pool, L["nrw_pT"][2 * pi:2 * pi + 2].rearrange("e k v -> (e k) v"), S[:, :], rd=[Z["S"]])
            yield

        blocks = []
        for pi in range(RW_PAIRS):
            for bi in range(T // NB):
                blocks.append((pi, bi * NB, NB, False))
            if RW_SAMPLE:
                blocks.append((pi, T, TSM, True))
        nb_ = len(blocks)
        for j in range(nb_ + 2):
            gens = []
            if j < nb_:
                b_ = blocks[j]; gens.append([prepA(b_[0], b_[1], b_[2], b_[3], j % 3, j % 2), 0, 20 if b_[3] else 22])
            if 0 <= j - 1 < nb_:
                b_ = blocks[j - 1]; gens.append([prepB(b_[0], b_[1], b_[2], b_[3], (j - 1) % 3, (j - 1) % 2), 0, 7 if b_[3] else 15])
            if 0 <= j - 2 < nb_:
                b_ = blocks[j - 2]; gens.append([chain(b_[0], b_[1], b_[2], b_[3], (j - 2) % 3), 0, 6 if b_[3] else 5])
            while gens:
                g = min(gens, key=lambda x: (x[1] + 0.5) / x[2])
                try:
                    next(g[0])
                    g[1] += 1
                except StopIteration:
                    gens.remove(g)
        k.barrier()
```

```python
import math
import numpy as np
from contextlib import ExitStack
import concourse.bass as bass
import concourse.mybir as mybir
from concourse.bass_utils import run_bass_kernel_spmd

F32 = mybir.dt.float32
BF16 = mybir.dt.bfloat16
AF = mybir.ActivationFunctionType
ALU = mybir.AluOpType
AX = mybir.AxisListType

D = 2048
NIN = 16576
T = 2048
NS = 16
TSM = 64
KC = 16
STAGE = 99
DEBUG = False
RW_PAIRS = 16
RW_SAMPLE = True


class Buf:
    __slots__ = ("w", "r")

    def __init__(self):
        self.w = None
        self.r = {}


class Eng:
    def __init__(self, nc, eng, key, sems, is_pe=False):
        self.eng = eng
        self.key = key
        self.sem = sems[key]
        self.cnt = 0
        self.waited = {}
        self.is_pe = is_pe
        self.slots = []
        self.uses = []
        self.n = 0


class K:
    def __init__(self, nc, es):
        self.nc = nc
        self.es = es
        self.sems = {}
        for k in ["pe", "act", "dve", "pool", "sp"]:
            self.sems[k] = es.enter_context(nc.semaphore("s_" + k))
        self.pe = Eng(nc, nc.tensor, "pe", self.sems, True)
        self.act = Eng(nc, nc.scalar, "act", self.sems)
        self.dve = Eng(nc, nc.vector, "dve", self.sems)
        self.pool = Eng(nc, nc.gpsimd, "pool", self.sems)
        self.sp = Eng(nc, nc.sync, "sp", self.sems)
        for q, n in ((self.sp, 8), (self.pool, 8)):
            for i in range(n):
                k = "%s_d%d" % (q.key, i)
                self.sems[k] = es.enter_context(nc.semaphore("s_" + k))
                q.slots.append(k)
                q.uses.append(0)
        self.engs = [self.pe, self.act, self.dve, self.pool, self.sp]
        self.nps = 0
        self.ps = []
        self.psb = []
        self.evq = 0
        self.held = set()

    def _deps(self, en, rd, wr, extra=None):
        deps = dict(extra or {})
        for b in rd:
            if b.w:
                k, v = b.w
                deps[k] = max(deps.get(k, 0), v)
        for b in wr:
            if b.w:
                k, v = b.w
                deps[k] = max(deps.get(k, 0), v)
            for k, v in b.r.items():
                deps[k] = max(deps.get(k, 0), v)
        for k, v in deps.items():
            if en.is_pe and k == en.key:
                continue
            if en.waited.get(k, 0) < v:
                en.eng.wait_ge(self.sems[k], v)
                en.waited[k] = v

    def _post(self, ev, rd, wr):
        k, v = ev
        for b in rd:
            b.r[k] = max(b.r.get(k, 0), v)
        for b in wr:
            b.w = ev
            b.r = {}

    def op(self, en, fn, rd=(), wr=()):
        self._deps(en, rd, wr)
        ins = fn(en.eng)
        en.cnt += 1
        ins.then_inc(en.sem, 1)
        self._post((en.key, en.cnt), rd, wr)

    def dma(self, q, out, in_, rd=(), wr=()):
        s = q.n % len(q.slots)
        q.n += 1
        k = q.slots[s]
        q.uses[s] += 1
        u = q.uses[s]
        extra = {k: 16 * (u - 1)} if u > 1 else None
        self._deps(q, rd, wr, extra)
        ins = q.eng.dma_start(out=out, in_=in_)
        ins.then_inc(self.sems[k], 16)
        self._post((k, 16 * u), rd, wr)

    def barrier(self):
        tgt = {}
        for e in self.engs:
            if e.cnt:
                tgt[e.key] = e.cnt
            for k, u in zip(e.slots, e.uses):
                if u:
                    tgt[k] = 16 * u
        for e in self.engs:
            for k, v in tgt.items():
                if k == e.key:
                    continue
                if e.waited.get(k, 0) < v:
                    e.eng.wait_ge(self.sems[k], v)
                    e.waited[k] = v

    def finish(self):
        self.barrier()

    def psum(self, exclude=()):
        while True:
            i = self.nps % 8
            self.nps += 1
            if self.psb[i] not in exclude and self.psb[i] not in self.held:
                return self.ps[i], self.psb[i]

    def ev_eng(self):
        self.evq += 1
        return self.dve if self.evq % 2 else self.act


def build(stage=STAGE, debug=DEBUG):
    nc = bass.Bass("TRN2", target_bir_lowering=False)

    def din(name, shape, dt=F32):
        return nc.dram_tensor(name, list(shape), dt, kind="ExternalInput").ap()

    def dout(name, shape, dt=F32):
        return nc.dram_tensor(name, list(shape), dt, kind="ExternalOutput").ap()

    xT = din("xT", [128, KC, T])
    x = din("x", [T, D])
    xsT = din("xsT", [128, KC, TSM])
    xs = din("xs", [TSM, D])
    hpT = din("hpT", [128, KC, NS])
    cT = din("cT", [128, KC, 17])
    sret = din("sret", [NS, 8, 256, 256])
    srwT = din("srwT", [NS, 32, 64, 64])
    w_ada = din("w_ada", [D, 3 * D])
    b_adaT = din("b_adaT", [128, 48])
    b_ada_rows = din("b_ada_rows", [17, 3 * D])
    w_in = din("w_in", [D, NIN])
    w_out = din("w_out", [2 * D, D])
    norm_wT = din("norm_wT", [128, KC])
    fnw_b = din("fnw_b", [128, D])
    rwp2 = din("rwp2", [128, 10, 16])
    bones = din("bones", [128, 128])
    rwp = din("rwp", [64, 10, 32])
    mu_lora = din("mu_lora", [96, 2])
    w2d = din("w2d", [96, D])
    a2 = din("a2", [96, D])
    ident = din("ident", [128, 128])
    cmask = din("cmask", [128, 6, 128])
    seqmask = din("seqmask", [64, NS])
    seqmaskT = din("seqmaskT", [128, NS, 64])
    rope = din("rope", [128, 4, T])
    ropes = din("ropes", [128, 2, TSM])
    rdec = din("rdec", [128, 8, 4, 128])
    ones = din("ones", [128, 128])
    selp = din("selp", [17, 128])
    sels = din("sels", [17, 64])

    y = dout("y", [T, D])
    ys = dout("ys", [TSM, D])
    nret_p = dout("nret_p", [8, 256, 256])
    nrw_pT = dout("nrw_pT", [32, 64, 64])
    nshift_p = dout("nshift_p", [128, KC])
    nret_s = dout("nret_s", [NS, 8, 256, 256])
    nrw_sT = dout("nrw_sT", [NS, 32, 64, 64])
    nshift_s = dout("nshift_s", [128, KC, NS])
    xnew_d = nc.dram_tensor("xnew_d", [T + TSM, D], F32, kind="Internal").ap()
    ymix_d = nc.dram_tensor("ymix_d", [32, 128, T + TSM], BF16, kind="Internal").ap()
    dbg = {}
    if debug:
        dbg["hT"] = dout("dbg_hT", [128, KC, T + 80])
        dbg["modT"] = dout("dbg_modT", [128, 48, 17])

    with ExitStack() as es:
        k = K(nc, es)

        _cnt = [0]

        def sb(name, shape, dt=F32, stack=es):
            _cnt[0] += 1
            return stack.enter_context(nc.sbuf_tensor("%s_%d" % (name, _cnt[0]), list(shape), dt))

        for i in range(8):
            k.ps.append(es.enter_context(nc.psum_tensor("ps%d" % i, [128, 512], F32)))
            k.psb.append(Buf())

        identf = sb("identf", [128, 128]); identb = sb("identb", [128, 128], BF16)
        onesf = sb("onesf", [128, 128])
        cm = sb("cm", [128, 6, 128])
        smk = sb("smk", [64, NS]); smkT = sb("smkT", [128, NS, 64])
        nwT = sb("nwT", [128, KC]); baT = sb("baT", [128, 48])
        rw = sb("rw", [64, 10, 32]); mul = sb("mul", [96, 2])
        cTs = sb("cTs", [128, KC, 17])
        modT = sb("modT", [128, 48, 17])
        gT = sb("gT", [128, KC, 17]); shT = sb("shT", [128, KC, 17])
        hst = ExitStack()
        hlast = sb("hlast", [128, KC], F32, hst); hlast_s = sb("hlast_s", [128, KC, NS], F32, hst)
        hT = sb("hT", [128, KC, T], BF16, hst)
        hTs = sb("hTs", [128, KC, 80], BF16, hst)
        B = {n: Buf() for n in ["const", "cTs", "modT", "gsh", "hT", "hTs", "misc"]}
        for dst, src in ((identf, ident), (onesf, ones), (cm, cmask), (smk, seqmask), (smkT, seqmaskT),
                         (nwT, norm_wT), (baT, b_adaT), (rw, rwp), (mul, mu_lora)):
            k.dma(k.sp, dst[:], src, wr=[B["const"]])
        k.dma(k.sp, cTs[:], cT, wr=[B["cTs"]])
        k.op(k.dve, lambda e: e.tensor_copy(out=identb[:], in_=identf[:]), rd=[B["const"]], wr=[B["const"]])
        k.op(k.act, lambda e: e.activation(out=cTs[:], in_=cTs[:], func=AF.Silu), rd=[B["cTs"]], wr=[B["cTs"]])

        with ExitStack() as ph:
            wa = [sb("wa%d" % i, [128, KC, 512], F32, ph) for i in range(2)]
            wab = [Buf(), Buf()]
            modrow = sb("modrow", [17, 3 * D], F32, ph); mrb = Buf()
            barow = sb("barow", [17, 3 * D], F32, ph); brb = Buf()
            k.dma(k.sp, barow[:], b_ada_rows, wr=[brb])
            for g in range(12):
                i = g % 2
                k.dma(k.sp, wa[i][:], w_ada[:, g * 512:(g + 1) * 512].rearrange("(c p) n -> p c n", p=128), wr=[wab[i]])
                ps, pb = k.psum()
                for c in range(KC):
                    k.op(k.pe, lambda e, c=c, i=i, ps=ps: e.matmul(ps[0:17, 0:512], lhsT=cTs[:, c, :], rhs=wa[i][:, c, :],
                                                                   start=(c == 0), stop=(c == KC - 1)), rd=[wab[i], B["cTs"]], wr=[pb])
                k.op(k.dve, lambda e, ps=ps, g=g: e.tensor_tensor(out=modrow[:, g * 512:(g + 1) * 512], in0=ps[0:17, 0:512],
                                                                  in1=barow[:, g * 512:(g + 1) * 512], op=ALU.add), rd=[pb, brb], wr=[mrb])
            for g4 in range(12):
                ps, pb = k.psum()
                for q in range(4):
                    fc = g4 * 4 + q
                    k.op(k.pe, lambda e, ps=ps, q=q, fc=fc: e.transpose(ps[:, q * 17:(q + 1) * 17], modrow[:, fc * 128:(fc + 1) * 128], identf[0:17, 0:17]),
                         rd=[mrb, B["const"]], wr=[pb])
                k.op(k.dve, lambda e, ps=ps, g4=g4: e.tensor_copy(out=modT[:, g4 * 4:(g4 + 1) * 4, :], in_=ps[:, 0:68].rearrange("p (q r) -> p q r", r=17)),
                     rd=[pb], wr=[B["modT"]])
            k.op(k.dve, lambda e: e.tensor_scalar(out=gT[:], in0=modT[:, 16:32, :], scalar1=1.0, scalar2=None, op0=ALU.add),
                 rd=[B["modT"]], wr=[B["gsh"]])
            k.op(k.dve, lambda e: e.tensor_tensor(out=gT[:], in0=gT[:], in1=nwT[:].unsqueeze(2).to_broadcast([128, KC, 17]), op=ALU.mult),
                 rd=[B["gsh"], B["const"]], wr=[B["gsh"]])
            k.op(k.dve, lambda e: e.tensor_copy(out=shT[:], in_=modT[:, 0:16, :]), rd=[B["modT"]], wr=[B["gsh"]])
            k.barrier()
        if debug:
            k.dma(k.pool, dbg["modT"], modT[:], rd=[B["modT"]])

        with ExitStack() as ph:
            xt = [sb("xt%d" % i, [128, KC, 512], F32, ph) for i in range(2)]
            xtb = [Buf(), Buf()]
            sq = sb("sq", [128, 512], F32, ph); sqb = Buf()
            rstd = sb("rstd", [128, 512], F32, ph); rsb = Buf()
            tmp = [sb("tmpn%d" % i, [128, 512], F32, ph) for i in range(2)]
            tmpb = [Buf(), Buf()]
            groups = [(xT, tg * 512, 512, tg) for tg in range(4)] + [(xsT, 0, TSM, 4)]
            for (src, t0, n, gi) in groups:
                i = gi % 2
                k.dma(k.sp, xt[i][:, :, 0:n], src[:, :, t0:t0 + n], wr=[xtb[i]])
                ps, pb = k.psum()
                for c in range(KC):
                    j2 = c % 2
                    k.op(k.act, lambda e, c=c, i=i, n=n, j2=j2: e.activation(out=tmp[j2][:, 0:n], in_=xt[i][:, c, 0:n], func=AF.Square),
                         rd=[xtb[i]], wr=[tmpb[j2]])
                    if c == 0:
                        k.op(k.dve, lambda e, n=n, j2=j2: e.tensor_copy(out=sq[:, 0:n], in_=tmp[j2][:, 0:n]), rd=[tmpb[j2]], wr=[sqb])
                    else:
                        k.op(k.dve, lambda e, n=n, j2=j2: e.tensor_tensor(out=sq[:, 0:n], in0=sq[:, 0:n], in1=tmp[j2][:, 0:n], op=ALU.add), rd=[tmpb[j2], sqb], wr=[sqb])
                k.op(k.pe, lambda e, ps=ps, n=n: e.matmul(ps[:, 0:n], lhsT=onesf[:, :], rhs=sq[:, 0:n], start=True, stop=True),
                     rd=[sqb, B["const"]], wr=[pb])
                k.op(k.dve, lambda e, ps=ps, n=n: e.tensor_scalar(out=rstd[:, 0:n], in0=ps[:, 0:n], scalar1=1.0 / D, scalar2=1e-6,
                                                              op0=ALU.mult, op1=ALU.add), rd=[pb], wr=[rsb])
                k.op(k.act, lambda e, n=n: e.activation(out=rstd[:, 0:n], in_=rstd[:, 0:n], func=AF.Sqrt), rd=[rsb], wr=[rsb])
                k.op(k.dve, lambda e, n=n: e.reciprocal(out=rstd[:, 0:n], in_=rstd[:, 0:n]), rd=[rsb], wr=[rsb])
                for c in range(KC):
                    j = c % 2
                    k.op(k.dve, lambda e, c=c, i=i, j=j, n=n: e.tensor_tensor(out=tmp[j][:, 0:n], in0=xt[i][:, c, 0:n], in1=rstd[:, 0:n], op=ALU.mult),
                         rd=[xtb[i], rsb], wr=[tmpb[j]])
                    if gi < 4:
                        k.op(k.act, lambda e, c=c, j=j, t0=t0, n=n: e.activation(out=hT[:, c, t0:t0 + n], in_=tmp[j][:, 0:n], func=AF.Identity,
                                                                             scale=gT[:, c, 0:1], bias=shT[:, c, 0:1]),
                             rd=[tmpb[j], B["gsh"]], wr=[B["hT"]])
                        if gi == 3:
                            k.op(k.act, lambda e, c=c, j=j: e.activation(out=hlast[:, c:c + 1], in_=tmp[j][:, 511:512], func=AF.Identity,
                                                                         scale=gT[:, c, 0:1], bias=shT[:, c, 0:1]),
                                 rd=[tmpb[j], B["gsh"]], wr=[B["misc"]])
                    else:
                        v3 = tmp[j][:, 0:TSM].rearrange("p (s j) -> p s j", j=4)
                        k.op(k.dve, lambda e, c=c, v3=v3: e.tensor_tensor(out=v3, in0=v3, in1=gT[:, c, 1:17].unsqueeze(2).to_broadcast([128, NS, 4]), op=ALU.mult),
                             rd=[tmpb[j], B["gsh"]], wr=[tmpb[j]])
                        k.op(k.dve, lambda e, c=c, v3=v3: e.tensor_tensor(out=v3, in0=v3, in1=shT[:, c, 1:17].unsqueeze(2).to_broadcast([128, NS, 4]), op=ALU.add),
                             rd=[tmpb[j], B["gsh"]], wr=[tmpb[j]])
                        k.op(k.act, lambda e, c=c, j=j: e.activation(out=hTs[:, c, 0:TSM], in_=tmp[j][:, 0:TSM], func=AF.Copy),
                             rd=[tmpb[j]], wr=[B["hTs"]])
                        k.op(k.dve, lambda e, c=c, v3=v3: e.tensor_copy(out=hlast_s[:, c, :], in_=v3[:, :, 3]), rd=[tmpb[j]], wr=[B["misc"]])
            k.dma(k.sp, xt[0][:, :, 0:NS], hpT, wr=[xtb[0]])
            k.op(k.dve, lambda e: e.tensor_copy(out=hTs[:, :, TSM:TSM + NS], in_=xt[0][:, :, 0:NS]), rd=[xtb[0]], wr=[B["hTs"]])
            k.dma(k.pool, nshift_p, hlast[:], rd=[B["misc"]])
            k.dma(k.pool, nshift_s, hlast_s[:], rd=[B["misc"]])
            k.barrier()
        if debug:
            with ExitStack() as ph:
                d32 = sb("d32", [128, KC, 512], F32, ph); db = Buf()
                for tg in range(4):
                    k.op(k.dve, lambda e, tg=tg: e.tensor_copy(out=d32[:], in_=hT[:, :, tg * 512:(tg + 1) * 512]), rd=[B["hT"]], wr=[db])
                    k.dma(k.pool, dbg["hT"][:, :, tg * 512:(tg + 1) * 512], d32[:], rd=[db])
                k.op(k.dve, lambda e: e.tensor_copy(out=d32[:, :, 0:80], in_=hTs[:]), rd=[B["hTs"]], wr=[db])
                k.dma(k.pool, dbg["hT"][:, :, T:T + 80], d32[:, :, 0:80], rd=[db])
                k.barrier()
        if stage >= 2:
            phase_ret(nc, k, sb, locals())
        if stage >= 3:
            phase_rwkv(nc, k, sb, locals())
        k.barrier()
        hst.close()
        if stage >= 4:
            phase_out(nc, k, sb, locals())
        k.finish()
    return nc


def _bf(a):
    return np.ascontiguousarray(a, dtype=np.float32)


def host_inputs(inp, core):
    pb = core % 4
    s0 = core * NS
    xp = np.asarray(inp["x_prompt"][pb], np.float32)
    xsm = np.asarray(inp["x_sample"][s0:s0 + NS], np.float32).reshape(TSM, D)
    cc = np.concatenate([np.asarray(inp["c_prompt"][pb:pb + 1]), np.asarray(inp["c_sample"][s0:s0 + NS])], 0).astype(np.float32)

    def tr(a):
        return _bf(a.reshape(a.shape[0], KC, 128).transpose(2, 1, 0))

    def colv(v, nch):
        return _bf(np.asarray(v, np.float32).reshape(nch, 128).T)

    m = {}
    m["xT"] = tr(xp); m["x"] = _bf(xp); m["xsT"] = tr(xsm); m["xs"] = _bf(xsm)
    m["hpT"] = tr(np.asarray(inp["state_shift"][0, s0:s0 + NS], np.float32))
    m["cT"] = tr(cc)
    m["sret"] = _bf(inp["state_ret"][0, s0:s0 + NS])
    m["srwT"] = _bf(np.asarray(inp["state_rwkv"][0, s0:s0 + NS]).transpose(0, 1, 3, 2))
    m["w_ada"] = _bf(inp["w_ada"][0]); m["b_adaT"] = colv(inp["b_ada"][0], 48)
    m["b_ada_rows"] = _bf(np.broadcast_to(np.asarray(inp["b_ada"][0], np.float32)[None, :], (17, 3 * D)))
    m["w_in"] = _bf(inp["w_in"][0]); m["w_out"] = _bf(inp["w_out"][0])
    m["norm_wT"] = colv(inp["norm_w"][0], KC)
    m["fnw_b"] = _bf(np.broadcast_to(np.asarray(inp["final_norm_w"], np.float32)[None, :], (128, D)))
    mu = np.asarray(inp["mu_shift"][0], np.float32)

    def hv(v):
        return np.asarray(v, np.float32).reshape(32, 64).T

    rwp = np.stack([hv(mu[0:D]), hv(mu[D:2 * D]), hv(mu[2 * D:3 * D]), hv(inp["w0_decay"][0]), hv(inp["a0"][0]),
                    hv(inp["k_k"][0]), hv(inp["k_a"][0]), hv(np.asarray(inp["r_k"][0]).reshape(-1)),
                    hv(inp["ln_x_w"][0]), hv(inp["ln_x_b"][0])], 1)
    m["rwp"] = _bf(rwp)
    m["rwp2"] = _bf(np.concatenate([rwp[:, :, 0::2], rwp[:, :, 1::2]], 0))
    m["mu_lora"] = _bf(np.stack([mu[3 * D:3 * D + 96], mu[3 * D + 96:3 * D + 192]], 1))
    m["w2d"] = _bf(inp["w2_decay"][0]); m["a2"] = _bf(inp["a2"][0])
    m.update(CONSTS)
    return m


def make_consts():
    c = {}
    c["ident"] = np.eye(128, dtype=np.float32)
    c["ones"] = np.ones((128, 128), np.float32)
    i = np.arange(128)
    cm = np.zeros((128, 6, 128), np.float32)
    cm[:, 0, :] = (i[None, :] >= i[:, None])
    cm[:, 1, :] = (i[None, :] < i[:, None])
    cm[:, 2, :] = (i[None, :] > i[:, None])
    blk = (i[:, None] // 4 == i[None, :] // 4) & (i[:, None] < 64) & (i[None, :] < 64)
    for q in range(3):
        cm[:, 3 + q, :] = cm[:, q, :] * blk
    c["cmask"] = cm
    sm = np.zeros((64, NS), np.float32)
    sm[np.arange(64), np.arange(64) // 4] = 1
    c["seqmask"] = sm
    c["seqmaskT"] = np.ascontiguousarray(np.broadcast_to(sm.T[None], (128, NS, 64))).astype(np.float32)
    half = 128
    inv = (10000.0 ** (-np.arange(half, dtype=np.float32) / half)).astype(np.float32)
    pos = np.arange(T, dtype=np.float32)
    ang = (pos[None, :] * inv[:, None]).astype(np.float32)
    rope = np.zeros((128, 4, T), np.float32)
    rope[:, 0] = np.cos(ang); rope[:, 1] = np.sin(ang)
    c["rope"] = rope
    poss = (16384.0 + (np.arange(TSM) % 4)).astype(np.float32)
    angs = (poss[None, :] * inv[:, None]).astype(np.float32)
    c["ropes"] = np.stack([np.cos(angs), np.sin(angs)], 1).astype(np.float32)
    rdec = np.zeros((128, 8, 4, 128), np.float32)
    for h in range(8):
        lg = np.log1p(-np.exp2(-5.0 - h))
        ip = np.arange(128)
        rdec[:, h, 0, :] = np.exp((ip + 1) * lg)[None]
        rdec[:, h, 1, :] = (np.exp(-(ip + 1) * lg) * 256 ** -0.5)[None]
        isq = np.arange(64) % 4
        rdec[:, h, 2, :64] = np.exp((isq + 1) * lg)[None]
        rdec[:, h, 3, :64] = (np.exp(-(isq + 1) * lg) * 256 ** -0.5)[None]
    c["rdec"] = rdec
    bo = np.zeros((128, 128), np.float32); bo[:64, :64] = 1; bo[64:, 64:] = 1
    c["bones"] = bo
    sp_ = np.zeros((17, 128), np.float32); sp_[0] = 1
    ss_ = np.zeros((17, 64), np.float32); ss_[1:, :] = sm.T
    c["selp"] = sp_; c["sels"] = ss_
    return c


CONSTS = make_consts()
_NC = {}


def run(inputs, stage=STAGE, debug=DEBUG):
    key = (stage, debug)
    if key not in _NC:
        _NC[key] = build(stage, debug)
    nc = _NC[key]
    in_maps = [host_inputs(inputs, c) for c in range(8)]
    res = run_bass_kernel_spmd(nc, in_maps, core_ids=list(range(8)))
    return res.results


def kernel(**inputs):
    r = run(inputs)
    y_prompt = np.stack([r[c]["y"] for c in range(4)], 0)
    y_sample = np.concatenate([r[c]["ys"].reshape(NS, 4, D) for c in range(8)], 0)
    nret_p = np.stack([r[c]["nret_p"] for c in range(4)], 0)[None]
    nrw_p = np.stack([r[c]["nrw_pT"].transpose(0, 2, 1) for c in range(4)], 0)[None]
    nsh_p = np.stack([r[c]["nshift_p"].T.reshape(D) for c in range(4)], 0)[None]
    nret_s = np.concatenate([r[c]["nret_s"] for c in range(8)], 0)[None]
    nrw_s = np.concatenate([r[c]["nrw_sT"].transpose(0, 1, 3, 2) for c in range(8)], 0)[None]
    nsh_s = np.concatenate([r[c]["nshift_s"].transpose(2, 1, 0).reshape(NS, D) for c in range(8)], 0)[None]
    f = lambda a: np.ascontiguousarray(a, dtype=np.float32)
    return (f(y_prompt), f(y_sample), f(nret_p), f(nrw_p), f(nsh_p), f(nret_s), f(nrw_s), f(nsh_s))


class WStream:
    def __init__(self, nc, k, sb, stack, w_in, hT, hTs, Bh, Bhs, alloc=True):
        self.nc, self.k, self.w_in, self.hT, self.hTs, self.Bh, self.Bhs = nc, k, w_in, hT, hTs, Bh, Bhs
        self.n = 0
        if not alloc:
            return
        self.stg = sb("wstg", [128, KC, 256], F32, stack); self.stgb = Buf()
        self.wbig = sb("wbig", [128, KC, 512], BF16, stack)
        self.wb = [self.wbig[:, :, 0:256], self.wbig[:, :, 256:512]]
        self.wbb = [Buf(), Buf()]
        self.n = 0

    def tm2(self, t0, n, sample=False):
        k = self.k
        ps, pb = k.psum()
        src, sbf = (self.hTs, self.Bhs) if sample else (self.hT, self.Bh)
        for c in range(KC):
            k.op(k.pe, lambda e, c=c: e.matmul(ps[0:n, 0:512], lhsT=src[:, c, t0:t0 + n], rhs=self.wbig[:, c, :],
                                              start=(c == 0), stop=(c == KC - 1)), rd=[self.wbb[0], self.wbb[1], sbf], wr=[pb])
        return ps, pb

    def load(self, col0, ncol):
        k = self.k
        i = self.n % 2
        self.n += 1
        k.dma(k.sp, self.stg[:, :, 0:ncol], self.w_in[:, col0:col0 + ncol].rearrange("(c p) n -> p c n", p=128), wr=[self.stgb])
        k.op(k.pool, lambda e: e.tensor_copy(out=self.wb[i][:, :, 0:ncol], in_=self.stg[:, :, 0:ncol]), rd=[self.stgb], wr=[self.wbb[i]])
        return self.wb[i], self.wbb[i]

    def fm(self, wt, wtb, c0, m, t0, n, sample=False):
        k = self.k
        ps, pb = k.psum()
        src, sbf = (self.hTs, self.Bhs) if sample else (self.hT, self.Bh)
        for c in range(KC):
            k.op(k.pe, lambda e, c=c: e.matmul(ps[0:m, 0:n], lhsT=wt[:, c, c0:c0 + m], rhs=src[:, c, t0:t0 + n],
                                              start=(c == 0), stop=(c == KC - 1)), rd=[wtb, sbf], wr=[pb])
        return ps, pb

    def fm_gen(self, wt, wtb, c0, m, t0, n, sample=False, every=4):
        k = self.k
        ps, pb = k.psum()
        k.held.add(pb)
        src, sbf = (self.hTs, self.Bhs) if sample else (self.hT, self.Bh)
        for c in range(KC):
            k.op(k.pe, lambda e, c=c: e.matmul(ps[0:m, 0:n], lhsT=wt[:, c, c0:c0 + m], rhs=src[:, c, t0:t0 + n],
                                              start=(c == 0), stop=(c == KC - 1)), rd=[wtb, sbf], wr=[pb])
            if c % every == every - 1 and c < KC - 1:
                yield
        k.held.discard(pb)
        return ps, pb

    def tm(self, wt, wtb, ncol, t0, n, sample=False):
        k = self.k
        ps, pb = k.psum()
        src, sbf = (self.hTs, self.Bhs) if sample else (self.hT, self.Bh)
        for c in range(KC):
            k.op(k.pe, lambda e, c=c: e.matmul(ps[0:n, 0:ncol], lhsT=src[:, c, t0:t0 + n], rhs=wt[:, c, 0:ncol],
                                              start=(c == 0), stop=(c == KC - 1)), rd=[wtb, sbf], wr=[pb])
        return ps, pb


def phase_ret(nc, k, sb, L):
    hT, hTs, B = L["hT"], L["hTs"], L["B"]
    cm, smk, smkT, identb = L["cm"], L["smk"], L["smkT"], L["identb"]
    TA = T + TSM
    with ExitStack() as ph:
        ws = WStream(nc, k, sb, ph, L["w_in"], hT, hTs, B["hT"], B["hTs"])
        cosT = sb("cosT", [128, TA], F32, ph); sinT = sb("sinT", [128, TA], F32, ph); rb = Buf()
        k.dma(k.sp, cosT[:, 0:T], L["rope"][:, 0, :], wr=[rb]); k.dma(k.sp, sinT[:, 0:T], L["rope"][:, 1, :], wr=[rb])
        k.dma(k.sp, cosT[:, T:TA], L["ropes"][:, 0, :], wr=[rb]); k.dma(k.sp, sinT[:, T:TA], L["ropes"][:, 1, :], wr=[rb])
        dec = sb("dec", [128, 4, 128], F32, ph); decb = Buf()
        qT = sb("qT", [128, 2, TA], BF16, ph); kT = sb("kT", [128, 2, TA], BF16, ph); qb = Buf(); kb = Buf()
        vtm = sb("vtm", [128, 17, 256], BF16, ph); sgt = sb("sgt", [128, 17, 256], BF16, ph); vb = Buf(); gb = Buf()
        t1 = sb("rt1", [128, 512], F32, ph); t2 = sb("rt2", [128, 512], F32, ph); t1b = Buf(); t2b = Buf()
        S = sb("Sret", [128, 2, 256], F32, ph); Sb = sb("Sretb", [128, 2, 256], BF16, ph); Sbuf = Buf()
        Ss = [sb("Ss%d" % i, [128, 2, 256], F32, ph) for i in range(4)]; Ssb = [Buf() for _ in range(4)]
        Qm = sb("Qm", [128, 2, NS, 64], F32, ph); Qmb = Buf()
        Vblk = sb("Vblk", [64, NS, 256], BF16, ph); Vbb = Buf()
        scm = sb("scm", [128, 128], BF16, ph); scb = Buf()
        ktm = sb("ktm", [128, 256], BF16, ph); ktb = Buf()
        st6 = sb("st6", [128, 6], F32, ph); mv = sb("mv", [128, 2], F32, ph); rs = sb("rsg", [128, 1], F32, ph); stb = Buf()
        yf = sb("yf", [128, 256], F32, ph); ybf = sb("ybf", [128, 256], BF16, ph); yfb = Buf(); ybb = Buf()
        yTt = [sb("yTt%d" % i, [128, 2, 128], BF16, ph) for i in range(2)]; yTb = [Buf(), Buf()]
        groups = [(tg * 512, 512, False) for tg in range(4)] + [(0, TSM, True)]
        tiles = [(i * 128, 128, False, i) for i in range(16)] + [(T, 64, True, 16)]
        nsd = 0
        for hd in range(8):
            lg = math.log1p(-2.0 ** (-5.0 - hd))
            gC = math.exp(128 * lg); g4 = math.exp(4 * lg)
            k.dma(k.sp, dec[:], L["rdec"][:, hd], wr=[decb])
            for which, dst, dstb in ((0, qT, qb), (1, kT, kb)):
                wt, wtb = ws.load(which * D + hd * 256, 256)
                for (t0, n, smp) in groups:
                    p1, p1b = ws.fm(wt, wtb, 0, 128, t0, n, smp)
                    p2, p2b = ws.fm(wt, wtb, 128, 128, t0, n, smp)
                    o0 = T if smp else t0
                    cs = cosT[:, o0:o0 + n]; sn = sinT[:, o0:o0 + n]
                    if smp:
                        dv = dec[:, 2 + which, 0:64]; shp = None
                    else:
                        dv = dec[:, which, :].unsqueeze(1).to_broadcast([128, 4, 128])
                    for half in range(2):
                        a, ab, b_, bb = (p1, p1b, p2, p2b)
                        ca, cb = (cs, sn) if half == 0 else (sn, cs)
                        op2 = ALU.subtract if half == 0 else ALU.add
                        k.op(k.dve, lambda e: e.tensor_tensor(out=t1[:, 0:n], in0=a[:, 0:n], in1=ca, op=ALU.mult), rd=[ab, rb], wr=[t1b])
                        k.op(k.dve, lambda e: e.tensor_tensor(out=t2[:, 0:n], in0=b_[:, 0:n], in1=cb, op=ALU.mult), rd=[bb, rb], wr=[t2b])
                        k.op(k.dve, lambda e: e.tensor_tensor(out=t1[:, 0:n], in0=t1[:, 0:n], in1=t2[:, 0:n], op=op2), rd=[t1b, t2b], wr=[t1b])
                        if smp:
                            k.op(k.dve, lambda e: e.tensor_tensor(out=dst[:, half, T:T + 64], in0=t1[:, 0:64], in1=dv, op=ALU.mult),
                                 rd=[t1b, decb], wr=[dstb])
                        else:
                            k.op(k.dve, lambda e: e.tensor_tensor(out=dst[:, half, t0:t0 + 512].rearrange("p (a b) -> p a b", b=128),
                                                                  in0=t1[:, 0:512].rearrange("p (a b) -> p a b", b=128), in1=dv, op=ALU.mult),
                                 rd=[t1b, decb], wr=[dstb])
            assert ws.n % 2 == 0
            ws.load(2 * D + hd * 256, 256)
            ws.load(3 * D + hd * 256, 256)
            for (t0, n, smp, ti) in tiles:
                ps, pb = ws.tm2(0 if smp else t0, n, smp)
                k.op(k.act, lambda e: e.activation(out=vtm[0:n, ti, :], in_=ps[0:n, 0:256], func=AF.Copy), rd=[pb], wr=[vb])
                k.op(k.act, lambda e: e.activation(out=sgt[0:n, ti, :], in_=ps[0:n, 256:512], func=AF.Silu), rd=[pb], wr=[gb])
            k.op(k.dve, lambda e: e.memset(S[:], 0.0), wr=[Sbuf])
            k.op(k.dve, lambda e: e.memset(Sb[:], 0.0), wr=[Sbuf])
            for h in range(2):
                k.op(k.dve, lambda e: e.tensor_tensor(out=Qm[:, h], in0=qT[:, h, T:TA].unsqueeze(1).to_broadcast([128, NS, 64]),
                                                      in1=smkT[:], op=ALU.mult), rd=[qb, B["const"]], wr=[Qmb])
            k.op(k.dve, lambda e: e.tensor_tensor(out=Vblk[:], in0=vtm[0:64, 16, :].unsqueeze(1).to_broadcast([64, NS, 256]),
                                                  in1=smk[:].unsqueeze(2).to_broadcast([64, NS, 256]), op=ALU.mult),
                 rd=[vb, B["const"]], wr=[Vbb])
            for (t0, n, smp, ti) in tiles:
                mi = 3 if smp else 0
                ps, pb = k.psum()
                for h in range(2):
                    k.op(k.pe, lambda e: e.matmul(ps[0:n, 0:n], lhsT=kT[:, h, t0:t0 + n], rhs=qT[:, h, t0:t0 + n], start=(h == 0), stop=(h == 1)),
                         rd=[kb, qb], wr=[pb])
                k.op(k.dve, lambda e: e.tensor_tensor(out=scm[0:n, 0:n], in0=ps[0:n, 0:n], in1=cm[0:n, mi, 0:n], op=ALU.mult),
                     rd=[pb, B["const"]], wr=[scb])
                po, pob = k.psum()
                k.op(k.pe, lambda e: e.matmul(po[0:n, 0:256], lhsT=scm[0:n, 0:n], rhs=vtm[0:n, ti, :], start=True, stop=False), rd=[scb, vb], wr=[pob])
                if not smp:
                    for h in range(2):
                        k.op(k.pe, lambda e: e.matmul(po[0:n, 0:256], lhsT=qT[:, h, t0:t0 + n], rhs=Sb[:, h, :], start=False, stop=(h == 1)),
                             rd=[qb, Sbuf], wr=[pob])
                else:
                    for b in range(NS):
                        j = nsd % 4
                        nsd += 1
                        k.dma(k.sp, Ss[j][:], L["sret"][b, hd].rearrange("(h p) e -> p h e", p=128), wr=[Ssb[j]])
                        for h in range(2):
                            k.op(k.pe, lambda e: e.matmul(po[0:n, 0:256], lhsT=Qm[:, h, b, :], rhs=Ss[j][:, h, :], start=False,
                                                          stop=(b == NS - 1 and h == 1)), rd=[Qmb, Ssb[j]], wr=[pob])
                        pu, pub = k.psum(exclude=[pob])
                        if b == 0:
                            pt, ptb = k.psum(exclude=[pob, pub])
                            ptv = pt[:].bitcast(BF16)
                            for h in range(2):
                                k.op(k.pe, lambda e: e.transpose(ptv[0:64, h * 128:(h + 1) * 128], kT[:, h, T:TA], identb[:]), rd=[kb, B["const"]], wr=[ptb])
                            k.op(k.dve, lambda e: e.tensor_copy(out=ktm[0:64, :], in_=ptv[0:64, 0:256]), rd=[ptb], wr=[ktb])
                        for h in range(2):
                            k.op(k.pe, lambda e: e.matmul(pu[:, h * 256:(h + 1) * 256], lhsT=ktm[0:64, h * 128:(h + 1) * 128], rhs=Vblk[:, b, :],
                                                          start=True, stop=True), rd=[ktb, Vbb], wr=[pub])
                        sv = Ss[j][:].rearrange("p h e -> p (h e)")
                        k.op(k.dve, lambda e: e.tensor_scalar(out=sv, in0=sv, scalar1=g4, scalar2=None, op0=ALU.mult), rd=[Ssb[j]], wr=[Ssb[j]])
                        k.op(k.dve, lambda e: e.scalar_tensor_tensor(out=sv, in0=pu[:, 0:512], scalar=g4, in1=sv, op0=ALU.mult, op1=ALU.add),
                             rd=[pub, Ssb[j]], wr=[Ssb[j]])
                        k.dma(k.pool, L["nret_s"][b, hd].rearrange("(h p) e -> p h e", p=128), Ss[j][:], rd=[Ssb[j]])
                k.op(k.dve, lambda e: e.bn_stats(out=st6[0:n, :], in_=po[0:n, 0:256]), rd=[pob], wr=[stb])
                k.op(k.dve, lambda e: e.bn_aggr(out=mv[0:n, :], in_=st6[0:n, :]), rd=[stb], wr=[stb])
                k.op(k.dve, lambda e: e.tensor_scalar(out=rs[0:n, :], in0=mv[0:n, 1:2], scalar1=1e-5, scalar2=None, op0=ALU.add), rd=[stb], wr=[stb])
                k.op(k.act, lambda e: e.activation(out=rs[0:n, :], in_=rs[0:n, :], func=AF.Sqrt), rd=[stb], wr=[stb])
                k.op(k.dve, lambda e: e.reciprocal(out=rs[0:n, :], in_=rs[0:n, :]), rd=[stb], wr=[stb])
                k.op(k.dve, lambda e: e.tensor_scalar(out=yf[0:n, :], in0=po[0:n, 0:256], scalar1=mv[0:n, 0:1], scalar2=rs[0:n, 0:1],
                                                      op0=ALU.subtract, op1=ALU.mult), rd=[pob, stb], wr=[yfb])
                k.op(k.dve, lambda e: e.tensor_tensor(out=ybf[0:n, :], in0=yf[0:n, :], in1=sgt[0:n, ti, :], op=ALU.mult), rd=[yfb, gb], wr=[ybb])
                pt, ptb = k.psum()
                ptv = pt[:].bitcast(BF16)
                for h in range(2):
                    k.op(k.pe, lambda e: e.transpose(ptv[:, h * 128:h * 128 + n], ybf[0:n, h * 128:(h + 1) * 128], identb[0:n, 0:n]),
                         rd=[ybb, B["const"]], wr=[ptb])
                yi = ti % 2
                k.op(k.act, lambda e: e.activation(out=yTt[yi][:, :, 0:n], in_=ptv[:, 0:256].rearrange("p (h t) -> p h t", h=2)[:, :, 0:n], func=AF.Copy),
                     rd=[ptb], wr=[yTb[yi]])
                k.dma(k.pool, L["ymix_d"][2 * hd:2 * hd + 2, :, t0:t0 + n].rearrange("h p t -> p h t"), yTt[yi][:, :, 0:n], rd=[yTb[yi]])
                if not smp:
                    pt2, pt2b = k.psum()
                    ptv2 = pt2[:].bitcast(BF16)
                    for h in range(2):
                        k.op(k.pe, lambda e: e.transpose(ptv2[:, h * 128:(h + 1) * 128], kT[:, h, t0:t0 + 128], identb[:]), rd=[kb, B["const"]], wr=[pt2b])
                    k.op(k.act, lambda e: e.activation(out=ktm[:, :], in_=ptv2[:, 0:256], func=AF.Copy), rd=[pt2b], wr=[ktb])
                    pu, pub = k.psum()
                    for h in range(2):
                        k.op(k.pe, lambda e: e.matmul(pu[:, h * 256:(h + 1) * 256], lhsT=ktm[:, h * 128:(h + 1) * 128], rhs=vtm[:, ti, :], start=True, stop=True),
                             rd=[ktb, vb], wr=[pub])
                    sv = S[:].rearrange("p h e -> p (h e)")
                    k.op(k.dve, lambda e: e.tensor_scalar(out=sv, in0=sv, scalar1=gC, scalar2=None, op0=ALU.mult), rd=[Sbuf], wr=[Sbuf])
                    k.op(k.dve, lambda e: e.scalar_tensor_tensor(out=sv, in0=pu[:, 0:512], scalar=gC, in1=sv, op0=ALU.mult, op1=ALU.add),
                         rd=[pub, Sbuf], wr=[Sbuf])
                    k.op(k.act, lambda e: e.activation(out=Sb[:], in_=S[:], func=AF.Copy), rd=[Sbuf], wr=[Sbuf])
                    if ti == 15:
                        k.dma(k.pool, L["nret_p"][hd].rearrange("(h p) e -> p h e", p=128), S[:], rd=[Sbuf])
        k.barrier()


def phase_out(nc, k, sb, L):
    B, modT, identf = L["B"], L["modT"], L["identf"]
    with ExitStack() as ph:
        grow = sb("grow", [17, D], F32, ph); grb = Buf()
        selp = sb("selp_s", [17, 128], F32, ph); sels = sb("sels_s", [17, 64], F32, ph); fnw = sb("fnw", [128, D], F32, ph); cb = Buf()
        k.dma(k.sp, selp[:], L["selp"], wr=[cb]); k.dma(k.sp, sels[:], L["sels"], wr=[cb]); k.dma(k.sp, fnw[:], L["fnw_b"], wr=[cb])
        for c in range(KC):
            ps, pb = k.psum()
            k.op(k.pe, lambda e: e.transpose(ps[0:17, 0:128], modT[:, 32 + c, :], identf[:]), rd=[B["modT"], B["const"]], wr=[pb])
            k.op(k.dve, lambda e: e.tensor_copy(out=grow[:, c * 128:(c + 1) * 128], in_=ps[0:17, 0:128]), rd=[pb], wr=[grb])
        wo = sb("wo", [128, 32, D], BF16, ph); wob = [Buf() for _ in range(32)]
        wst = [sb("wost%d" % i, [128, 4, 256], F32, ph) for i in range(2)]; wstb = [Buf(), Buf()]
        nld = 0
        for fc4 in range(8):
            for cg in range(8):
                i = nld % 2
                nld += 1
                k.dma(k.sp, wst[i][:], L["w_out"][fc4 * 512:(fc4 + 1) * 512, cg * 256:(cg + 1) * 256].rearrange("(c p) n -> p c n", p=128), wr=[wstb[i]])
                dstw = wo[:, fc4 * 4:(fc4 + 1) * 4, cg * 256:(cg + 1) * 256]
                wbufs = [wob[fc4 * 4 + q] for q in range(4)]
                if nld % 2:
                    k.op(k.act, lambda e: e.activation(out=dstw, in_=wst[i][:], func=AF.Copy), rd=[wstb[i]], wr=wbufs)
                else:
                    k.op(k.dve, lambda e: e.tensor_copy(out=dstw, in_=wst[i][:]), rd=[wstb[i]], wr=wbufs)
        ym = sb("ym", [128, 32, 128], BF16, ph); ymb = Buf()
        xr = [sb("xr%d" % i, [128, D], F32, ph) for i in range(2)]; xrb = [Buf(), Buf()]
        G = [sb("Gt%d" % i, [128, 512], F32, ph) for i in range(2)]; Gb = [Buf(), Buf()]
        junk = sb("junk", [128, D], BF16, ph); jb = Buf()
        ssq = sb("ssq", [128, 1], F32, ph); sqb = Buf()
        tiles = [(i * 128, 128, False) for i in range(16)] + [(T, 64, True)]
        ng = 0
        for ti, (t0, n, smp) in enumerate(tiles):
            i = ti % 2
            k.dma(k.sp, ym[:, :, 0:n], L["ymix_d"][:, :, t0:t0 + n].rearrange("c p t -> p c t"), wr=[ymb])
            k.dma(k.sp, xr[i][0:n, :], L["xs"][0:n, :] if smp else L["x"][t0:t0 + n, :], wr=[xrb[i]])
            sel = sels if smp else selp
            for cg in range(4):
                c0 = cg * 512
                gi = ng % 2
                ng += 1
                pg, pgb = k.psum()
                k.op(k.pe, lambda e: e.matmul(pg[0:n, 0:512], lhsT=sel[:, 0:n], rhs=grow[:, c0:c0 + 512], start=True, stop=True), rd=[cb, grb], wr=[pgb])
                k.op(k.act, lambda e: e.activation(out=G[gi][0:n, :], in_=pg[0:n, 0:512], func=AF.Copy), rd=[pgb], wr=[Gb[gi]])
                ps, pb = k.psum()
                for fc in range(32):
                    k.op(k.pe, lambda e: e.matmul(ps[0:n, 0:512], lhsT=ym[:, fc, 0:n], rhs=wo[:, fc, c0:c0 + 512], start=(fc == 0), stop=(fc == 31)),
                         rd=[ymb, wob[fc]], wr=[pb])
                k.op(k.dve, lambda e: e.tensor_tensor(out=G[gi][0:n, :], in0=ps[0:n, 0:512], in1=G[gi][0:n, :], op=ALU.mult), rd=[pb, Gb[gi]], wr=[Gb[gi]])
                k.op(k.dve, lambda e: e.tensor_tensor(out=xr[i][0:n, c0:c0 + 512], in0=G[gi][0:n, :], in1=xr[i][0:n, c0:c0 + 512], op=ALU.add), rd=[Gb[gi], xrb[i]], wr=[xrb[i]])
            k.op(k.act, lambda e: e.activation(out=junk[0:n, :], in_=xr[i][0:n, :], func=AF.Square, accum_out=ssq[0:n, :]), rd=[xrb[i]], wr=[jb, sqb])
            k.op(k.dve, lambda e: e.tensor_scalar(out=ssq[0:n, :], in0=ssq[0:n, :], scalar1=1.0 / D, scalar2=1e-6, op0=ALU.mult, op1=ALU.add), rd=[sqb], wr=[sqb])
            k.op(k.act, lambda e: e.activation(out=ssq[0:n, :], in_=ssq[0:n, :], func=AF.Sqrt), rd=[sqb], wr=[sqb])
            k.op(k.dve, lambda e: e.reciprocal(out=ssq[0:n, :], in_=ssq[0:n, :]), rd=[sqb], wr=[sqb])
            k.op(k.dve, lambda e: e.scalar_tensor_tensor(out=xr[i][0:n, :], in0=xr[i][0:n, :], scalar=ssq[0:n, 0:1], in1=fnw[0:n, :], op0=ALU.mult, op1=ALU.mult),
                 rd=[xrb[i], sqb, cb], wr=[xrb[i]])
            dst = L["ys"][0:n, :] if smp else L["y"][t0:t0 + n, :]
            k.dma(k.pool, dst, xr[i][0:n, :], rd=[xrb[i]])
        k.barrier()


def phase_rwkv(nc, k, sb, L):
    hT, hTs, B = L["hT"], L["hTs"], L["B"]
    cm, smk, smkT, identb, onesf, rw, mul = L["cm"], L["smk"], L["smkT"], L["identb"], L["onesf"], L["rw"], L["mul"]
    TA = T + TSM
    EM = math.exp(-0.5)
    with ExitStack() as ph:
        ws = WStream(nc, k, sb, ph, L["w_in"], hT, hTs, B["hT"], B["hTs"], alloc=False)
        cB = B["const"]
        rw2 = sb("rw2", [128, 10, 16], F32, ph); bones = sb("bones", [128, 128], F32, ph)
        k.dma(k.sp, rw2[:], L["rwp2"], wr=[cB]); k.dma(k.sp, bones[:], L["bones"], wr=[cB])
        omm = sb("omm", [128, 3, 16], F32, ph); omml = sb("omml", [96, 2], F32, ph)
        k.op(k.dve, lambda e: e.tensor_scalar(out=omm[:], in0=rw2[:, 0:3, :], scalar1=-1.0, scalar2=1.0, op0=ALU.mult, op1=ALU.add), rd=[cB], wr=[cB])
        k.op(k.dve, lambda e: e.tensor_scalar(out=omml[:], in0=mul[:], scalar1=-1.0, scalar2=1.0, op0=ALU.mult, op1=ALU.add), rd=[cB], wr=[cB])
        w2b = sb("w2b", [96, D], BF16, ph); a2b = sb("a2b", [96, D], BF16, ph)
        with ExitStack() as p2:
            st = sb("lst", [96, D], F32, p2); stb = Buf()
            for src, dst in ((L["w2d"], w2b), (L["a2"], a2b)):
                k.dma(k.sp, st[:], src, wr=[stb])
                k.op(k.dve, lambda e: e.tensor_copy(out=dst[:], in_=st[:]), rd=[stb], wr=[cB])
            k.barrier()
        tanhwd = sb("tanhwd", [96, TA], BF16, ph); adm = sb("adm", [96, TA], BF16, ph); lb = Buf()
        with ExitStack() as p2:
            cur = sb("lcur", [96, 2, 1 + T], F32, p2); curs = sb("lcurs", [96, 2, 80], F32, p2); prv = sb("lprv", [96, 64], F32, p2)
            tmpl = sb("ltmp", [96, T], F32, p2); cb_ = Buf(); tb_ = Buf()
            lst_ = sb("lwst", [128, KC, 192], F32, p2); wt = sb("lwb", [128, KC, 192], BF16, p2); wtb = Buf()
            k.dma(k.sp, lst_[:], L["w_in"][:, 16384:16576].rearrange("(c p) n -> p c n", p=128), wr=[wtb])
            k.op(k.pool, lambda e: e.tensor_copy(out=wt[:], in_=lst_[:]), rd=[wtb], wr=[wtb])
            k.op(k.dve, lambda e: e.memset(cur[:, :, 0:1], 0.0), wr=[cb_])
            for which in range(2):
                for tg in range(4):
                    ps, pb = ws.fm(wt, wtb, which * 96, 96, tg * 512, 512)
                    k.op(k.act, lambda e: e.activation(out=cur[:, which, 1 + tg * 512:1 + (tg + 1) * 512], in_=ps[0:96, 0:512], func=AF.Copy), rd=[pb], wr=[cb_])
                ps, pb = ws.fm(wt, wtb, which * 96, 96, 0, 80, True)
                k.op(k.act, lambda e: e.activation(out=curs[:, which, :], in_=ps[0:96, 0:80], func=AF.Copy), rd=[pb], wr=[cb_])
                dst = tanhwd if which == 0 else adm
                fn = AF.Tanh if which == 0 else AF.Copy
                k.op(k.dve, lambda e: e.tensor_scalar(out=tmpl[:, 0:T], in0=cur[:, which, 0:T], scalar1=mul[:, which:which + 1], scalar2=None, op0=ALU.mult), rd=[cb_, cB], wr=[tb_])
                k.op(k.dve, lambda e: e.scalar_tensor_tensor(out=tmpl[:, 0:T], in0=cur[:, which, 1:1 + T], scalar=omml[:, which:which + 1], in1=tmpl[:, 0:T], op0=ALU.mult, op1=ALU.add),
                     rd=[cb_, cB, tb_], wr=[tb_])
                k.op(k.act, lambda e: e.activation(out=dst[:, 0:T], in_=tmpl[:, 0:T], func=fn), rd=[tb_], wr=[lb])
                c3 = curs[:, which, 0:64].rearrange("p (s j) -> p s j", j=4); p3 = prv[:].rearrange("p (s j) -> p s j", j=4)
                k.op(k.dve, lambda e: e.tensor_copy(out=p3[:, :, 1:4], in_=c3[:, :, 0:3]), rd=[cb_], wr=[tb_])
                k.op(k.dve, lambda e: e.tensor_copy(out=p3[:, :, 0], in_=curs[:, which, 64:80]), rd=[cb_], wr=[tb_])
                k.op(k.dve, lambda e: e.tensor_scalar(out=prv[:], in0=prv[:], scalar1=mul[:, which:which + 1], scalar2=None, op0=ALU.mult), rd=[tb_, cB], wr=[tb_])
                k.op(k.dve, lambda e: e.scalar_tensor_tensor(out=prv[:], in0=curs[:, which, 0:64], scalar=omml[:, which:which + 1], in1=prv[:], op0=ALU.mult, op1=ALU.add),
                     rd=[cb_, cB, tb_], wr=[tb_])
                k.op(k.act, lambda e: e.activation(out=dst[:, T:TA], in_=prv[:], func=fn), rd=[tb_], wr=[lb])
            k.barrier()
        NB = 256
        F = {n: sb("rf_" + n, [128, 1 + NB], F32, ph) for n in ["cr", "ck", "cv", "rm", "km", "vm", "t7", "W", "Wp"]}
        Z = {n: Buf() for n in ["cr", "ck", "cv", "rm", "km", "vm", "t7", "W", "Wp", "al", "be", "kt", "vb", "mp", "pwA", "pwB", "car", "wb",
                                "Ubp", "oT", "t7c", "y", "S", "S0s", "msk"]}
        for a_, b_ in {"w": "cr", "a": "ck", "kk": "cv"}.items():
            F[a_] = F[b_]; Z[a_] = Z[b_]
        HS = []
        for par2 in range(2):
            hb_ = {n: sb("rb_" + n, [128, NB], BF16, ph) for n in ["al", "be", "kt", "vb"]}
            hbp_ = {n: [sb("rbp_%s%d" % (n, e_), [128, NB], BF16, ph) for e_ in range(2)] for n in ["al", "be", "kt"]}
            hz_ = {n: Buf() for n in ["al", "be", "kt", "vb"]}
            for n_ in ["al", "be", "kt"]:
                for e_ in range(2):
                    hz_[n_ + "P%d" % e_] = Buf()
                    k.op(k.pool, lambda e: e.memset(hbp_[n_][e_][:], 0.0), wr=[hz_[n_ + "P%d" % e_]])
            HS.append(dict(Hb=hb_, HbP=hbp_, Z=hz_))
        mp = sb("mats_pre", [128, 3, 4, 128], BF16, ph)
        pw = [sb("pw%d" % i, [128, 2, 4, 128], BF16, ph) for i in range(2)]
        car = sb("car", [128, 3], F32, ph)
        wb = sb("wbp", [128, KC, 512], BF16, ph)
        stg4 = [sb("stg4_%d" % i, [128, KC, 64], F32, ph) for i in range(2)]; stg4b = [Buf(), Buf()]
        CS = []
        for par in range(3):
            d = dict(P=sb("Pm", [128, 4, 128], BF16, ph), m34=sb("m34", [128, 2, 4, 128], BF16, ph), tp=sb("tmpad", [128, 2, 4, 192], BF16, ph),
                     ah=sb("ah", [128, 2, 128], BF16, ph), Yb=sb("Yb", [128, 2, 2, 64], BF16, ph), wc=sb("wc", [128, 16], F32, ph),
                     bv=sb("bv", [128, NB], F32, ph), sg=sb("sg", [128, NB], BF16, ph), rt=sb("rt", [128, NB], BF16, ph))
            d["Z"] = {n: Buf() for n in ["P", "m34", "tp", "ah", "Yb", "wc", "bv", "sg", "rt"]}
            k.op(k.dve, lambda e: e.memset(d["tp"][:], 0.0), wr=[d["Z"]["tp"]])
            CS.append(d)
        Ubp = sb("Ubp", [128, 192], BF16, ph)
        k.op(k.dve, lambda e: e.memset(Ubp[:], 0.0), wr=[Z["Ubp"]])
        oT = sb("oT", [128, NB], F32, ph); t7c = sb("t7c", [128, NB], F32, ph); yb = sb("ybf", [128, NB], BF16, ph)
        S = sb("Srw", [128, 64], F32, ph); Sbd = sb("Sbd", [128, 128], BF16, ph)
        S0s = sb("S0s", [128, NS, 64], F32, ph); Sbds = sb("Sbds", [128, NS, 128], BF16, ph)
        ahm = sb("ahm", [128, NS, 64], BF16, ph); rtm = sb("rtm", [128, NS, 64], BF16, ph)
        Ublk = sb("Ublk", [64, 2, NS, 64], BF16, ph); Vblk = sb("Vblk2", [64, 2, NS, 64], BF16, ph)
        k.op(k.dve, lambda e: e.memset(Sbds[:], 0.0), wr=[Z["S0s"]])
        nstg = [0]

        def load_w(pi):
            for q in range(8):
                si = nstg[0] % 2
                nstg[0] += 1
                c0 = 8192 + (q // 2) * D + pi * 128 + (q % 2) * 64
                k.dma(k.sp, stg4[si][:], L["w_in"][:, c0:c0 + 64].rearrange("(c p) n -> p c n", p=128), wr=[stg4b[si]])
                k.op(k.pool, lambda e: e.tensor_copy(out=wb[:, :, q * 64:(q + 1) * 64], in_=stg4[si][:]), rd=[stg4b[si]], wr=[Z["wb"]])

        def prepA(pi, t0, n, smp, par, par2):
            C = CS[par]; CZ = C["Z"]; H = HS[par2]; Hb = H["Hb"]; HbP = H["HbP"]; HZ = H["Z"]
            TT = 64 if smp else 128
            ntile = n // TT
            NM = 2 * ntile
            pc = lambda j: rw2[:, j, pi:pi + 1]
            lc = T if smp else t0
            if t0 == 0 and not smp:
                load_w(pi)
                k.op(k.dve, lambda e: e.memset(car[:], 0.0), wr=[Z["car"]])
            for qi, cn in enumerate(("cr", "ck", "cv")):
                mn = ("rm", "km", "vm")[qi]
                if smp:
                    ps, pb = ws.fm(wb, Z["wb"], qi * 128, 128, 0, 80, True)
                    k.op(k.act, lambda e: e.activation(out=F[cn][:, 1:81], in_=ps[:, 0:80], func=AF.Copy), rd=[pb], wr=[Z[cn]])
                    c3 = F[cn][:, 1:65].rearrange("p (s j) -> p s j", j=4); p3 = F["t7"][:, 0:64].rearrange("p (s j) -> p s j", j=4)
                    k.op(k.dve, lambda e: e.tensor_copy(out=p3[:, :, 1:4], in_=c3[:, :, 0:3]), rd=[Z[cn]], wr=[Z["t7"]])
                    k.op(k.dve, lambda e: e.tensor_copy(out=p3[:, :, 0], in_=F[cn][:, 65:81]), rd=[Z[cn]], wr=[Z["t7"]])
                    k.op(k.dve, lambda e: e.tensor_scalar(out=F["t7"][:, 0:n], in0=F["t7"][:, 0:n], scalar1=pc(qi), scalar2=None, op0=ALU.mult), rd=[Z["t7"], cB], wr=[Z["t7"]])
                else:
                    k.op(k.dve, lambda e: e.tensor_copy(out=F[cn][:, 0:1], in_=car[:, qi:qi + 1]), rd=[Z["car"]], wr=[Z[cn]])
                    ps, pb = yield from ws.fm_gen(wb, Z["wb"], qi * 128, 128, t0, n)
                    k.op(k.act, lambda e: e.activation(out=F[cn][:, 1:1 + n], in_=ps[:, 0:n], func=AF.Copy), rd=[pb], wr=[Z[cn]])
                    k.op(k.dve, lambda e: e.tensor_copy(out=car[:, qi:qi + 1], in_=F[cn][:, n:n + 1]), rd=[Z[cn]], wr=[Z["car"]])
                    k.op(k.dve, lambda e: e.tensor_scalar(out=F["t7"][:, 0:n], in0=F[cn][:, 0:n], scalar1=pc(qi), scalar2=None, op0=ALU.mult), rd=[Z[cn], cB], wr=[Z["t7"]])
                k.op(k.dve, lambda e: e.scalar_tensor_tensor(out=F[mn][:, 0:n], in0=F[cn][:, 1:1 + n], scalar=omm[:, qi, pi:pi + 1], in1=F["t7"][:, 0:n], op0=ALU.mult, op1=ALU.add),
                     rd=[Z[cn], Z["t7"], cB], wr=[Z[mn]])
                yield
            ps, pb = yield from ws.fm_gen(wb, Z["wb"], 384, 128, 0 if smp else t0, n, smp)
            k.op(k.act, lambda e: e.activation(out=C["sg"][:, 0:n], in_=ps[:, 0:n], func=AF.Silu), rd=[pb], wr=[CZ["sg"]])
            ps, pb = k.psum()
            k.op(k.pe, lambda e: e.matmul(ps[:, 0:n], lhsT=w2b[:, pi * 128:(pi + 1) * 128], rhs=tanhwd[:, lc:lc + n], start=True, stop=True), rd=[cB, lb], wr=[pb])
            k.op(k.act, lambda e: e.activation(out=F["w"][:, 0:n], in_=ps[:, 0:n], func=AF.Sigmoid, bias=pc(3)), rd=[pb, cB], wr=[Z["w"]])
            ps, pb = k.psum()
            k.op(k.pe, lambda e: e.matmul(ps[:, 0:n], lhsT=a2b[:, pi * 128:(pi + 1) * 128], rhs=adm[:, lc:lc + n], start=True, stop=True), rd=[cB, lb], wr=[pb])
            k.op(k.act, lambda e: e.activation(out=F["a"][:, 0:n], in_=ps[:, 0:n], func=AF.Sigmoid, bias=pc(4)), rd=[pb, cB], wr=[Z["a"]])
            k.op(k.act, lambda e: e.activation(out=F["w"][:, 0:n], in_=F["w"][:, 0:n], func=AF.Exp, scale=-EM), rd=[Z["w"]], wr=[Z["w"]])
            yield
            k.op(k.dve, lambda e: e.tensor_scalar(out=F["kk"][:, 0:n], in0=F["km"][:, 0:n], scalar1=pc(5), scalar2=None, op0=ALU.mult), rd=[Z["km"], cB], wr=[Z["kk"]])
            k.op(k.dve, lambda e: e.tensor_tensor(out=F["t7"][:, 0:n], in0=F["kk"][:, 0:n], in1=F["kk"][:, 0:n], op=ALU.mult), rd=[Z["kk"]], wr=[Z["t7"]])
            ps, pb = k.psum()
            k.op(k.pe, lambda e: e.matmul(ps[:, 0:n], lhsT=bones[:, :], rhs=F["t7"][:, 0:n], start=True, stop=True), rd=[cB, Z["t7"]], wr=[pb])
            k.op(k.act, lambda e: e.activation(out=F["Wp"][:, 0:n], in_=ps[:, 0:n], func=AF.Sqrt), rd=[pb], wr=[Z["Wp"]])
            k.op(k.dve, lambda e: e.tensor_scalar(out=F["Wp"][:, 0:n], in0=F["Wp"][:, 0:n], scalar1=1e-12, scalar2=None, op0=ALU.max), rd=[Z["Wp"]], wr=[Z["Wp"]])
            k.op(k.dve, lambda e: e.reciprocal(out=F["Wp"][:, 0:n], in_=F["Wp"][:, 0:n]), rd=[Z["Wp"]], wr=[Z["Wp"]])
            k.op(k.dve, lambda e: e.tensor_tensor(out=F["kk"][:, 0:n], in0=F["kk"][:, 0:n], in1=F["Wp"][:, 0:n], op=ALU.mult), rd=[Z["kk"], Z["Wp"]], wr=[Z["kk"]])
            yield
            k.op(k.dve, lambda e: e.tensor_scalar(out=F["t7"][:, 0:n], in0=F["a"][:, 0:n], scalar1=pc(6), scalar2=pc(6), op0=ALU.mult, op1=ALU.subtract), rd=[Z["a"], cB], wr=[Z["t7"]])
            k.op(k.dve, lambda e: e.scalar_tensor_tensor(out=F["km"][:, 0:n], in0=F["t7"][:, 0:n], scalar=1.0, in1=F["km"][:, 0:n], op0=ALU.add, op1=ALU.mult),
                 rd=[Z["t7"], Z["km"]], wr=[Z["km"]])
            CL = 4 if smp else 128
            for c0 in range(0, n, CL):
                k.op(k.dve, lambda e: e.tensor_tensor_scan(out=F["W"][:, c0:c0 + CL], data0=F["w"][:, c0:c0 + CL], data1=F["w"][:, c0:c0 + CL], initial=1.0, op0=ALU.mult, op1=ALU.bypass),
                     rd=[Z["w"]], wr=[Z["W"]])
            W3 = F["W"][:, 0:n].rearrange("p (c j) -> p c j", j=CL); Wp3 = F["Wp"][:, 0:n].rearrange("p (c j) -> p c j", j=CL)
            k.op(k.dve, lambda e: e.tensor_copy(out=Wp3[:, :, 1:CL], in_=W3[:, :, 0:CL - 1]), rd=[Z["W"]], wr=[Z["Wp"]])
            k.op(k.dve, lambda e: e.memset(Wp3[:, :, 0:1], 1.0), wr=[Z["Wp"]])
            k.op(k.dve, lambda e: e.tensor_copy(out=C["wc"][:, 0:n // CL], in_=W3[:, :, CL - 1]), rd=[Z["W"]], wr=[CZ["wc"]])
            yield
            k.op(k.dve, lambda e: e.scalar_tensor_tensor(out=Hb["al"][:, 0:n], in0=F["kk"][:, 0:n], scalar=-1.0, in1=F["Wp"][:, 0:n], op0=ALU.mult, op1=ALU.mult), rd=[Z["kk"], Z["Wp"]], wr=[HZ["al"]])
            k.op(k.dve, lambda e: e.tensor_tensor(out=C["rt"][:, 0:n], in0=F["rm"][:, 0:n], in1=F["W"][:, 0:n], op=ALU.mult), rd=[Z["rm"], Z["W"]], wr=[CZ["rt"]])
            k.op(k.dve, lambda e: e.tensor_tensor(out=F["t7"][:, 0:n], in0=F["kk"][:, 0:n], in1=F["a"][:, 0:n], op=ALU.mult), rd=[Z["kk"], Z["a"]], wr=[Z["t7"]])
            k.op(k.dve, lambda e: e.reciprocal(out=F["W"][:, 0:n], in_=F["W"][:, 0:n]), rd=[Z["W"]], wr=[Z["W"]])
            k.op(k.dve, lambda e: e.tensor_tensor(out=Hb["be"][:, 0:n], in0=F["t7"][:, 0:n], in1=F["W"][:, 0:n], op=ALU.mult), rd=[Z["t7"], Z["W"]], wr=[HZ["be"]])
            k.op(k.dve, lambda e: e.tensor_tensor(out=Hb["kt"][:, 0:n], in0=F["km"][:, 0:n], in1=F["W"][:, 0:n], op=ALU.mult), rd=[Z["km"], Z["W"]], wr=[HZ["kt"]])
            k.op(k.act, lambda e: e.activation(out=Hb["vb"][:, 0:n], in_=F["vm"][:, 0:n], func=AF.Copy), rd=[Z["vm"]], wr=[HZ["vb"]])
            yield
            k.op(k.dve, lambda e: e.scalar_tensor_tensor(out=F["t7"][:, 0:n], in0=F["rm"][:, 0:n], scalar=pc(7), in1=F["km"][:, 0:n], op0=ALU.mult, op1=ALU.mult), rd=[Z["rm"], Z["km"], cB], wr=[Z["t7"]])
            ps, pb = k.psum()
            k.op(k.pe, lambda e: e.matmul(ps[:, 0:n], lhsT=bones[:, :], rhs=F["t7"][:, 0:n], start=True, stop=True), rd=[cB, Z["t7"]], wr=[pb])
            k.op(k.dve, lambda e: e.tensor_tensor(out=C["bv"][:, 0:n], in0=ps[:, 0:n], in1=F["vm"][:, 0:n], op=ALU.mult), rd=[pb, Z["vm"]], wr=[CZ["bv"]])
            yield
            for n_ in ("al", "be", "kt"):
                k.op(k.act, lambda e: e.activation(out=HbP[n_][0][0:64, 0:n], in_=Hb[n_][0:64, 0:n], func=AF.Copy), rd=[HZ[n_]], wr=[HZ[n_ + "P0"]])
                k.op(k.act, lambda e: e.activation(out=HbP[n_][1][64:128, 0:n], in_=Hb[n_][64:128, 0:n], func=AF.Copy), rd=[HZ[n_]], wr=[HZ[n_ + "P1"]])
            yield
        def prepB(pi, t0, n, smp, par, par2):
            C = CS[par]; CZ = C["Z"]; H = HS[par2]; Hb = H["Hb"]; HbP = H["HbP"]; HZ = H["Z"]
            TT = 64 if smp else 128
            ntile = n // TT
            NM = 2 * ntile
            for ti in range(ntile):
                sl = slice(ti * TT, (ti + 1) * TT)
                pt, ptb = k.psum()
                ptv = pt[:].bitcast(BF16)
                for qi, nm in enumerate(("al", "be", "kt", "vb")):
                    k.op(k.pe, lambda e: e.transpose(ptv[0:TT, qi * 128:(qi + 1) * 128], Hb[nm][:, sl], identb[:, :]), rd=[HZ[nm], cB], wr=[ptb])
                pv = ptv[0:TT, 0:512].rearrange("p (q c) -> p q c", c=128)
                k.op(k.dve, lambda e: e.tensor_copy(out=C["tp"][0:TT, ti, :, 0:64], in_=pv[:, :, 0:64]), rd=[ptb], wr=[CZ["tp"]])
                k.op(k.act, lambda e: e.activation(out=C["tp"][0:TT, ti, :, 128:192], in_=pv[:, :, 64:128], func=AF.Copy), rd=[ptb], wr=[CZ["tp"]])
            yield
            m1, m2, m0 = (4, 5, 3) if smp else (1, 2, 0)
            specs = [("al", "be", m1), ("be", "al", m2), ("kt", "al", m2), ("be", "rt", m0), ("kt", "rt", m0)]
            banks = [k.psum() for _ in range(5)]
            for ti in range(ntile):
                sl = slice(ti * TT, (ti + 1) * TT)
                for e_ in range(2):
                    m = ti * 2 + e_
                    for si_, (ln, rn, mk) in enumerate(specs):
                        lt = HbP[ln][e_][:, sl]
                        rtn = C["rt"][:, sl] if rn == "rt" else Hb[rn][:, sl]
                        rb_ = CZ["rt"] if rn == "rt" else HZ[rn]
                        k.op(k.pe, lambda e: e.matmul(banks[si_][0][0:TT, m * 128:m * 128 + TT], lhsT=lt, rhs=rtn, start=True, stop=True), rd=[HZ[ln + "P%d" % e_], rb_], wr=[banks[si_][1]])
            for si_, (ln, rn, mk) in enumerate(specs):
                dst = mp[0:TT, si_, 0:NM, 0:TT] if si_ < 3 else C["m34"][0:TT, si_ - 3, 0:NM, 0:TT]
                db_ = Z["mp"] if si_ < 3 else CZ["m34"]
                src_ = banks[si_][0][0:TT, 0:NM * 128].rearrange("p (m c) -> p m c", c=128)[:, :, 0:TT]
                k.op(k.dve, lambda e: e.tensor_tensor(out=dst, in0=src_, in1=cm[0:TT, mk, 0:TT].unsqueeze(1).to_broadcast([TT, NM, TT]), op=ALU.mult), rd=[banks[si_][1], cB], wr=[db_])
            yield
            P = C["P"]
            k.op(k.dve, lambda e: e.tensor_tensor(out=P[0:TT, 0:NM, 0:TT], in0=mp[0:TT, 1, 0:NM, 0:TT], in1=identb[0:TT, 0:TT].unsqueeze(1).to_broadcast([TT, NM, TT]), op=ALU.add),
                 rd=[Z["mp"], cB], wr=[CZ["P"]])
            cA = lambda m: mp[0:TT, 0, m, 0:TT]
            cBm = lambda m: mp[0:TT, 1, m, 0:TT]
            nlev = 2 if smp else 6
            for lv in range(nlev):
                pqa, pqab = k.psum(); pqb, pqbb = k.psum()
                for m in range(NM):
                    k.op(k.pe, lambda e: e.matmul(pqb[0:TT, m * 128:m * 128 + TT], lhsT=cA(m), rhs=cBm(m), start=True, stop=True), rd=[Z["mp"], Z["pwA"], Z["pwB"]], wr=[pqbb])
                    k.op(k.pe, lambda e: e.matmul(pqa[0:TT, m * 128:m * 128 + TT], lhsT=cBm(m), rhs=cA(m), start=True, stop=True), rd=[Z["mp"], Z["pwA"], Z["pwB"]], wr=[pqab])
                nw = pw[lv % 2]
                va = pqa[0:TT, 0:NM * 128].rearrange("p (m c) -> p m c", c=128)[:, :, 0:TT]
                vb_ = pqb[0:TT, 0:NM * 128].rearrange("p (m c) -> p m c", c=128)[:, :, 0:TT]
                k.op(k.act, lambda e: e.activation(out=nw[0:TT, 0, 0:NM, 0:TT], in_=va, func=AF.Copy), rd=[pqab], wr=[Z["pwA"]])
                k.op(k.dve, lambda e: e.tensor_copy(out=nw[0:TT, 1, 0:NM, 0:TT], in_=vb_), rd=[pqbb], wr=[Z["pwB"]])
                cA = lambda m, nw=nw: nw[0:TT, 0, m, 0:TT]
                cBm = lambda m, nw=nw: nw[0:TT, 1, m, 0:TT]
                yield
                pr, prb = k.psum()
                for m in range(NM):
                    k.op(k.pe, lambda e: e.matmul(pr[0:TT, m * 128:m * 128 + TT], lhsT=cA(m), rhs=P[0:TT, m, 0:TT], start=True, stop=True), rd=[Z["pwA"], CZ["P"]], wr=[prb])
                vp = pr[0:TT, 0:NM * 128].rearrange("p (m c) -> p m c", c=128)[:, :, 0:TT]
                k.op(k.dve, lambda e: e.tensor_tensor(out=P[0:TT, 0:NM, 0:TT], in0=vp, in1=P[0:TT, 0:NM, 0:TT], op=ALU.add), rd=[prb, CZ["P"]], wr=[CZ["P"]])
                yield
            pa, pab = k.psum(); py, pyb = k.psum()
            for ti in range(ntile):
                for e_ in range(2):
                    m = ti * 2 + e_
                    k.op(k.pe, lambda e: e.matmul(pa[:, ti * 128:ti * 128 + TT], lhsT=C["tp"][0:TT, ti, 0, e_ * 64:e_ * 64 + 128], rhs=P[0:TT, m, 0:TT], start=(e_ == 0), stop=(e_ == 1)),
                         rd=[CZ["tp"], CZ["P"]], wr=[pab])
                    k.op(k.pe, lambda e: e.matmul(py[0:TT, m * 64:(m + 1) * 64], lhsT=mp[0:TT, 2, m, 0:TT], rhs=C["tp"][0:TT, ti, 3, e_ * 128:e_ * 128 + 64], start=True, stop=True),
                         rd=[Z["mp"], CZ["tp"]], wr=[pyb])
            k.op(k.act, lambda e: e.activation(out=C["ah"][:, 0:ntile, 0:TT], in_=pa[:, 0:ntile * 128].rearrange("p (t c) -> p t c", c=128)[:, :, 0:TT], func=AF.Copy), rd=[pab], wr=[CZ["ah"]])
            k.op(k.dve, lambda e: e.tensor_copy(out=C["Yb"][0:TT, 0:ntile].rearrange("p t e v -> p (t e v)"), in_=py[0:TT, 0:NM * 64]), rd=[pyb], wr=[CZ["Yb"]])
            yield

        def chain(pi, t0, n, smp, par):
            C = CS[par]; CZ = C["Z"]
            TT = 64 if smp else 128
            ntile = n // TT
            P = C["P"]
            pc = lambda j: rw2[:, j, pi:pi + 1]
            if t0 == 0 and not smp:
                k.op(k.dve, lambda e: e.memset(S[:], 0.0), wr=[Z["S"]])
                k.op(k.dve, lambda e: e.memset(Sbd[:], 0.0), wr=[Z["S"]])
            if smp:
                for e_ in range(2):
                    k.dma(k.sp, S0s[64 * e_:64 * e_ + 64], L["srwT"][:, 2 * pi + e_].rearrange("b k v -> k b v"), wr=[Z["S0s"]])
                k.op(k.act, lambda e: e.activation(out=Sbds[0:64, :, 0:64], in_=S0s[0:64, :, :], func=AF.Copy), rd=[Z["S0s"]], wr=[Z["S0s"]])
                k.op(k.dve, lambda e: e.tensor_copy(out=Sbds[64:128, :, 64:128], in_=S0s[64:128, :, :]), rd=[Z["S0s"]], wr=[Z["S0s"]])
                k.op(k.dve, lambda e: e.tensor_tensor(out=ahm[:], in0=C["ah"][:, 0, 0:64].unsqueeze(1).to_broadcast([128, NS, 64]), in1=smkT[:], op=ALU.mult), rd=[CZ["ah"], cB], wr=[Z["msk"]])
                k.op(k.dve, lambda e: e.tensor_tensor(out=rtm[:], in0=C["rt"][:, 0:64].unsqueeze(1).to_broadcast([128, NS, 64]), in1=smkT[:], op=ALU.mult), rd=[CZ["rt"], cB], wr=[Z["msk"]])
            po, pob = k.psum()
            k.held.add(pob)
            for ti in range(ntile):
                sl = slice(ti * TT, (ti + 1) * TT)
                pu, pub = k.psum(exclude=[pob])
                for e_ in range(2):
                    k.op(k.pe, lambda e: e.matmul(pu[0:TT, e_ * 64:(e_ + 1) * 64], lhsT=P[0:TT, ti * 2 + e_, 0:TT], rhs=C["Yb"][0:TT, ti, e_, :], start=(e_ == 0), stop=False,
                                                  skip_group_check=True),
                         rd=[CZ["P"], CZ["Yb"]], wr=[pub])
                if not smp:
                    k.op(k.pe, lambda e: e.matmul(pu[0:TT, 0:128], lhsT=C["ah"][:, ti, 0:TT], rhs=Sbd[:, :], start=False, stop=True, skip_group_check=True), rd=[CZ["ah"], Z["S"]], wr=[pub])
                else:
                    for b in range(NS):
                        k.op(k.pe, lambda e: e.matmul(pu[0:TT, 0:128], lhsT=ahm[:, b, :], rhs=Sbds[:, b, :], start=False, stop=(b == NS - 1), skip_group_check=True), rd=[Z["msk"], Z["S0s"]], wr=[pub])
                k.op(k.dve, lambda e: e.tensor_copy(out=Ubp[0:TT, 0:64], in_=pu[0:TT, 0:64]), rd=[pub], wr=[Z["Ubp"]])
                k.op(k.act, lambda e: e.activation(out=Ubp[0:TT, 128:192], in_=pu[0:TT, 64:128], func=AF.Copy), rd=[pub], wr=[Z["Ubp"]])
                if smp:
                    yield
                oc = slice(ti * TT, (ti + 1) * TT)
                if not smp:
                    k.op(k.pe, lambda e: e.matmul(po[:, oc], lhsT=Sbd[:, :], rhs=C["rt"][:, sl], start=True, stop=False), rd=[Z["S"], CZ["rt"]], wr=[pob])
                else:
                    for b in range(NS):
                        k.op(k.pe, lambda e: e.matmul(po[:, oc], lhsT=Sbds[:, b, :], rhs=rtm[:, b, :], start=(b == 0), stop=False), rd=[Z["S0s"], Z["msk"]], wr=[pob])
                for e_ in range(2):
                    m = ti * 2 + e_
                    k.op(k.pe, lambda e: e.matmul(po[:, oc], lhsT=Ubp[0:TT, e_ * 64:e_ * 64 + 128], rhs=C["m34"][0:TT, 0, m, 0:TT], start=False, stop=False), rd=[Z["Ubp"], CZ["m34"]], wr=[pob])
                    k.op(k.pe, lambda e: e.matmul(po[:, oc], lhsT=C["tp"][0:TT, ti, 3, e_ * 64:e_ * 64 + 128], rhs=C["m34"][0:TT, 1, m, 0:TT], start=False, stop=(e_ == 1)), rd=[CZ["tp"], CZ["m34"]], wr=[pob])
                if smp:
                    yield
                if not smp:
                    pS, pSb = k.psum(exclude=[pob])
                    for e_ in range(2):
                        k.op(k.pe, lambda e: e.matmul(pS[:, 0:64], lhsT=C["tp"][0:TT, ti, 1, e_ * 64:e_ * 64 + 128], rhs=Ubp[0:TT, e_ * 128:e_ * 128 + 64], start=(e_ == 0), stop=False), rd=[CZ["tp"], Z["Ubp"]], wr=[pSb])
                        k.op(k.pe, lambda e: e.matmul(pS[:, 0:64], lhsT=C["tp"][0:TT, ti, 2, e_ * 64:e_ * 64 + 128], rhs=C["tp"][0:TT, ti, 3, e_ * 128:e_ * 128 + 64], start=False, stop=(e_ == 1)), rd=[CZ["tp"]], wr=[pSb])
                    k.op(k.dve, lambda e: e.tensor_tensor(out=S[:, :], in0=pS[:, 0:64], in1=S[:, :], op=ALU.add), rd=[pSb, Z["S"]], wr=[Z["S"]])
                    k.op(k.dve, lambda e: e.tensor_scalar(out=S[:, :], in0=S[:, :], scalar1=C["wc"][:, ti:ti + 1], scalar2=None, op0=ALU.mult), rd=[Z["S"], CZ["wc"]], wr=[Z["S"]])
                    k.op(k.act, lambda e: e.activation(out=Sbd[0:64, 0:64], in_=S[0:64, :], func=AF.Copy), rd=[Z["S"]], wr=[Z["S"]])
                    k.op(k.dve, lambda e: e.tensor_copy(out=Sbd[64:128, 64:128], in_=S[64:128, :]), rd=[Z["S"]], wr=[Z["S"]])
                    yield
                else:
                    for e_ in range(2):
                        k.op(k.dve, lambda e: e.tensor_tensor(out=Ublk[:, e_], in0=Ubp[0:64, e_ * 128:e_ * 128 + 64].unsqueeze(1).to_broadcast([64, NS, 64]), in1=smk[:].unsqueeze(2).to_broadcast([64, NS, 64]), op=ALU.mult), rd=[Z["Ubp"], cB], wr=[Z["msk"]])
                        k.op(k.dve, lambda e: e.tensor_tensor(out=Vblk[:, e_], in0=C["tp"][0:64, 0, 3, e_ * 128:e_ * 128 + 64].unsqueeze(1).to_broadcast([64, NS, 64]), in1=smk[:].unsqueeze(2).to_broadcast([64, NS, 64]), op=ALU.mult), rd=[CZ["tp"], cB], wr=[Z["msk"]])
                    for half in range(2):
                        pS, pSb = k.psum(exclude=[pob])
                        for b8 in range(8):
                            b = half * 8 + b8
                            for e_ in range(2):
                                k.op(k.pe, lambda e: e.matmul(pS[:, b8 * 64:(b8 + 1) * 64], lhsT=C["tp"][0:64, 0, 1, e_ * 64:e_ * 64 + 128], rhs=Ublk[:, e_, b, :], start=(e_ == 0), stop=False), rd=[CZ["tp"], Z["msk"]], wr=[pSb])
                                k.op(k.pe, lambda e: e.matmul(pS[:, b8 * 64:(b8 + 1) * 64], lhsT=C["tp"][0:64, 0, 2, e_ * 64:e_ * 64 + 128], rhs=Vblk[:, e_, b, :], start=False, stop=(e_ == 1)), rd=[CZ["tp"], Z["msk"]], wr=[pSb])
                        sv = S0s[:, half * 8:(half + 1) * 8, :]
                        k.op(k.dve, lambda e: e.tensor_tensor(out=sv, in0=pS[:, 0:512].rearrange("p (b v) -> p b v", v=64), in1=sv, op=ALU.add), rd=[pSb, Z["S0s"]], wr=[Z["S0s"]])
                        k.op(k.dve, lambda e: e.tensor_tensor(out=sv, in0=sv, in1=C["wc"][:, half * 8:(half + 1) * 8].unsqueeze(2).to_broadcast([128, 8, 64]), op=ALU.mult), rd=[Z["S0s"], CZ["wc"]], wr=[Z["S0s"]])
                        yield
                    for e_ in range(2):
                        k.dma(k.pool, L["nrw_sT"][:, 2 * pi + e_].rearrange("b k v -> k b v"), S0s[64 * e_:64 * e_ + 64], rd=[Z["S0s"]])
            k.op(k.act, lambda e: e.activation(out=oT[:, 0:n], in_=po[:, 0:n], func=AF.Copy), rd=[pob], wr=[Z["oT"]])
            k.held.discard(pob)
            ps, pb = k.psum()
            k.op(k.pe, lambda e: e.matmul(ps[:, 0:n], lhsT=bones[:, :], rhs=oT[:, 0:n], start=True, stop=True), rd=[cB, Z["oT"]], wr=[pb])
            k.op(k.dve, lambda e: e.scalar_tensor_tensor(out=oT[:, 0:n], in0=ps[:, 0:n], scalar=-1.0 / 64, in1=oT[:, 0:n], op0=ALU.mult, op1=ALU.add), rd=[pb, Z["oT"]], wr=[Z["oT"]])
            k.op(k.dve, lambda e: e.tensor_tensor(out=t7c[:, 0:n], in0=oT[:, 0:n], in1=oT[:, 0:n], op=ALU.mult), rd=[Z["oT"]], wr=[Z["t7c"]])
            if not smp:
                yield
            ps, pb = k.psum()
            k.op(k.pe, lambda e: e.matmul(ps[:, 0:n], lhsT=bones[:, :], rhs=t7c[:, 0:n], start=True, stop=True), rd=[cB, Z["t7c"]], wr=[pb])
            k.op(k.dve, lambda e: e.tensor_scalar(out=t7c[:, 0:n], in0=ps[:, 0:n], scalar1=1.0 / 64, scalar2=64e-5, op0=ALU.mult, op1=ALU.add), rd=[pb], wr=[Z["t7c"]])
            k.op(k.act, lambda e: e.activation(out=t7c[:, 0:n], in_=t7c[:, 0:n], func=AF.Ln), rd=[Z["t7c"]], wr=[Z["t7c"]])
            k.op(k.act, lambda e: e.activation(out=t7c[:, 0:n], in_=t7c[:, 0:n], func=AF.Exp, scale=-0.5), rd=[Z["t7c"]], wr=[Z["t7c"]])
            k.op(k.dve, lambda e: e.tensor_tensor(out=oT[:, 0:n], in0=oT[:, 0:n], in1=t7c[:, 0:n], op=ALU.mult), rd=[Z["oT"], Z["t7c"]], wr=[Z["oT"]])
            k.op(k.dve, lambda e: e.tensor_scalar(out=oT[:, 0:n], in0=oT[:, 0:n], scalar1=pc(8), scalar2=pc(9), op0=ALU.mult, op1=ALU.add), rd=[Z["oT"], cB], wr=[Z["oT"]])
            k.op(k.dve, lambda e: e.tensor_tensor(out=oT[:, 0:n], in0=oT[:, 0:n], in1=C["bv"][:, 0:n], op=ALU.add), rd=[Z["oT"], CZ["bv"]], wr=[Z["oT"]])
            k.op(k.dve, lambda e: e.tensor_tensor(out=yb[:, 0:n], in0=oT[:, 0:n], in1=C["sg"][:, 0:n], op=ALU.mult), rd=[Z["oT"], CZ["sg"]], wr=[Z["y"]])
            k.dma(k.pool, L["ymix_d"][16 + pi, :, t0:t0 + n], yb[:, 0:n], rd=[Z["y"]])
            if (not smp) and t0 + n == T:
                k.dma(k.pool, L["nrw_pT"][2 * pi:2 * pi + 2].rearrange("e k v -> (e k) v"), S[:, :], rd=[Z["S"]])
            yield

        blocks = []
        for pi in range(RW_PAIRS):
            for bi in range(T // NB):
                blocks.append((pi, bi * NB, NB, False))
            if RW_SAMPLE:
                blocks.append((pi, T, TSM, True))
        nb_ = len(blocks)
        for j in range(nb_ + 2):
            gens = []
            if j < nb_:
                b_ = blocks[j]; gens.append([prepA(b_[0], b_[1], b_[2], b_[3], j % 3, j % 2), 0, 20 if b_[3] else 22])
            if 0 <= j - 1 < nb_:
                b_ = blocks[j - 1]; gens.append([prepB(b_[0], b_[1], b_[2], b_[3], (j - 1) % 3, (j - 1) % 2), 0, 7 if b_[3] else 15])
            if 0 <= j - 2 < nb_:
                b_ = blocks[j - 2]; gens.append([chain(b_[0], b_[1], b_[2], b_[3], (j - 2) % 3), 0, 6 if b_[3] else 5])
            while gens:
                g = min(gens, key=lambda x: (x[1] + 0.5) / x[2])
                try:
                    next(g[0])
                    g[1] += 1
                except StopIteration:
                    gens.remove(g)
        k.barrier()
```

```python
import math
import numpy as np
from contextlib import ExitStack
import concourse.bass as bass
import concourse.mybir as mybir
from concourse.bass_utils import run_bass_kernel_spmd

F32 = mybir.dt.float32
BF16 = mybir.dt.bfloat16
AF = mybir.ActivationFunctionType
ALU = mybir.AluOpType
AX = mybir.AxisListType

D = 2048
NIN = 16576
T = 2048
NS = 16
TSM = 64
KC = 16
STAGE = 99
DEBUG = False
RW_PAIRS = 16
RW_SAMPLE = True


class Buf:
    __slots__ = ("w", "r")

    def __init__(self):
        self.w = None
        self.r = {}


class Eng:
    def __init__(self, nc, eng, key, sems, is_pe=False):
        self.eng = eng
        self.key = key
        self.sem = sems[key]
        self.cnt = 0
        self.waited = {}
        self.is_pe = is_pe
        self.slots = []
        self.uses = []
        self.n = 0


class K:
    def __init__(self, nc, es):
        self.nc = nc
        self.es = es
        self.sems = {}
        for k in ["pe", "act", "dve", "pool", "sp"]:
            self.sems[k] = es.enter_context(nc.semaphore("s_" + k))
        self.pe = Eng(nc, nc.tensor, "pe", self.sems, True)
        self.act = Eng(nc, nc.scalar, "act", self.sems)
        self.dve = Eng(nc, nc.vector, "dve", self.sems)
        self.pool = Eng(nc, nc.gpsimd, "pool", self.sems)
        self.sp = Eng(nc, nc.sync, "sp", self.sems)
        for q, n in ((self.sp, 8), (self.pool, 8)):
            for i in range(n):
                k = "%s_d%d" % (q.key, i)
                self.sems[k] = es.enter_context(nc.semaphore("s_" + k))
                q.slots.append(k)
                q.uses.append(0)
        self.engs = [self.pe, self.act, self.dve, self.pool, self.sp]
        self.nps = 0
        self.ps = []
        self.psb = []
        self.evq = 0
        self.held = set()

    def _deps(self, en, rd, wr, extra=None):
        deps = dict(extra or {})
        for b in rd:
            if b.w:
                k, v = b.w
                deps[k] = max(deps.get(k, 0), v)
        for b in wr:
            if b.w:
                k, v = b.w
                deps[k] = max(deps.get(k, 0), v)
            for k, v in b.r.items():
                deps[k] = max(deps.get(k, 0), v)
        for k, v in deps.items():
            if en.is_pe and k == en.key:
                continue
            if en.waited.get(k, 0) < v:
                en.eng.wait_ge(self.sems[k], v)
                en.waited[k] = v

    def _post(self, ev, rd, wr):
        k, v = ev
        for b in rd:
            b.r[k] = max(b.r.get(k, 0), v)
        for b in wr:
            b.w = ev
            b.r = {}

    def op(self, en, fn, rd=(), wr=()):
        self._deps(en, rd, wr)
        ins = fn(en.eng)
        en.cnt += 1
        ins.then_inc(en.sem, 1)
        self._post((en.key, en.cnt), rd, wr)

    def dma(self, q, out, in_, rd=(), wr=()):
        s = q.n % len(q.slots)
        q.n += 1
        k = q.slots[s]
        q.uses[s] += 1
        u = q.uses[s]
        extra = {k: 16 * (u - 1)} if u > 1 else None
        self._deps(q, rd, wr, extra)
        ins = q.eng.dma_start(out=out, in_=in_)
        ins.then_inc(self.sems[k], 16)
        self._post((k, 16 * u), rd, wr)

    def barrier(self):
        tgt = {}
        for e in self.engs:
            if e.cnt:
                tgt[e.key] = e.cnt
            for k, u in zip(e.slots, e.uses):
                if u:
                    tgt[k] = 16 * u
        for e in self.engs:
            for k, v in tgt.items():
                if k == e.key:
                    continue
                if e.waited.get(k, 0) < v:
                    e.eng.wait_ge(self.sems[k], v)
                    e.waited[k] = v

    def finish(self):
        self.barrier()

    def psum(self, exclude=()):
        while True:
            i = self.nps % 8
            self.nps += 1
            if self.psb[i] not in exclude and self.psb[i] not in self.held:
                return self.ps[i], self.psb[i]

    def ev_eng(self):
        self.evq += 1
        return self.dve if self.evq % 2 else self.act


def build(stage=STAGE, debug=DEBUG):
    nc = bass.Bass("TRN2", target_bir_lowering=False)

    def din(name, shape, dt=F32):
        return nc.dram_tensor(name, list(shape), dt, kind="ExternalInput").ap()

    def dout(name, shape, dt=F32):
        return nc.dram_tensor(name, list(shape), dt, kind="ExternalOutput").ap()

    xT = din("xT", [128, KC, T])
    x = din("x", [T, D])
    xsT = din("xsT", [128, KC, TSM])
    xs = din("xs", [TSM, D])
    hpT = din("hpT", [128, KC, NS])
    cT = din("cT", [128, KC, 17])
    sret = din("sret", [NS, 8, 256, 256])
    srwT = din("srwT", [NS, 32, 64, 64])
    w_ada = din("w_ada", [D, 3 * D])
    b_adaT = din("b_adaT", [128, 48])
    b_ada_rows = din("b_ada_rows", [17, 3 * D])
    w_in = din("w_in", [D, NIN])
    w_out = din("w_out", [2 * D, D])
    norm_wT = din("norm_wT", [128, KC])
    fnw_b = din("fnw_b", [128, D])
    rwp2 = din("rwp2", [128, 10, 16])
    bones = din("bones", [128, 128])
    rwp = din("rwp", [64, 10, 32])
    mu_lora = din("mu_lora", [96, 2])
    w2d = din("w2d", [96, D])
    a2 = din("a2", [96, D])
    ident = din("ident", [128, 128])
    cmask = din("cmask", [128, 6, 128])
    seqmask = din("seqmask", [64, NS])
    seqmaskT = din("seqmaskT", [128, NS, 64])
    rope = din("rope", [128, 4, T])
    ropes = din("ropes", [128, 2, TSM])
    rdec = din("rdec", [128, 8, 4, 128])
    ones = din("ones", [128, 128])
    selp = din("selp", [17, 128])
    sels = din("sels", [17, 64])

    y = dout("y", [T, D])
    ys = dout("ys", [TSM, D])
    nret_p = dout("nret_p", [8, 256, 256])
    nrw_pT = dout("nrw_pT", [32, 64, 64])
    nshift_p = dout("nshift_p", [128, KC])
    nret_s = dout("nret_s", [NS, 8, 256, 256])
    nrw_sT = dout("nrw_sT", [NS, 32, 64, 64])
    nshift_s = dout("nshift_s", [128, KC, NS])
    xnew_d = nc.dram_tensor("xnew_d", [T + TSM, D], F32, kind="Internal").ap()
    ymix_d = nc.dram_tensor("ymix_d", [32, 128, T + TSM], BF16, kind="Internal").ap()
    dbg = {}
    if debug:
        dbg["hT"] = dout("dbg_hT", [128, KC, T + 80])
        dbg["modT"] = dout("dbg_modT", [128, 48, 17])

    with ExitStack() as es:
        k = K(nc, es)

        _cnt = [0]

        def sb(name, shape, dt=F32, stack=es):
            _cnt[0] += 1
            return stack.enter_context(nc.sbuf_tensor("%s_%d" % (name, _cnt[0]), list(shape), dt))

        for i in range(8):
            k.ps.append(es.enter_context(nc.psum_tensor("ps%d" % i, [128, 512], F32)))
            k.psb.append(Buf())

        identf = sb("identf", [128, 128]); identb = sb("identb", [128, 128], BF16)
        onesf = sb("onesf", [128, 128])
        cm = sb("cm", [128, 6, 128])
        smk = sb("smk", [64, NS]); smkT = sb("smkT", [128, NS, 64])
        nwT = sb("nwT", [128, KC]); baT = sb("baT", [128, 48])
        rw = sb("rw", [64, 10, 32]); mul = sb("mul", [96, 2])
        cTs = sb("cTs", [128, KC, 17])
        modT = sb("modT", [128, 48, 17])
        gT = sb("gT", [128, KC, 17]); shT = sb("shT", [128, KC, 17])
        hst = ExitStack()
        hlast = sb("hlast", [128, KC], F32, hst); hlast_s = sb("hlast_s", [128, KC, NS], F32, hst)
        hT = sb("hT", [128, KC, T], BF16, hst)
        hTs = sb("hTs", [128, KC, 80], BF16, hst)
        B = {n: Buf() for n in ["const", "cTs", "modT", "gsh", "hT", "hTs", "misc"]}
        for dst, src in ((identf, ident), (onesf, ones), (cm, cmask), (smk, seqmask), (smkT, seqmaskT),
                         (nwT, norm_wT), (baT, b_adaT), (rw, rwp), (mul, mu_lora)):
            k.dma(k.sp, dst[:], src, wr=[B["const"]])
        k.dma(k.sp, cTs[:], cT, wr=[B["cTs"]])
        k.op(k.dve, lambda e: e.tensor_copy(out=identb[:], in_=identf[:]), rd=[B["const"]], wr=[B["const"]])
        k.op(k.act, lambda e: e.activation(out=cTs[:], in_=cTs[:], func=AF.Silu), rd=[B["cTs"]], wr=[B["cTs"]])

        with ExitStack() as ph:
            wa = [sb("wa%d" % i, [128, KC, 512], F32, ph) for i in range(2)]
            wab = [Buf(), Buf()]
            modrow = sb("modrow", [17, 3 * D], F32, ph); mrb = Buf()
            barow = sb("barow", [17, 3 * D], F32, ph); brb = Buf()
            k.dma(k.sp, barow[:], b_ada_rows, wr=[brb])
            for g in range(12):
                i = g % 2
                k.dma(k.sp, wa[i][:], w_ada[:, g * 512:(g + 1) * 512].rearrange("(c p) n -> p c n", p=128), wr=[wab[i]])
                ps, pb = k.psum()
                for c in range(KC):
                    k.op(k.pe, lambda e, c=c, i=i, ps=ps: e.matmul(ps[0:17, 0:512], lhsT=cTs[:, c, :], rhs=wa[i][:, c, :],
                                                                   start=(c == 0), stop=(c == KC - 1)), rd=[wab[i], B["cTs"]], wr=[pb])
                k.op(k.dve, lambda e, ps=ps, g=g: e.tensor_tensor(out=modrow[:, g * 512:(g + 1) * 512], in0=ps[0:17, 0:512],
                                                                  in1=barow[:, g * 512:(g + 1) * 512], op=ALU.add), rd=[pb, brb], wr=[mrb])
            for g4 in range(12):
                ps, pb = k.psum()
                for q in range(4):
                    fc = g4 * 4 + q
                    k.op(k.pe, lambda e, ps=ps, q=q, fc=fc: e.transpose(ps[:, q * 17:(q + 1) * 17], modrow[:, fc * 128:(fc + 1) * 128], identf[0:17, 0:17]),
                         rd=[mrb, B["const"]], wr=[pb])
                k.op(k.dve, lambda e, ps=ps, g4=g4: e.tensor_copy(out=modT[:, g4 * 4:(g4 + 1) * 4, :], in_=ps[:, 0:68].rearrange("p (q r) -> p q r", r=17)),
                     rd=[pb], wr=[B["modT"]])
            k.op(k.dve, lambda e: e.tensor_scalar(out=gT[:], in0=modT[:, 16:32, :], scalar1=1.0, scalar2=None, op0=ALU.add),
                 rd=[B["modT"]], wr=[B["gsh"]])
            k.op(k.dve, lambda e: e.tensor_tensor(out=gT[:], in0=gT[:], in1=nwT[:].unsqueeze(2).to_broadcast([128, KC, 17]), op=ALU.mult),
                 rd=[B["gsh"], B["const"]], wr=[B["gsh"]])
            k.op(k.dve, lambda e: e.tensor_copy(out=shT[:], in_=modT[:, 0:16, :]), rd=[B["modT"]], wr=[B["gsh"]])
            k.barrier()
        if debug:
            k.dma(k.pool, dbg["modT"], modT[:], rd=[B["modT"]])

        with ExitStack() as ph:
            xt = [sb("xt%d" % i, [128, KC, 512], F32, ph) for i in range(2)]
            xtb = [Buf(), Buf()]
            sq = sb("sq", [128, 512], F32, ph); sqb = Buf()
            rstd = sb("rstd", [128, 512], F32, ph); rsb = Buf()
            tmp = [sb("tmpn%d" % i, [128, 512], F32, ph) for i in range(2)]
            tmpb = [Buf(), Buf()]
            groups = [(xT, tg * 512, 512, tg) for tg in range(4)] + [(xsT, 0, TSM, 4)]
            for (src, t0, n, gi) in groups:
                i = gi % 2
                k.dma(k.sp, xt[i][:, :, 0:n], src[:, :, t0:t0 + n], wr=[xtb[i]])
                ps, pb = k.psum()
                for c in range(KC):
                    j2 = c % 2
                    k.op(k.act, lambda e, c=c, i=i, n=n, j2=j2: e.activation(out=tmp[j2][:, 0:n], in_=xt[i][:, c, 0:n], func=AF.Square),
                         rd=[xtb[i]], wr=[tmpb[j2]])
                    if c == 0:
                        k.op(k.dve, lambda e, n=n, j2=j2: e.tensor_copy(out=sq[:, 0:n], in_=tmp[j2][:, 0:n]), rd=[tmpb[j2]], wr=[sqb])
                    else:
                        k.op(k.dve, lambda e, n=n, j2=j2: e.tensor_tensor(out=sq[:, 0:n], in0=sq[:, 0:n], in1=tmp[j2][:, 0:n], op=ALU.add), rd=[tmpb[j2], sqb], wr=[sqb])
                k.op(k.pe, lambda e, ps=ps, n=n: e.matmul(ps[:, 0:n], lhsT=onesf[:, :], rhs=sq[:, 0:n], start=True, stop=True),
                     rd=[sqb, B["const"]], wr=[pb])
                k.op(k.dve, lambda e, ps=ps, n=n: e.tensor_scalar(out=rstd[:, 0:n], in0=ps[:, 0:n], scalar1=1.0 / D, scalar2=1e-6,
                                                              op0=ALU.mult, op1=ALU.add), rd=[pb], wr=[rsb])
                k.op(k.act, lambda e, n=n: e.activation(out=rstd[:, 0:n], in_=rstd[:, 0:n], func=AF.Sqrt), rd=[rsb], wr=[rsb])
                k.op(k.dve, lambda e, n=n: e.reciprocal(out=rstd[:, 0:n], in_=rstd[:, 0:n]), rd=[rsb], wr=[rsb])
                for c in range(KC):
                    j = c % 2
                    k.op(k.dve, lambda e, c=c, i=i, j=j, n=n: e.tensor_tensor(out=tmp[j][:, 0:n], in0=xt[i][:, c, 0:n], in1=rstd[:, 0:n], op=ALU.mult),
                         rd=[xtb[i], rsb], wr=[tmpb[j]])
                    if gi < 4:
                        k.op(k.act, lambda e, c=c, j=j, t0=t0, n=n: e.activation(out=hT[:, c, t0:t0 + n], in_=tmp[j][:, 0:n], func=AF.Identity,
                                                                             scale=gT[:, c, 0:1], bias=shT[:, c, 0:1]),
                             rd=[tmpb[j], B["gsh"]], wr=[B["hT"]])
                        if gi == 3:
                            k.op(k.act, lambda e, c=c, j=j: e.activation(out=hlast[:, c:c + 1], in_=tmp[j][:, 511:512], func=AF.Identity,
                                                                         scale=gT[:, c, 0:1], bias=shT[:, c, 0:1]),
                                 rd=[tmpb[j], B["gsh"]], wr=[B["misc"]])
                    else:
                        v3 = tmp[j][:, 0:TSM].rearrange("p (s j) -> p s j", j=4)
                        k.op(k.dve, lambda e, c=c, v3=v3: e.tensor_tensor(out=v3, in0=v3, in1=gT[:, c, 1:17].unsqueeze(2).to_broadcast([128, NS, 4]), op=ALU.mult),
                             rd=[tmpb[j], B["gsh"]], wr=[tmpb[j]])
                        k.op(k.dve, lambda e, c=c, v3=v3: e.tensor_tensor(out=v3, in0=v3, in1=shT[:, c, 1:17].unsqueeze(2).to_broadcast([128, NS, 4]), op=ALU.add),
                             rd=[tmpb[j], B["gsh"]], wr=[tmpb[j]])
                        k.op(k.act, lambda e, c=c, j=j: e.activation(out=hTs[:, c, 0:TSM], in_=tmp[j][:, 0:TSM], func=AF.Copy),
                             rd=[tmpb[j]], wr=[B["hTs"]])
                        k.op(k.dve, lambda e, c=c, v3=v3: e.tensor_copy(out=hlast_s[:, c, :], in_=v3[:, :, 3]), rd=[tmpb[j]], wr=[B["misc"]])
            k.dma(k.sp, xt[0][:, :, 0:NS], hpT, wr=[xtb[0]])
            k.op(k.dve, lambda e: e.tensor_copy(out=hTs[:, :, TSM:TSM + NS], in_=xt[0][:, :, 0:NS]), rd=[xtb[0]], wr=[B["hTs"]])
            k.dma(k.pool, nshift_p, hlast[:], rd=[B["misc"]])
            k.dma(k.pool, nshift_s, hlast_s[:], rd=[B["misc"]])
            k.barrier()
        if debug:
            with ExitStack() as ph:
                d32 = sb("d32", [128, KC, 512], F32, ph); db = Buf()
                for tg in range(4):
                    k.op(k.dve, lambda e, tg=tg: e.tensor_copy(out=d32[:], in_=hT[:, :, tg * 512:(tg + 1) * 512]), rd=[B["hT"]], wr=[db])
                    k.dma(k.pool, dbg["hT"][:, :, tg * 512:(tg + 1) * 512], d32[:], rd=[db])
                k.op(k.dve, lambda e: e.tensor_copy(out=d32[:, :, 0:80], in_=hTs[:]), rd=[B["hTs"]], wr=[db])
                k.dma(k.pool, dbg["hT"][:, :, T:T + 80], d32[:, :, 0:80], rd=[db])
                k.barrier()
        if stage >= 2:
            phase_ret(nc, k, sb, locals())
        if stage >= 3:
            phase_rwkv(nc, k, sb, locals())
        k.barrier()
        hst.close()
        if stage >= 4:
            phase_out(nc, k, sb, locals())
        k.finish()
    return nc


def _bf(a):
    return np.ascontiguousarray(a, dtype=np.float32)


def host_inputs(inp, core):
    pb = core % 4
    s0 = core * NS
    xp = np.asarray(inp["x_prompt"][pb], np.float32)
    xsm = np.asarray(inp["x_sample"][s0:s0 + NS], np.float32).reshape(TSM, D)
    cc = np.concatenate([np.asarray(inp["c_prompt"][pb:pb + 1]), np.asarray(inp["c_sample"][s0:s0 + NS])], 0).astype(np.float32)

    def tr(a):
        return _bf(a.reshape(a.shape[0], KC, 128).transpose(2, 1, 0))

    def colv(v, nch):
        return _bf(np.asarray(v, np.float32).reshape(nch, 128).T)

    m = {}
    m["xT"] = tr(xp); m["x"] = _bf(xp); m["xsT"] = tr(xsm); m["xs"] = _bf(xsm)
    m["hpT"] = tr(np.asarray(inp["state_shift"][0, s0:s0 + NS], np.float32))
    m["cT"] = tr(cc)
    m["sret"] = _bf(inp["state_ret"][0, s0:s0 + NS])
    m["srwT"] = _bf(np.asarray(inp["state_rwkv"][0, s0:s0 + NS]).transpose(0, 1, 3, 2))
    m["w_ada"] = _bf(inp["w_ada"][0]); m["b_adaT"] = colv(inp["b_ada"][0], 48)
    m["b_ada_rows"] = _bf(np.broadcast_to(np.asarray(inp["b_ada"][0], np.float32)[None, :], (17, 3 * D)))
    m["w_in"] = _bf(inp["w_in"][0]); m["w_out"] = _bf(inp["w_out"][0])
    m["norm_wT"] = colv(inp["norm_w"][0], KC)
    m["fnw_b"] = _bf(np.broadcast_to(np.asarray(inp["final_norm_w"], np.float32)[None, :], (128, D)))
    mu = np.asarray(inp["mu_shift"][0], np.float32)

    def hv(v):
        return np.asarray(v, np.float32).reshape(32, 64).T

    rwp = np.stack([hv(mu[0:D]), hv(mu[D:2 * D]), hv(mu[2 * D:3 * D]), hv(inp["w0_decay"][0]), hv(inp["a0"][0]),
                    hv(inp["k_k"][0]), hv(inp["k_a"][0]), hv(np.asarray(inp["r_k"][0]).reshape(-1)),
                    hv(inp["ln_x_w"][0]), hv(inp["ln_x_b"][0])], 1)
    m["rwp"] = _bf(rwp)
    m["rwp2"] = _bf(np.concatenate([rwp[:, :, 0::2], rwp[:, :, 1::2]], 0))
    m["mu_lora"] = _bf(np.stack([mu[3 * D:3 * D + 96], mu[3 * D + 96:3 * D + 192]], 1))
    m["w2d"] = _bf(inp["w2_decay"][0]); m["a2"] = _bf(inp["a2"][0])
    m.update(CONSTS)
    return m


def make_consts():
    c = {}
    c["ident"] = np.eye(128, dtype=np.float32)
    c["ones"] = np.ones((128, 128), np.float32)
    i = np.arange(128)
    cm = np.zeros((128, 6, 128), np.float32)
    cm[:, 0, :] = (i[None, :] >= i[:, None])
    cm[:, 1, :] = (i[None, :] < i[:, None])
    cm[:, 2, :] = (i[None, :] > i[:, None])
    blk = (i[:, None] // 4 == i[None, :] // 4) & (i[:, None] < 64) & (i[None, :] < 64)
    for q in range(3):
        cm[:, 3 + q, :] = cm[:, q, :] * blk
    c["cmask"] = cm
    sm = np.zeros((64, NS), np.float32)
    sm[np.arange(64), np.arange(64) // 4] = 1
    c["seqmask"] = sm
    c["seqmaskT"] = np.ascontiguousarray(np.broadcast_to(sm.T[None], (128, NS, 64))).astype(np.float32)
    half = 128
    inv = (10000.0 ** (-np.arange(half, dtype=np.float32) / half)).astype(np.float32)
    pos = np.arange(T, dtype=np.float32)
    ang = (pos[None, :] * inv[:, None]).astype(np.float32)
    rope = np.zeros((128, 4, T), np.float32)
    rope[:, 0] = np.cos(ang); rope[:, 1] = np.sin(ang)
    c["rope"] = rope
    poss = (16384.0 + (np.arange(TSM) % 4)).astype(np.float32)
    angs = (poss[None, :] * inv[:, None]).astype(np.float32)
    c["ropes"] = np.stack([np.cos(angs), np.sin(angs)], 1).astype(np.float32)
    rdec = np.zeros((128, 8, 4, 128), np.float32)
    for h in range(8):
        lg = np.log1p(-np.exp2(-5.0 - h))
        ip = np.arange(128)
        rdec[:, h, 0, :] = np.exp((ip + 1) * lg)[None]
        rdec[:, h, 1, :] = (np.exp(-(ip + 1) * lg) * 256 ** -0.5)[None]
        isq = np.arange(64) % 4
        rdec[:, h, 2, :64] = np.exp((isq + 1) * lg)[None]
        rdec[:, h, 3, :64] = (np.exp(-(isq + 1) * lg) * 256 ** -0.5)[None]
    c["rdec"] = rdec
    bo = np.zeros((128, 128), np.float32); bo[:64, :64] = 1; bo[64:, 64:] = 1
    c["bones"] = bo
    sp_ = np.zeros((17, 128), np.float32); sp_[0] = 1
    ss_ = np.zeros((17, 64), np.float32); ss_[1:, :] = sm.T
    c["selp"] = sp_; c["sels"] = ss_
    return c


CONSTS = make_consts()
_NC = {}


def run(inputs, stage=STAGE, debug=DEBUG):
    key = (stage, debug)
    if key not in _NC:
        _NC[key] = build(stage, debug)
    nc = _NC[key]
    in_maps = [host_inputs(inputs, c) for c in range(8)]
    res = run_bass_kernel_spmd(nc, in_maps, core_ids=list(range(8)))
    return res.results


def kernel(**inputs):
    r = run(inputs)
    y_prompt = np.stack([r[c]["y"] for c in range(4)], 0)
    y_sample = np.concatenate([r[c]["ys"].reshape(NS, 4, D) for c in range(8)], 0)
    nret_p = np.stack([r[c]["nret_p"] for c in range(4)], 0)[None]
    nrw_p = np.stack([r[c]["nrw_pT"].transpose(0, 2, 1) for c in range(4)], 0)[None]
    nsh_p = np.stack([r[c]["nshift_p"].T.reshape(D) for c in range(4)], 0)[None]
    nret_s = np.concatenate([r[c]["nret_s"] for c in range(8)], 0)[None]
    nrw_s = np.concatenate([r[c]["nrw_sT"].transpose(0, 1, 3, 2) for c in range(8)], 0)[None]
    nsh_s = np.concatenate([r[c]["nshift_s"].transpose(2, 1, 0).reshape(NS, D) for c in range(8)], 0)[None]
    f = lambda a: np.ascontiguousarray(a, dtype=np.float32)
    return (f(y_prompt), f(y_sample), f(nret_p), f(nrw_p), f(nsh_p), f(nret_s), f(nrw_s), f(nsh_s))


class WStream:
    def __init__(self, nc, k, sb, stack, w_in, hT, hTs, Bh, Bhs, alloc=True):
        self.nc, self.k, self.w_in, self.hT, self.hTs, self.Bh, self.Bhs = nc, k, w_in, hT, hTs, Bh, Bhs
        self.n = 0
        if not alloc:
            return
        self.stg = sb("wstg", [128, KC, 256], F32, stack); self.stgb = Buf()
        self.wbig = sb("wbig", [128, KC, 512], BF16, stack)
        self.wb = [self.wbig[:, :, 0:256], self.wbig[:, :, 256:512]]
        self.wbb = [Buf(), Buf()]
        self.n = 0

    def tm2(self, t0, n, sample=False):
        k = self.k
        ps, pb = k.psum()
        src, sbf = (self.hTs, self.Bhs) if sample else (self.hT, self.Bh)
        for c in range(KC):
            k.op(k.pe, lambda e, c=c: e.matmul(ps[0:n, 0:512], lhsT=src[:, c, t0:t0 + n], rhs=self.wbig[:, c, :],
                                              start=(c == 0), stop=(c == KC - 1)), rd=[self.wbb[0], self.wbb[1], sbf], wr=[pb])
        return ps, pb

    def load(self, col0, ncol):
        k = self.k
        i = self.n % 2
        self.n += 1
        k.dma(k.sp, self.stg[:, :, 0:ncol], self.w_in[:, col0:col0 + ncol].rearrange("(c p) n -> p c n", p=128), wr=[self.stgb])
        k.op(k.pool, lambda e: e.tensor_copy(out=self.wb[i][:, :, 0:ncol], in_=self.stg[:, :, 0:ncol]), rd=[self.stgb], wr=[self.wbb[i]])
        return self.wb[i], self.wbb[i]

    def fm(self, wt, wtb, c0, m, t0, n, sample=False):
        k = self.k
        ps, pb = k.psum()
        src, sbf = (self.hTs, self.Bhs) if sample else (self.hT, self.Bh)
        for c in range(KC):
            k.op(k.pe, lambda e, c=c: e.matmul(ps[0:m, 0:n], lhsT=wt[:, c, c0:c0 + m], rhs=src[:, c, t0:t0 + n],
                                              start=(c == 0), stop=(c == KC - 1)), rd=[wtb, sbf], wr=[pb])
        return ps, pb

    def fm_gen(self, wt, wtb, c0, m, t0, n, sample=False, every=4):
        k = self.k
        ps, pb = k.psum()
        k.held.add(pb)
        src, sbf = (self.hTs, self.Bhs) if sample else (self.hT, self.Bh)
        for c in range(KC):
            k.op(k.pe, lambda e, c=c: e.matmul(ps[0:m, 0:n], lhsT=wt[:, c, c0:c0 + m], rhs=src[:, c, t0:t0 + n],
                                              start=(c == 0), stop=(c == KC - 1)), rd=[wtb, sbf], wr=[pb])
            if c % every == every - 1 and c < KC - 1:
                yield
        k.held.discard(pb)
        return ps, pb

    def tm(self, wt, wtb, ncol, t0, n, sample=False):
        k = self.k
        ps, pb = k.psum()
        src, sbf = (self.hTs, self.Bhs) if sample else (self.hT, self.Bh)
        for c in range(KC):
            k.op(k.pe, lambda e, c=c: e.matmul(ps[0:n, 0:ncol], lhsT=src[:, c, t0:t0 + n], rhs=wt[:, c, 0:ncol],
                                              start=(c == 0), stop=(c == KC - 1)), rd=[wtb, sbf], wr=[pb])
        return ps, pb


def phase_ret(nc, k, sb, L):
    hT, hTs, B = L["hT"], L["hTs"], L["B"]
    cm, smk, smkT, identb = L["cm"], L["smk"], L["smkT"], L["identb"]
    TA = T + TSM
    with ExitStack() as ph:
        ws = WStream(nc, k, sb, ph, L["w_in"], hT, hTs, B["hT"], B["hTs"])
        cosT = sb("cosT", [128, TA], F32, ph); sinT = sb("sinT", [128, TA], F32, ph); rb = Buf()
        k.dma(k.sp, cosT[:, 0:T], L["rope"][:, 0, :], wr=[rb]); k.dma(k.sp, sinT[:, 0:T], L["rope"][:, 1, :], wr=[rb])
        k.dma(k.sp, cosT[:, T:TA], L["ropes"][:, 0, :], wr=[rb]); k.dma(k.sp, sinT[:, T:TA], L["ropes"][:, 1, :], wr=[rb])
        dec = sb("dec", [128, 4, 128], F32, ph); decb = Buf()
        qT = sb("qT", [128, 2, TA], BF16, ph); kT = sb("kT", [128, 2, TA], BF16, ph); qb = Buf(); kb = Buf()
        vtm = sb("vtm", [128, 17, 256], BF16, ph); sgt = sb("sgt", [128, 17, 256], BF16, ph); vb = Buf(); gb = Buf()
        t1 = sb("rt1", [128, 512], F32, ph); t2 = sb("rt2", [128, 512], F32, ph); t1b = Buf(); t2b = Buf()
        S = sb("Sret", [128, 2, 256], F32, ph); Sb = sb("Sretb", [128, 2, 256], BF16, ph); Sbuf = Buf()
        Ss = [sb("Ss%d" % i, [128, 2, 256], F32, ph) for i in range(4)]; Ssb = [Buf() for _ in range(4)]
        Qm = sb("Qm", [128, 2, NS, 64], F32, ph); Qmb = Buf()
        Vblk = sb("Vblk", [64, NS, 256], BF16, ph); Vbb = Buf()
        scm = sb("scm", [128, 128], BF16, ph); scb = Buf()
        ktm = sb("ktm", [128, 256], BF16, ph); ktb = Buf()
        st6 = sb("st6", [128, 6], F32, ph); mv = sb("mv", [128, 2], F32, ph); rs = sb("rsg", [128, 1], F32, ph); stb = Buf()
        yf = sb("yf", [128, 256], F32, ph); ybf = sb("ybf", [128, 256], BF16, ph); yfb = Buf(); ybb = Buf()
        yTt = [sb("yTt%d" % i, [128, 2, 128], BF16, ph) for i in range(2)]; yTb = [Buf(), Buf()]
        groups = [(tg * 512, 512, False) for tg in range(4)] + [(0, TSM, True)]
        tiles = [(i * 128, 128, False, i) for i in range(16)] + [(T, 64, True, 16)]
        nsd = 0
        for hd in range(8):
            lg = math.log1p(-2.0 ** (-5.0 - hd))
            gC = math.exp(128 * lg); g4 = math.exp(4 * lg)
            k.dma(k.sp, dec[:], L["rdec"][:, hd], wr=[decb])
            for which, dst, dstb in ((0, qT, qb), (1, kT, kb)):
                wt, wtb = ws.load(which * D + hd * 256, 256)
                for (t0, n, smp) in groups:
                    p1, p1b = ws.fm(wt, wtb, 0, 128, t0, n, smp)
                    p2, p2b = ws.fm(wt, wtb, 128, 128, t0, n, smp)
                    o0 = T if smp else t0
                    cs = cosT[:, o0:o0 + n]; sn = sinT[:, o0:o0 + n]
                    if smp:
                        dv = dec[:, 2 + which, 0:64]; shp = None
                    else:
                        dv = dec[:, which, :].unsqueeze(1).to_broadcast([128, 4, 128])
                    for half in range(2):
                        a, ab, b_, bb = (p1, p1b, p2, p2b)
                        ca, cb = (cs, sn) if half == 0 else (sn, cs)
                        op2 = ALU.subtract if half == 0 else ALU.add
                        k.op(k.dve, lambda e: e.tensor_tensor(out=t1[:, 0:n], in0=a[:, 0:n], in1=ca, op=ALU.mult), rd=[ab, rb], wr=[t1b])
                        k.op(k.dve, lambda e: e.tensor_tensor(out=t2[:, 0:n], in0=b_[:, 0:n], in1=cb, op=ALU.mult), rd=[bb, rb], wr=[t2b])
                        k.op(k.dve, lambda e: e.tensor_tensor(out=t1[:, 0:n], in0=t1[:, 0:n], in1=t2[:, 0:n], op=op2), rd=[t1b, t2b], wr=[t1b])
                        if smp:
                            k.op(k.dve, lambda e: e.tensor_tensor(out=dst[:, half, T:T + 64], in0=t1[:, 0:64], in1=dv, op=ALU.mult),
                                 rd=[t1b, decb], wr=[dstb])
                        else:
                            k.op(k.dve, lambda e: e.tensor_tensor(out=dst[:, half, t0:t0 + 512].rearrange("p (a b) -> p a b", b=128),
                                                                  in0=t1[:, 0:512].rearrange("p (a b) -> p a b", b=128), in1=dv, op=ALU.mult),
                                 rd=[t1b, decb], wr=[dstb])
            assert ws.n % 2 == 0
            ws.load(2 * D + hd * 256, 256)
            ws.load(3 * D + hd * 256, 256)
            for (t0, n, smp, ti) in tiles:
                ps, pb = ws.tm2(0 if smp else t0, n, smp)
                k.op(k.act, lambda e: e.activation(out=vtm[0:n, ti, :], in_=ps[0:n, 0:256], func=AF.Copy), rd=[pb], wr=[vb])
                k.op(k.act, lambda e: e.activation(out=sgt[0:n, ti, :], in_=ps[0:n, 256:512], func=AF.Silu), rd=[pb], wr=[gb])
            k.op(k.dve, lambda e: e.memset(S[:], 0.0), wr=[Sbuf])
            k.op(k.dve, lambda e: e.memset(Sb[:], 0.0), wr=[Sbuf])
            for h in range(2):
                k.op(k.dve, lambda e: e.tensor_tensor(out=Qm[:, h], in0=qT[:, h, T:TA].unsqueeze(1).to_broadcast([128, NS, 64]),
                                                      in1=smkT[:], op=ALU.mult), rd=[qb, B["const"]], wr=[Qmb])
            k.op(k.dve, lambda e: e.tensor_tensor(out=Vblk[:], in0=vtm[0:64, 16, :].unsqueeze(1).to_broadcast([64, NS, 256]),
                                                  in1=smk[:].unsqueeze(2).to_broadcast([64, NS, 256]), op=ALU.mult),
                 rd=[vb, B["const"]], wr=[Vbb])
            for (t0, n, smp, ti) in tiles:
                mi = 3 if smp else 0
                ps, pb = k.psum()
                for h in range(2):
                    k.op(k.pe, lambda e: e.matmul(ps[0:n, 0:n], lhsT=kT[:, h, t0:t0 + n], rhs=qT[:, h, t0:t0 + n], start=(h == 0), stop=(h == 1)),
                         rd=[kb, qb], wr=[pb])
                k.op(k.dve, lambda e: e.tensor_tensor(out=scm[0:n, 0:n], in0=ps[0:n, 0:n], in1=cm[0:n, mi, 0:n], op=ALU.mult),
                     rd=[pb, B["const"]], wr=[scb])
                po, pob = k.psum()
                k.op(k.pe, lambda e: e.matmul(po[0:n, 0:256], lhsT=scm[0:n, 0:n], rhs=vtm[0:n, ti, :], start=True, stop=False), rd=[scb, vb], wr=[pob])
                if not smp:
                    for h in range(2):
                        k.op(k.pe, lambda e: e.matmul(po[0:n, 0:256], lhsT=qT[:, h, t0:t0 + n], rhs=Sb[:, h, :], start=False, stop=(h == 1)),
                             rd=[qb, Sbuf], wr=[pob])
                else:
                    for b in range(NS):
                        j = nsd % 4
                        nsd += 1
                        k.dma(k.sp, Ss[j][:], L["sret"][b, hd].rearrange("(h p) e -> p h e", p=128), wr=[Ssb[j]])
                        for h in range(2):
                            k.op(k.pe, lambda e: e.matmul(po[0:n, 0:256], lhsT=Qm[:, h, b, :], rhs=Ss[j][:, h, :], start=False,
                                                          stop=(b == NS - 1 and h == 1)), rd=[Qmb, Ssb[j]], wr=[pob])
                        pu, pub = k.psum(exclude=[pob])
                        if b == 0:
                            pt, ptb = k.psum(exclude=[pob, pub])
                            ptv = pt[:].bitcast(BF16)
                            for h in range(2):
                                k.op(k.pe, lambda e: e.transpose(ptv[0:64, h * 128:(h + 1) * 128], kT[:, h, T:TA], identb[:]), rd=[kb, B["const"]], wr=[ptb])
                            k.op(k.dve, lambda e: e.tensor_copy(out=ktm[0:64, :], in_=ptv[0:64, 0:256]), rd=[ptb], wr=[ktb])
                        for h in range(2):
                            k.op(k.pe, lambda e: e.matmul(pu[:, h * 256:(h + 1) * 256], lhsT=ktm[0:64, h * 128:(h + 1) * 128], rhs=Vblk[:, b, :],
                                                          start=True, stop=True), rd=[ktb, Vbb], wr=[pub])
                        sv = Ss[j][:].rearrange("p h e -> p (h e)")
                        k.op(k.dve, lambda e: e.tensor_scalar(out=sv, in0=sv, scalar1=g4, scalar2=None, op0=ALU.mult), rd=[Ssb[j]], wr=[Ssb[j]])
                        k.op(k.dve, lambda e: e.scalar_tensor_tensor(out=sv, in0=pu[:, 0:512], scalar=g4, in1=sv, op0=ALU.mult, op1=ALU.add),
                             rd=[pub, Ssb[j]], wr=[Ssb[j]])
                        k.dma(k.pool, L["nret_s"][b, hd].rearrange("(h p) e -> p h e", p=128), Ss[j][:], rd=[Ssb[j]])
                k.op(k.dve, lambda e: e.bn_stats(out=st6[0:n, :], in_=po[0:n, 0:256]), rd=[pob], wr=[stb])
                k.op(k.dve, lambda e: e.bn_aggr(out=mv[0:n, :], in_=st6[0:n, :]), rd=[stb], wr=[stb])
                k.op(k.dve, lambda e: e.tensor_scalar(out=rs[0:n, :], in0=mv[0:n, 1:2], scalar1=1e-5, scalar2=None, op0=ALU.add), rd=[stb], wr=[stb])
                k.op(k.act, lambda e: e.activation(out=rs[0:n, :], in_=rs[0:n, :], func=AF.Sqrt), rd=[stb], wr=[stb])
                k.op(k.dve, lambda e: e.reciprocal(out=rs[0:n, :], in_=rs[0:n, :]), rd=[stb], wr=[stb])
                k.op(k.dve, lambda e: e.tensor_scalar(out=yf[0:n, :], in0=po[0:n, 0:256], scalar1=mv[0:n, 0:1], scalar2=rs[0:n, 0:1],
                                                      op0=ALU.subtract, op1=ALU.mult), rd=[pob, stb], wr=[yfb])
                k.op(k.dve, lambda e: e.tensor_tensor(out=ybf[0:n, :], in0=yf[0:n, :], in1=sgt[0:n, ti, :], op=ALU.mult), rd=[yfb, gb], wr=[ybb])
                pt, ptb = k.psum()
                ptv = pt[:].bitcast(BF16)
                for h in range(2):
                    k.op(k.pe, lambda e: e.transpose(ptv[:, h * 128:h * 128 + n], ybf[0:n, h * 128:(h + 1) * 128], identb[0:n, 0:n]),
                         rd=[ybb, B["const"]], wr=[ptb])
                yi = ti % 2
                k.op(k.act, lambda e: e.activation(out=yTt[yi][:, :, 0:n], in_=ptv[:, 0:256].rearrange("p (h t) -> p h t", h=2)[:, :, 0:n], func=AF.Copy),
                     rd=[ptb], wr=[yTb[yi]])
                k.dma(k.pool, L["ymix_d"][2 * hd:2 * hd + 2, :, t0:t0 + n].rearrange("h p t -> p h t"), yTt[yi][:, :, 0:n], rd=[yTb[yi]])
                if not smp:
                    pt2, pt2b = k.psum()
                    ptv2 = pt2[:].bitcast(BF16)
                    for h in range(2):
                        k.op(k.pe, lambda e: e.transpose(ptv2[:, h * 128:(h + 1) * 128], kT[:, h, t0:t0 + 128], identb[:]), rd=[kb, B["const"]], wr=[pt2b])
                    k.op(k.act, lambda e: e.activation(out=ktm[:, :], in_=ptv2[:, 0:256], func=AF.Copy), rd=[pt2b], wr=[ktb])
                    pu, pub = k.psum()
                    for h in range(2):
                        k.op(k.pe, lambda e: e.matmul(pu[:, h * 256:(h + 1) * 256], lhsT=ktm[:, h * 128:(h + 1) * 128], rhs=vtm[:, ti, :], start=True, stop=True),
                             rd=[ktb, vb], wr=[pub])
                    sv = S[:].rearrange("p h e -> p (h e)")
                    k.op(k.dve, lambda e: e.tensor_scalar(out=sv, in0=sv, scalar1=gC, scalar2=None, op0=ALU.mult), rd=[Sbuf], wr=[Sbuf])
                    k.op(k.dve, lambda e: e.scalar_tensor_tensor(out=sv, in0=pu[:, 0:512], scalar=gC, in1=sv, op0=ALU.mult, op1=ALU.add),
                         rd=[pub, Sbuf], wr=[Sbuf])
                    k.op(k.act, lambda e: e.activation(out=Sb[:], in_=S[:], func=AF.Copy), rd=[Sbuf], wr=[Sbuf])
                    if ti == 15:
                        k.dma(k.pool, L["nret_p"][hd].rearrange("(h p) e -> p h e", p=128), S[:], rd=[Sbuf])
        k.barrier()


def phase_out(nc, k, sb, L):
    B, modT, identf = L["B"], L["modT"], L["identf"]
    with ExitStack() as ph:
        grow = sb("grow", [17, D], F32, ph); grb = Buf()
        selp = sb("selp_s", [17, 128], F32, ph); sels = sb("sels_s", [17, 64], F32, ph); fnw = sb("fnw", [128, D], F32, ph); cb = Buf()
        k.dma(k.sp, selp[:], L["selp"], wr=[cb]); k.dma(k.sp, sels[:], L["sels"], wr=[cb]); k.dma(k.sp, fnw[:], L["fnw_b"], wr=[cb])
        for c in range(KC):
            ps, pb = k.psum()
            k.op(k.pe, lambda e: e.transpose(ps[0:17, 0:128], modT[:, 32 + c, :], identf[:]), rd=[B["modT"], B["const"]], wr=[pb])
            k.op(k.dve, lambda e: e.tensor_copy(out=grow[:, c * 128:(c + 1) * 128], in_=ps[0:17, 0:128]), rd=[pb], wr=[grb])
        wo = sb("wo", [128, 32, D], BF16, ph); wob = [Buf() for _ in range(32)]
        wst = [sb("wost%d" % i, [128, 4, 256], F32, ph) for i in range(2)]; wstb = [Buf(), Buf()]
        nld = 0
        for fc4 in range(8):
            for cg in range(8):
                i = nld % 2
                nld += 1
                k.dma(k.sp, wst[i][:], L["w_out"][fc4 * 512:(fc4 + 1) * 512, cg * 256:(cg + 1) * 256].rearrange("(c p) n -> p c n", p=128), wr=[wstb[i]])
                dstw = wo[:, fc4 * 4:(fc4 + 1) * 4, cg * 256:(cg + 1) * 256]
                wbufs = [wob[fc4 * 4 + q] for q in range(4)]
                if nld % 2:
                    k.op(k.act, lambda e: e.activation(out=dstw, in_=wst[i][:], func=AF.Copy), rd=[wstb[i]], wr=wbufs)
                else:
                    k.op(k.dve, lambda e: e.tensor_copy(out=dstw, in_=wst[i][:]), rd=[wstb[i]], wr=wbufs)
        ym = sb("ym", [128, 32, 128], BF16, ph); ymb = Buf()
        xr = [sb("xr%d" % i, [128, D], F32, ph) for i in range(2)]; xrb = [Buf(), Buf()]
        G = [sb("Gt%d" % i, [128, 512], F32, ph) for i in range(2)]; Gb = [Buf(), Buf()]
        junk = sb("junk", [128, D], BF16, ph); jb = Buf()
        ssq = sb("ssq", [128, 1], F32, ph); sqb = Buf()
        tiles = [(i * 128, 128, False) for i in range(16)] + [(T, 64, True)]
        ng = 0
        for ti, (t0, n, smp) in enumerate(tiles):
            i = ti % 2
            k.dma(k.sp, ym[:, :, 0:n], L["ymix_d"][:, :, t0:t0 + n].rearrange("c p t -> p c t"), wr=[ymb])
            k.dma(k.sp, xr[i][0:n, :], L["xs"][0:n, :] if smp else L["x"][t0:t0 + n, :], wr=[xrb[i]])
            sel = sels if smp else selp
            for cg in range(4):
                c0 = cg * 512
                gi = ng % 2
                ng += 1
                pg, pgb = k.psum()
                k.op(k.pe, lambda e: e.matmul(pg[0:n, 0:512], lhsT=sel[:, 0:n], rhs=grow[:, c0:c0 + 512], start=True, stop=True), rd=[cb, grb], wr=[pgb])
                k.op(k.act, lambda e: e.activation(out=G[gi][0:n, :], in_=pg[0:n, 0:512], func=AF.Copy), rd=[pgb], wr=[Gb[gi]])
                ps, pb = k.psum()
                for fc in range(32):
                    k.op(k.pe, lambda e: e.matmul(ps[0:n, 0:512], lhsT=ym[:, fc, 0:n], rhs=wo[:, fc, c0:c0 + 512], start=(fc == 0), stop=(fc == 31)),
                         rd=[ymb, wob[fc]], wr=[pb])
                k.op(k.dve, lambda e: e.tensor_tensor(out=G[gi][0:n, :], in0=ps[0:n, 0:512], in1=G[gi][0:n, :], op=ALU.mult), rd=[pb, Gb[gi]], wr=[Gb[gi]])
                k.op(k.dve, lambda e: e.tensor_tensor(out=xr[i][0:n, c0:c0 + 512], in0=G[gi][0:n, :], in1=xr[i][0:n, c0:c0 + 512], op=ALU.add), rd=[Gb[gi], xrb[i]], wr=[xrb[i]])
            k.op(k.act, lambda e: e.activation(out=junk[0:n, :], in_=xr[i][0:n, :], func=AF.Square, accum_out=ssq[0:n, :]), rd=[xrb[i]], wr=[jb, sqb])
            k.op(k.dve, lambda e: e.tensor_scalar(out=ssq[0:n, :], in0=ssq[0:n, :], scalar1=1.0 / D, scalar2=1e-6, op0=ALU.mult, op1=ALU.add), rd=[sqb], wr=[sqb])
            k.op(k.act, lambda e: e.activation(out=ssq[0:n, :], in_=ssq[0:n, :], func=AF.Sqrt), rd=[sqb], wr=[sqb])
            k.op(k.dve, lambda e: e.reciprocal(out=ssq[0:n, :], in_=ssq[0:n, :]), rd=[sqb], wr=[sqb])
            k.op(k.dve, lambda e: e.scalar_tensor_tensor(out=xr[i][0:n, :], in0=xr[i][0:n, :], scalar=ssq[0:n, 0:1], in1=fnw[0:n, :], op0=ALU.mult, op1=ALU.mult),
                 rd=[xrb[i], sqb, cb], wr=[xrb[i]])
            dst = L["ys"][0:n, :] if smp else L["y"][t0:t0 + n, :]
            k.dma(k.pool, dst, xr[i][0:n, :], rd=[xrb[i]])
        k.barrier()


def phase_rwkv(nc, k, sb, L):
    hT, hTs, B = L["hT"], L["hTs"], L["B"]
    cm, smk, smkT, identb, onesf, rw, mul = L["cm"], L["smk"], L["smkT"], L["identb"], L["onesf"], L["rw"], L["mul"]
    TA = T + TSM
    EM = math.exp(-0.5)
    with ExitStack() as ph:
        ws = WStream(nc, k, sb, ph, L["w_in"], hT, hTs, B["hT"], B["hTs"], alloc=False)
        cB = B["const"]
        rw2 = sb("rw2", [128, 10, 16], F32, ph); bones = sb("bones", [128, 128], F32, ph)
        k.dma(k.sp, rw2[:], L["rwp2"], wr=[cB]); k.dma(k.sp, bones[:], L["bones"], wr=[cB])
        omm = sb("omm", [128, 3, 16], F32, ph); omml = sb("omml", [96, 2], F32, ph)
        k.op(k.dve, lambda e: e.tensor_scalar(out=omm[:], in0=rw2[:, 0:3, :], scalar1=-1.0, scalar2=1.0, op0=ALU.mult, op1=ALU.add), rd=[cB], wr=[cB])
        k.op(k.dve, lambda e: e.tensor_scalar(out=omml[:], in0=mul[:], scalar1=-1.0, scalar2=1.0, op0=ALU.mult, op1=ALU.add), rd=[cB], wr=[cB])
        w2b = sb("w2b", [96, D], BF16, ph); a2b = sb("a2b", [96, D], BF16, ph)
        with ExitStack() as p2:
            st = sb("lst", [96, D], F32, p2); stb = Buf()
            for src, dst in ((L["w2d"], w2b), (L["a2"], a2b)):
                k.dma(k.sp, st[:], src, wr=[stb])
                k.op(k.dve, lambda e: e.tensor_copy(out=dst[:], in_=st[:]), rd=[stb], wr=[cB])
            k.barrier()
        tanhwd = sb("tanhwd", [96, TA], BF16, ph); adm = sb("adm", [96, TA], BF16, ph); lb = Buf()
        with ExitStack() as p2:
            cur = sb("lcur", [96, 2, 1 + T], F32, p2); curs = sb("lcurs", [96, 2, 80], F32, p2); prv = sb("lprv", [96, 64], F32, p2)
            tmpl = sb("ltmp", [96, T], F32, p2); cb_ = Buf(); tb_ = Buf()
            lst_ = sb("lwst", [128, KC, 192], F32, p2); wt = sb("lwb", [128, KC, 192], BF16, p2); wtb = Buf()
            k.dma(k.sp, lst_[:], L["w_in"][:, 16384:16576].rearrange("(c p) n -> p c n", p=128), wr=[wtb])
            k.op(k.pool, lambda e: e.tensor_copy(out=wt[:], in_=lst_[:]), rd=[wtb], wr=[wtb])
            k.op(k.dve, lambda e: e.memset(cur[:, :, 0:1], 0.0), wr=[cb_])
            for which in range(2):
                for tg in range(4):
                    ps, pb = ws.fm(wt, wtb, which * 96, 96, tg * 512, 512)
                    k.op(k.act, lambda e: e.activation(out=cur[:, which, 1 + tg * 512:1 + (tg + 1) * 512], in_=ps[0:96, 0:512], func=AF.Copy), rd=[pb], wr=[cb_])
                ps, pb = ws.fm(wt, wtb, which * 96, 96, 0, 80, True)
                k.op(k.act, lambda e: e.activation(out=curs[:, which, :], in_=ps[0:96, 0:80], func=AF.Copy), rd=[pb], wr=[cb_])
                dst = tanhwd if which == 0 else adm
                fn = AF.Tanh if which == 0 else AF.Copy
                k.op(k.dve, lambda e: e.tensor_scalar(out=tmpl[:, 0:T], in0=cur[:, which, 0:T], scalar1=mul[:, which:which + 1], scalar2=None, op0=ALU.mult), rd=[cb_, cB], wr=[tb_])
                k.op(k.dve, lambda e: e.scalar_tensor_tensor(out=tmpl[:, 0:T], in0=cur[:, which, 1:1 + T], scalar=omml[:, which:which + 1], in1=tmpl[:, 0:T], op0=ALU.mult, op1=ALU.add),
                     rd=[cb_, cB, tb_], wr=[tb_])
                k.op(k.act, lambda e: e.activation(out=dst[:, 0:T], in_=tmpl[:, 0:T], func=fn), rd=[tb_], wr=[lb])
                c3 = curs[:, which, 0:64].rearrange("p (s j) -> p s j", j=4); p3 = prv[:].rearrange("p (s j) -> p s j", j=4)
                k.op(k.dve, lambda e: e.tensor_copy(out=p3[:, :, 1:4], in_=c3[:, :, 0:3]), rd=[cb_], wr=[tb_])
                k.op(k.dve, lambda e: e.tensor_copy(out=p3[:, :, 0], in_=curs[:, which, 64:80]), rd=[cb_], wr=[tb_])
                k.op(k.dve, lambda e: e.tensor_scalar(out=prv[:], in0=prv[:], scalar1=mul[:, which:which + 1], scalar2=None, op0=ALU.mult), rd=[tb_, cB], wr=[tb_])
                k.op(k.dve, lambda e: e.scalar_tensor_tensor(out=prv[:], in0=curs[:, which, 0:64], scalar=omml[:, which:which + 1], in1=prv[:], op0=ALU.mult, op1=ALU.add),
                     rd=[cb_, cB, tb_], wr=[tb_])
                k.op(k.act, lambda e: e.activation(out=dst[:, T:TA], in_=prv[:], func=fn), rd=[tb_], wr=[lb])
            k.barrier()
        NB = 256
        F = {n: sb("rf_" + n, [128, 1 + NB], F32, ph) for n in ["cr", "ck", "cv", "rm", "km", "vm", "t7", "W", "Wp"]}
        Z = {n: Buf() for n in ["cr", "ck", "cv", "rm", "km", "vm", "t7", "W", "Wp", "al", "be", "kt", "vb", "mp", "pwA", "pwB", "car", "wb",
                                "Ubp", "oT", "t7c", "y", "S", "S0s", "msk"]}
        for a_, b_ in {"w": "cr", "a": "ck", "kk": "cv"}.items():
            F[a_] = F[b_]; Z[a_] = Z[b_]
        HS = []
        for par2 in range(2):
            hb_ = {n: sb("rb_" + n, [128, NB], BF16, ph) for n in ["al", "be", "kt", "vb"]}
            hbp_ = {n: [sb("rbp_%s%d" % (n, e_), [128, NB], BF16, ph) for e_ in range(2)] for n in ["al", "be", "kt"]}
            hz_ = {n: Buf() for n in ["al", "be", "kt", "vb"]}
            for n_ in ["al", "be", "kt"]:
                for e_ in range(2):
                    hz_[n_ + "P%d" % e_] = Buf()
                    k.op(k.pool, lambda e: e.memset(hbp_[n_][e_][:], 0.0), wr=[hz_[n_ + "P%d" % e_]])
            HS.append(dict(Hb=hb_, HbP=hbp_, Z=hz_))
        mp = sb("mats_pre", [128, 3, 4, 128], BF16, ph)
        pw = [sb("pw%d" % i, [128, 2, 4, 128], BF16, ph) for i in range(2)]
        car = sb("car", [128, 3], F32, ph)
        wb = sb("wbp", [128, KC, 512], BF16, ph)
        stg4 = [sb("stg4_%d" % i, [128, KC, 64], F32, ph) for i in range(2)]; stg4b = [Buf(), Buf()]
        CS = []
        for par in range(3):
            d = dict(P=sb("Pm", [128, 4, 128], BF16, ph), m34=sb("m34", [128, 2, 4, 128], BF16, ph), tp=sb("tmpad", [128, 2, 4, 192], BF16, ph),
                     ah=sb("ah", [128, 2, 128], BF16, ph), Yb=sb("Yb", [128, 2, 2, 64], BF16, ph), wc=sb("wc", [128, 16], F32, ph),
                     bv=sb("bv", [128, NB], F32, ph), sg=sb("sg", [128, NB], BF16, ph), rt=sb("rt", [128, NB], BF16, ph))
            d["Z"] = {n: Buf() for n in ["P", "m34", "tp", "ah", "Yb", "wc", "bv", "sg", "rt"]}
            k.op(k.dve, lambda e: e.memset(d["tp"][:], 0.0), wr=[d["Z"]["tp"]])
            CS.append(d)
        Ubp = sb("Ubp", [128, 192], BF16, ph)
        k.op(k.dve, lambda e: e.memset(Ubp[:], 0.0), wr=[Z["Ubp"]])
        oT = sb("oT", [128, NB], F32, ph); t7c = sb("t7c", [128, NB], F32, ph); yb = sb("ybf", [128, NB], BF16, ph)
        S = sb("Srw", [128, 64], F32, ph); Sbd = sb("Sbd", [128, 128], BF16, ph)
        S0s = sb("S0s", [128, NS, 64], F32, ph); Sbds = sb("Sbds", [128, NS, 128], BF16, ph)
        ahm = sb("ahm", [128, NS, 64], BF16, ph); rtm = sb("rtm", [128, NS, 64], BF16, ph)
        Ublk = sb("Ublk", [64, 2, NS, 64], BF16, ph); Vblk = sb("Vblk2", [64, 2, NS, 64], BF16, ph)
        k.op(k.dve, lambda e: e.memset(Sbds[:], 0.0), wr=[Z["S0s"]])
        nstg = [0]

        def load_w(pi):
            for q in range(8):
                si = nstg[0] % 2
                nstg[0] += 1
                c0 = 8192 + (q // 2) * D + pi * 128 + (q % 2) * 64
                k.dma(k.sp, stg4[si][:], L["w_in"][:, c0:c0 + 64].rearrange("(c p) n -> p c n", p=128), wr=[stg4b[si]])
                if q % 2 == 0:
                    k.op(k.dve, lambda e: e.tensor_copy(out=wb[:, :, q * 64:(q + 1) * 64], in_=stg4[si][:]), rd=[stg4b[si]], wr=[Z["wb"]])
                else:
                    k.op(k.act, lambda e: e.activation(out=wb[:, :, q * 64:(q + 1) * 64], in_=stg4[si][:], func=AF.Copy), rd=[stg4b[si]], wr=[Z["wb"]])

        def prepA(pi, t0, n, smp, par, par2):
            C = CS[par]; CZ = C["Z"]; H = HS[par2]; Hb = H["Hb"]; HbP = H["HbP"]; HZ = H["Z"]
            TT = 64 if smp else 128
            ntile = n // TT
            NM = 2 * ntile
            pc = lambda j: rw2[:, j, pi:pi + 1]
            lc = T if smp else t0
            if t0 == 0 and not smp:
                load_w(pi)
                k.op(k.dve, lambda e: e.memset(car[:], 0.0), wr=[Z["car"]])
            for qi, cn in enumerate(("cr", "ck", "cv")):
                mn = ("rm", "km", "vm")[qi]
                if smp:
                    ps, pb = ws.fm(wb, Z["wb"], qi * 128, 128, 0, 80, True)
                    k.op(k.act, lambda e: e.activation(out=F[cn][:, 1:81], in_=ps[:, 0:80], func=AF.Copy), rd=[pb], wr=[Z[cn]])
                    c3 = F[cn][:, 1:65].rearrange("p (s j) -> p s j", j=4); p3 = F["t7"][:, 0:64].rearrange("p (s j) -> p s j", j=4)
                    k.op(k.dve, lambda e: e.tensor_copy(out=p3[:, :, 1:4], in_=c3[:, :, 0:3]), rd=[Z[cn]], wr=[Z["t7"]])
                    k.op(k.dve, lambda e: e.tensor_copy(out=p3[:, :, 0], in_=F[cn][:, 65:81]), rd=[Z[cn]], wr=[Z["t7"]])
                    k.op(k.dve, lambda e: e.tensor_scalar(out=F["t7"][:, 0:n], in0=F["t7"][:, 0:n], scalar1=pc(qi), scalar2=None, op0=ALU.mult), rd=[Z["t7"], cB], wr=[Z["t7"]])
                else:
                    k.op(k.dve, lambda e: e.tensor_copy(out=F[cn][:, 0:1], in_=car[:, qi:qi + 1]), rd=[Z["car"]], wr=[Z[cn]])
                    ps, pb = yield from ws.fm_gen(wb, Z["wb"], qi * 128, 128, t0, n)
                    k.op(k.act, lambda e: e.activation(out=F[cn][:, 1:1 + n], in_=ps[:, 0:n], func=AF.Copy), rd=[pb], wr=[Z[cn]])
                    k.op(k.dve, lambda e: e.tensor_copy(out=car[:, qi:qi + 1], in_=F[cn][:, n:n + 1]), rd=[Z[cn]], wr=[Z["car"]])
                    k.op(k.dve, lambda e: e.tensor_scalar(out=F["t7"][:, 0:n], in0=F[cn][:, 0:n], scalar1=pc(qi), scalar2=None, op0=ALU.mult), rd=[Z[cn], cB], wr=[Z["t7"]])
                k.op(k.dve, lambda e: e.scalar_tensor_tensor(out=F[mn][:, 0:n], in0=F[cn][:, 1:1 + n], scalar=omm[:, qi, pi:pi + 1], in1=F["t7"][:, 0:n], op0=ALU.mult, op1=ALU.add),
                     rd=[Z[cn], Z["t7"], cB], wr=[Z[mn]])
                yield
            ps, pb = yield from ws.fm_gen(wb, Z["wb"], 384, 128, 0 if smp else t0, n, smp)
            k.op(k.act, lambda e: e.activation(out=C["sg"][:, 0:n], in_=ps[:, 0:n], func=AF.Silu), rd=[pb], wr=[CZ["sg"]])
            ps, pb = k.psum()
            k.op(k.pe, lambda e: e.matmul(ps[:, 0:n], lhsT=w2b[:, pi * 128:(pi + 1) * 128], rhs=tanhwd[:, lc:lc + n], start=True, stop=True), rd=[cB, lb], wr=[pb])
            k.op(k.act, lambda e: e.activation(out=F["w"][:, 0:n], in_=ps[:, 0:n], func=AF.Sigmoid, bias=pc(3)), rd=[pb, cB], wr=[Z["w"]])
            ps, pb = k.psum()
            k.op(k.pe, lambda e: e.matmul(ps[:, 0:n], lhsT=a2b[:, pi * 128:(pi + 1) * 128], rhs=adm[:, lc:lc + n], start=True, stop=True), rd=[cB, lb], wr=[pb])
            k.op(k.act, lambda e: e.activation(out=F["a"][:, 0:n], in_=ps[:, 0:n], func=AF.Sigmoid, bias=pc(4)), rd=[pb, cB], wr=[Z["a"]])
            k.op(k.act, lambda e: e.activation(out=F["w"][:, 0:n], in_=F["w"][:, 0:n], func=AF.Exp, scale=-EM), rd=[Z["w"]], wr=[Z["w"]])
            yield
            k.op(k.dve, lambda e: e.tensor_scalar(out=F["kk"][:, 0:n], in0=F["km"][:, 0:n], scalar1=pc(5), scalar2=None, op0=ALU.mult), rd=[Z["km"], cB], wr=[Z["kk"]])
            k.op(k.dve, lambda e: e.tensor_tensor(out=F["t7"][:, 0:n], in0=F["kk"][:, 0:n], in1=F["kk"][:, 0:n], op=ALU.mult), rd=[Z["kk"]], wr=[Z["t7"]])
            ps, pb = k.psum()
            k.op(k.pe, lambda e: e.matmul(ps[:, 0:n], lhsT=bones[:, :], rhs=F["t7"][:, 0:n], start=True, stop=True), rd=[cB, Z["t7"]], wr=[pb])
            k.op(k.act, lambda e: e.activation(out=F["Wp"][:, 0:n], in_=ps[:, 0:n], func=AF.Sqrt), rd=[pb], wr=[Z["Wp"]])
            k.op(k.dve, lambda e: e.tensor_scalar(out=F["Wp"][:, 0:n], in0=F["Wp"][:, 0:n], scalar1=1e-12, scalar2=None, op0=ALU.max), rd=[Z["Wp"]], wr=[Z["Wp"]])
            k.op(k.dve, lambda e: e.reciprocal(out=F["Wp"][:, 0:n], in_=F["Wp"][:, 0:n]), rd=[Z["Wp"]], wr=[Z["Wp"]])
            k.op(k.dve, lambda e: e.tensor_tensor(out=F["kk"][:, 0:n], in0=F["kk"][:, 0:n], in1=F["Wp"][:, 0:n], op=ALU.mult), rd=[Z["kk"], Z["Wp"]], wr=[Z["kk"]])
            yield
            k.op(k.dve, lambda e: e.tensor_scalar(out=F["t7"][:, 0:n], in0=F["a"][:, 0:n], scalar1=pc(6), scalar2=pc(6), op0=ALU.mult, op1=ALU.subtract), rd=[Z["a"], cB], wr=[Z["t7"]])
            k.op(k.dve, lambda e: e.scalar_tensor_tensor(out=F["km"][:, 0:n], in0=F["t7"][:, 0:n], scalar=1.0, in1=F["km"][:, 0:n], op0=ALU.add, op1=ALU.mult),
                 rd=[Z["t7"], Z["km"]], wr=[Z["km"]])
            CL = 4 if smp else 128
            for c0 in range(0, n, CL):
                k.op(k.dve, lambda e: e.tensor_tensor_scan(out=F["W"][:, c0:c0 + CL], data0=F["w"][:, c0:c0 + CL], data1=F["w"][:, c0:c0 + CL], initial=1.0, op0=ALU.mult, op1=ALU.bypass),
                     rd=[Z["w"]], wr=[Z["W"]])
            W3 = F["W"][:, 0:n].rearrange("p (c j) -> p c j", j=CL); Wp3 = F["Wp"][:, 0:n].rearrange("p (c j) -> p c j", j=CL)
            k.op(k.dve, lambda e: e.tensor_copy(out=Wp3[:, :, 1:CL], in_=W3[:, :, 0:CL - 1]), rd=[Z["W"]], wr=[Z["Wp"]])
            k.op(k.dve, lambda e: e.memset(Wp3[:, :, 0:1], 1.0), wr=[Z["Wp"]])
            k.op(k.dve, lambda e: e.tensor_copy(out=C["wc"][:, 0:n // CL], in_=W3[:, :, CL - 1]), rd=[Z["W"]], wr=[CZ["wc"]])
            yield
            k.op(k.dve, lambda e: e.scalar_tensor_tensor(out=Hb["al"][:, 0:n], in0=F["kk"][:, 0:n], scalar=-1.0, in1=F["Wp"][:, 0:n], op0=ALU.mult, op1=ALU.mult), rd=[Z["kk"], Z["Wp"]], wr=[HZ["al"]])
            k.op(k.dve, lambda e: e.tensor_tensor(out=C["rt"][:, 0:n], in0=F["rm"][:, 0:n], in1=F["W"][:, 0:n], op=ALU.mult), rd=[Z["rm"], Z["W"]], wr=[CZ["rt"]])
            k.op(k.dve, lambda e: e.tensor_tensor(out=F["t7"][:, 0:n], in0=F["kk"][:, 0:n], in1=F["a"][:, 0:n], op=ALU.mult), rd=[Z["kk"], Z["a"]], wr=[Z["t7"]])
            k.op(k.dve, lambda e: e.reciprocal(out=F["W"][:, 0:n], in_=F["W"][:, 0:n]), rd=[Z["W"]], wr=[Z["W"]])
            k.op(k.dve, lambda e: e.tensor_tensor(out=Hb["be"][:, 0:n], in0=F["t7"][:, 0:n], in1=F["W"][:, 0:n], op=ALU.mult), rd=[Z["t7"], Z["W"]], wr=[HZ["be"]])
            k.op(k.dve, lambda e: e.tensor_tensor(out=Hb["kt"][:, 0:n], in0=F["km"][:, 0:n], in1=F["W"][:, 0:n], op=ALU.mult), rd=[Z["km"], Z["W"]], wr=[HZ["kt"]])
            k.op(k.act, lambda e: e.activation(out=Hb["vb"][:, 0:n], in_=F["vm"][:, 0:n], func=AF.Copy), rd=[Z["vm"]], wr=[HZ["vb"]])
            yield
            k.op(k.dve, lambda e: e.scalar_tensor_tensor(out=F["t7"][:, 0:n], in0=F["rm"][:, 0:n], scalar=pc(7), in1=F["km"][:, 0:n], op0=ALU.mult, op1=ALU.mult), rd=[Z["rm"], Z["km"], cB], wr=[Z["t7"]])
            ps, pb = k.psum()
            k.op(k.pe, lambda e: e.matmul(ps[:, 0:n], lhsT=bones[:, :], rhs=F["t7"][:, 0:n], start=True, stop=True), rd=[cB, Z["t7"]], wr=[pb])
            k.op(k.dve, lambda e: e.tensor_tensor(out=C["bv"][:, 0:n], in0=ps[:, 0:n], in1=F["vm"][:, 0:n], op=ALU.mult), rd=[pb, Z["vm"]], wr=[CZ["bv"]])
            yield
            for n_ in ("al", "be", "kt"):
                k.op(k.act, lambda e: e.activation(out=HbP[n_][0][0:64, 0:n], in_=Hb[n_][0:64, 0:n], func=AF.Copy), rd=[HZ[n_]], wr=[HZ[n_ + "P0"]])
                k.op(k.act, lambda e: e.activation(out=HbP[n_][1][64:128, 0:n], in_=Hb[n_][64:128, 0:n], func=AF.Copy), rd=[HZ[n_]], wr=[HZ[n_ + "P1"]])
            yield
        def prepB(pi, t0, n, smp, par, par2):
            C = CS[par]; CZ = C["Z"]; H = HS[par2]; Hb = H["Hb"]; HbP = H["HbP"]; HZ = H["Z"]
            TT = 64 if smp else 128
            ntile = n // TT
            NM = 2 * ntile
            for ti in range(ntile):
                sl = slice(ti * TT, (ti + 1) * TT)
                pt, ptb = k.psum()
                ptv = pt[:].bitcast(BF16)
                for qi, nm in enumerate(("al", "be", "kt", "vb")):
                    k.op(k.pe, lambda e: e.transpose(ptv[0:TT, qi * 128:(qi + 1) * 128], Hb[nm][:, sl], identb[:, :]), rd=[HZ[nm], cB], wr=[ptb])
                pv = ptv[0:TT, 0:512].rearrange("p (q c) -> p q c", c=128)
                k.op(k.dve, lambda e: e.tensor_copy(out=C["tp"][0:TT, ti, :, 0:64], in_=pv[:, :, 0:64]), rd=[ptb], wr=[CZ["tp"]])
                k.op(k.act, lambda e: e.activation(out=C["tp"][0:TT, ti, :, 128:192], in_=pv[:, :, 64:128], func=AF.Copy), rd=[ptb], wr=[CZ["tp"]])
            yield
            m1, m2, m0 = (4, 5, 3) if smp else (1, 2, 0)
            specs = [("al", "be", m1), ("be", "al", m2), ("kt", "al", m2), ("be", "rt", m0), ("kt", "rt", m0)]
            banks = [k.psum() for _ in range(5)]
            for ti in range(ntile):
                sl = slice(ti * TT, (ti + 1) * TT)
                for e_ in range(2):
                    m = ti * 2 + e_
                    for si_, (ln, rn, mk) in enumerate(specs):
                        lt = HbP[ln][e_][:, sl]
                        rtn = C["rt"][:, sl] if rn == "rt" else Hb[rn][:, sl]
                        rb_ = CZ["rt"] if rn == "rt" else HZ[rn]
                        k.op(k.pe, lambda e: e.matmul(banks[si_][0][0:TT, m * 128:m * 128 + TT], lhsT=lt, rhs=rtn, start=True, stop=True), rd=[HZ[ln + "P%d" % e_], rb_], wr=[banks[si_][1]])
            for si_, (ln, rn, mk) in enumerate(specs):
                dst = mp[0:TT, si_, 0:NM, 0:TT] if si_ < 3 else C["m34"][0:TT, si_ - 3, 0:NM, 0:TT]
                db_ = Z["mp"] if si_ < 3 else CZ["m34"]
                src_ = banks[si_][0][0:TT, 0:NM * 128].rearrange("p (m c) -> p m c", c=128)[:, :, 0:TT]
                k.op(k.dve, lambda e: e.tensor_tensor(out=dst, in0=src_, in1=cm[0:TT, mk, 0:TT].unsqueeze(1).to_broadcast([TT, NM, TT]), op=ALU.mult), rd=[banks[si_][1], cB], wr=[db_])
            yield
            P = C["P"]
            k.op(k.dve, lambda e: e.tensor_tensor(out=P[0:TT, 0:NM, 0:TT], in0=mp[0:TT, 1, 0:NM, 0:TT], in1=identb[0:TT, 0:TT].unsqueeze(1).to_broadcast([TT, NM, TT]), op=ALU.add),
                 rd=[Z["mp"], cB], wr=[CZ["P"]])
            cA = lambda m: mp[0:TT, 0, m, 0:TT]
            cBm = lambda m: mp[0:TT, 1, m, 0:TT]
            nlev = 2 if smp else 6
            for lv in range(nlev):
                pqa, pqab = k.psum(); pqb, pqbb = k.psum()
                for m in range(NM):
                    k.op(k.pe, lambda e: e.matmul(pqb[0:TT, m * 128:m * 128 + TT], lhsT=cA(m), rhs=cBm(m), start=True, stop=True), rd=[Z["mp"], Z["pwA"], Z["pwB"]], wr=[pqbb])
                    k.op(k.pe, lambda e: e.matmul(pqa[0:TT, m * 128:m * 128 + TT], lhsT=cBm(m), rhs=cA(m), start=True, stop=True), rd=[Z["mp"], Z["pwA"], Z["pwB"]], wr=[pqab])
                nw = pw[lv % 2]
                va = pqa[0:TT, 0:NM * 128].rearrange("p (m c) -> p m c", c=128)[:, :, 0:TT]
                vb_ = pqb[0:TT, 0:NM * 128].rearrange("p (m c) -> p m c", c=128)[:, :, 0:TT]
                k.op(k.act, lambda e: e.activation(out=nw[0:TT, 0, 0:NM, 0:TT], in_=va, func=AF.Copy), rd=[pqab], wr=[Z["pwA"]])
                k.op(k.dve, lambda e: e.tensor_copy(out=nw[0:TT, 1, 0:NM, 0:TT], in_=vb_), rd=[pqbb], wr=[Z["pwB"]])
                cA = lambda m, nw=nw: nw[0:TT, 0, m, 0:TT]
                cBm = lambda m, nw=nw: nw[0:TT, 1, m, 0:TT]
                yield
                pr, prb = k.psum()
                for m in range(NM):
                    k.op(k.pe, lambda e: e.matmul(pr[0:TT, m * 128:m * 128 + TT], lhsT=cA(m), rhs=P[0:TT, m, 0:TT], start=True, stop=True), rd=[Z["pwA"], CZ["P"]], wr=[prb])
                vp = pr[0:TT, 0:NM * 128].rearrange("p (m c) -> p m c", c=128)[:, :, 0:TT]
                k.op(k.dve, lambda e: e.tensor_tensor(out=P[0:TT, 0:NM, 0:TT], in0=vp, in1=P[0:TT, 0:NM, 0:TT], op=ALU.add), rd=[prb, CZ["P"]], wr=[CZ["P"]])
                yield
            pa, pab = k.psum(); py, pyb = k.psum()
            for ti in range(ntile):
                for e_ in range(2):
                    m = ti * 2 + e_
                    k.op(k.pe, lambda e: e.matmul(pa[:, ti * 128:ti * 128 + TT], lhsT=C["tp"][0:TT, ti, 0, e_ * 64:e_ * 64 + 128], rhs=P[0:TT, m, 0:TT], start=(e_ == 0), stop=(e_ == 1)),
                         rd=[CZ["tp"], CZ["P"]], wr=[pab])
                    k.op(k.pe, lambda e: e.matmul(py[0:TT, m * 64:(m + 1) * 64], lhsT=mp[0:TT, 2, m, 0:TT], rhs=C["tp"][0:TT, ti, 3, e_ * 128:e_ * 128 + 64], start=True, stop=True),
                         rd=[Z["mp"], CZ["tp"]], wr=[pyb])
            k.op(k.act, lambda e: e.activation(out=C["ah"][:, 0:ntile, 0:TT], in_=pa[:, 0:ntile * 128].rearrange("p (t c) -> p t c", c=128)[:, :, 0:TT], func=AF.Copy), rd=[pab], wr=[CZ["ah"]])
            k.op(k.dve, lambda e: e.tensor_copy(out=C["Yb"][0:TT, 0:ntile].rearrange("p t e v -> p (t e v)"), in_=py[0:TT, 0:NM * 64]), rd=[pyb], wr=[CZ["Yb"]])
            yield

        def chain(pi, t0, n, smp, par):
            C = CS[par]; CZ = C["Z"]
            TT = 64 if smp else 128
            ntile = n // TT
            P = C["P"]
            pc = lambda j: rw2[:, j, pi:pi + 1]
            if t0 == 0 and not smp:
                k.op(k.dve, lambda e: e.memset(S[:], 0.0), wr=[Z["S"]])
                k.op(k.dve, lambda e: e.memset(Sbd[:], 0.0), wr=[Z["S"]])
            if smp:
                for e_ in range(2):
                    k.dma(k.sp, S0s[64 * e_:64 * e_ + 64], L["srwT"][:, 2 * pi + e_].rearrange("b k v -> k b v"), wr=[Z["S0s"]])
                k.op(k.act, lambda e: e.activation(out=Sbds[0:64, :, 0:64], in_=S0s[0:64, :, :], func=AF.Copy), rd=[Z["S0s"]], wr=[Z["S0s"]])
                k.op(k.dve, lambda e: e.tensor_copy(out=Sbds[64:128, :, 64:128], in_=S0s[64:128, :, :]), rd=[Z["S0s"]], wr=[Z["S0s"]])
                k.op(k.dve, lambda e: e.tensor_tensor(out=ahm[:], in0=C["ah"][:, 0, 0:64].unsqueeze(1).to_broadcast([128, NS, 64]), in1=smkT[:], op=ALU.mult), rd=[CZ["ah"], cB], wr=[Z["msk"]])
                k.op(k.dve, lambda e: e.tensor_tensor(out=rtm[:], in0=C["rt"][:, 0:64].unsqueeze(1).to_broadcast([128, NS, 64]), in1=smkT[:], op=ALU.mult), rd=[CZ["rt"], cB], wr=[Z["msk"]])
            po, pob = k.psum()
            k.held.add(pob)
            for ti in range(ntile):
                sl = slice(ti * TT, (ti + 1) * TT)
                pu, pub = k.psum(exclude=[pob])
                for e_ in range(2):
                    k.op(k.pe, lambda e: e.matmul(pu[0:TT, e_ * 64:(e_ + 1) * 64], lhsT=P[0:TT, ti * 2 + e_, 0:TT], rhs=C["Yb"][0:TT, ti, e_, :], start=(e_ == 0), stop=False,
                                                  skip_group_check=True),
                         rd=[CZ["P"], CZ["Yb"]], wr=[pub])
                if not smp:
                    k.op(k.pe, lambda e: e.matmul(pu[0:TT, 0:128], lhsT=C["ah"][:, ti, 0:TT], rhs=Sbd[:, :], start=False, stop=True, skip_group_check=True), rd=[CZ["ah"], Z["S"]], wr=[pub])
                else:
                    for b in range(NS):
                        k.op(k.pe, lambda e: e.matmul(pu[0:TT, 0:128], lhsT=ahm[:, b, :], rhs=Sbds[:, b, :], start=False, stop=(b == NS - 1), skip_group_check=True), rd=[Z["msk"], Z["S0s"]], wr=[pub])
                k.op(k.dve, lambda e: e.tensor_copy(out=Ubp[0:TT, 0:64], in_=pu[0:TT, 0:64]), rd=[pub], wr=[Z["Ubp"]])
                k.op(k.act, lambda e: e.activation(out=Ubp[0:TT, 128:192], in_=pu[0:TT, 64:128], func=AF.Copy), rd=[pub], wr=[Z["Ubp"]])
                if smp:
                    yield
                oc = slice(ti * TT, (ti + 1) * TT)
                if not smp:
                    k.op(k.pe, lambda e: e.matmul(po[:, oc], lhsT=Sbd[:, :], rhs=C["rt"][:, sl], start=True, stop=False), rd=[Z["S"], CZ["rt"]], wr=[pob])
                else:
                    for b in range(NS):
                        k.op(k.pe, lambda e: e.matmul(po[:, oc], lhsT=Sbds[:, b, :], rhs=rtm[:, b, :], start=(b == 0), stop=False), rd=[Z["S0s"], Z["msk"]], wr=[pob])
                for e_ in range(2):
                    m = ti * 2 + e_
                    k.op(k.pe, lambda e: e.matmul(po[:, oc], lhsT=Ubp[0:TT, e_ * 64:e_ * 64 + 128], rhs=C["m34"][0:TT, 0, m, 0:TT], start=False, stop=False), rd=[Z["Ubp"], CZ["m34"]], wr=[pob])
                    k.op(k.pe, lambda e: e.matmul(po[:, oc], lhsT=C["tp"][0:TT, ti, 3, e_ * 64:e_ * 64 + 128], rhs=C["m34"][0:TT, 1, m, 0:TT], start=False, stop=(e_ == 1)), rd=[CZ["tp"], CZ["m34"]], wr=[pob])
                if smp:
                    yield
                if not smp:
                    pS, pSb = k.psum(exclude=[pob])
                    for e_ in range(2):
                        k.op(k.pe, lambda e: e.matmul(pS[:, 0:64], lhsT=C["tp"][0:TT, ti, 1, e_ * 64:e_ * 64 + 128], rhs=Ubp[0:TT, e_ * 128:e_ * 128 + 64], start=(e_ == 0), stop=False), rd=[CZ["tp"], Z["Ubp"]], wr=[pSb])
                        k.op(k.pe, lambda e: e.matmul(pS[:, 0:64], lhsT=C["tp"][0:TT, ti, 2, e_ * 64:e_ * 64 + 128], rhs=C["tp"][0:TT, ti, 3, e_ * 128:e_ * 128 + 64], start=False, stop=(e_ == 1)), rd=[CZ["tp"]], wr=[pSb])
                    k.op(k.dve, lambda e: e.tensor_tensor(out=S[:, :], in0=pS[:, 0:64], in1=S[:, :], op=ALU.add), rd=[pSb, Z["S"]], wr=[Z["S"]])
                    k.op(k.dve, lambda e: e.tensor_scalar(out=S[:, :], in0=S[:, :], scalar1=C["wc"][:, ti:ti + 1], scalar2=None, op0=ALU.mult), rd=[Z["S"], CZ["wc"]], wr=[Z["S"]])
                    k.op(k.act, lambda e: e.activation(out=Sbd[0:64, 0:64], in_=S[0:64, :], func=AF.Copy), rd=[Z["S"]], wr=[Z["S"]])
                    k.op(k.dve, lambda e: e.tensor_copy(out=Sbd[64:128, 64:128], in_=S[64:128, :]), rd=[Z["S"]], wr=[Z["S"]])
                    yield
                else:
                    for e_ in range(2):
                        k.op(k.dve, lambda e: e.tensor_tensor(out=Ublk[:, e_], in0=Ubp[0:64, e_ * 128:e_ * 128 + 64].unsqueeze(1).to_broadcast([64, NS, 64]), in1=smk[:].unsqueeze(2).to_broadcast([64, NS, 64]), op=ALU.mult), rd=[Z["Ubp"], cB], wr=[Z["msk"]])
                        k.op(k.dve, lambda e: e.tensor_tensor(out=Vblk[:, e_], in0=C["tp"][0:64, 0, 3, e_ * 128:e_ * 128 + 64].unsqueeze(1).to_broadcast([64, NS, 64]), in1=smk[:].unsqueeze(2).to_broadcast([64, NS, 64]), op=ALU.mult), rd=[CZ["tp"], cB], wr=[Z["msk"]])
                    for half in range(2):
                        pS, pSb = k.psum(exclude=[pob])
                        for b8 in range(8):
                            b = half * 8 + b8
                            for e_ in range(2):
                                k.op(k.pe, lambda e: e.matmul(pS[:, b8 * 64:(b8 + 1) * 64], lhsT=C["tp"][0:64, 0, 1, e_ * 64:e_ * 64 + 128], rhs=Ublk[:, e_, b, :], start=(e_ == 0), stop=False), rd=[CZ["tp"], Z["msk"]], wr=[pSb])
                                k.op(k.pe, lambda e: e.matmul(pS[:, b8 * 64:(b8 + 1) * 64], lhsT=C["tp"][0:64, 0, 2, e_ * 64:e_ * 64 + 128], rhs=Vblk[:, e_, b, :], start=False, stop=(e_ == 1)), rd=[CZ["tp"], Z["msk"]], wr=[pSb])
                        sv = S0s[:, half * 8:(half + 1) * 8, :]
                        k.op(k.dve, lambda e: e.tensor_tensor(out=sv, in0=pS[:, 0:512].rearrange("p (b v) -> p b v", v=64), in1=sv, op=ALU.add), rd=[pSb, Z["S0s"]], wr=[Z["S0s"]])
                        k.op(k.dve, lambda e: e.tensor_tensor(out=sv, in0=sv, in1=C["wc"][:, half * 8:(half + 1) * 8].unsqueeze(2).to_broadcast([128, 8, 64]), op=ALU.mult), rd=[Z["S0s"], CZ["wc"]], wr=[Z["S0s"]])
                        yield
                    for e_ in range(2):
                        k.dma(k.pool, L["nrw_sT"][:, 2 * pi + e_].rearrange("b k v -> k b v"), S0s[64 * e_:64 * e_ + 64], rd=[Z["S0s"]])
            k.op(k.act, lambda e: e.activation(out=oT[:, 0:n], in_=po[:, 0:n], func=AF.Copy), rd=[pob], wr=[Z["oT"]])
            k.held.discard(pob)
            ps, pb = k.psum()
            k.op(k.pe, lambda e: e.matmul(ps[:, 0:n], lhsT=bones[:, :], rhs=oT[:, 0:n], start=True, stop=True), rd=[cB, Z["oT"]], wr=[pb])
            k.op(k.dve, lambda e: e.scalar_tensor_tensor(out=oT[:, 0:n], in0=ps[:, 0:n], scalar=-1.0 / 64, in1=oT[:, 0:n], op0=ALU.mult, op1=ALU.add), rd=[pb, Z["oT"]], wr=[Z["oT"]])
            k.op(k.dve, lambda e: e.tensor_tensor(out=t7c[:, 0:n], in0=oT[:, 0:n], in1=oT[:, 0:n], op=ALU.mult), rd=[Z["oT"]], wr=[Z["t7c"]])
            if not smp:
                yield
            ps, pb = k.psum()
            k.op(k.pe, lambda e: e.matmul(ps[:, 0:n], lhsT=bones[:, :], rhs=t7c[:, 0:n], start=True, stop=True), rd=[cB, Z["t7c"]], wr=[pb])
            k.op(k.dve, lambda e: e.tensor_scalar(out=t7c[:, 0:n], in0=ps[:, 0:n], scalar1=1.0 / 64, scalar2=64e-5, op0=ALU.mult, op1=ALU.add), rd=[pb], wr=[Z["t7c"]])
            k.op(k.act, lambda e: e.activation(out=t7c[:, 0:n], in_=t7c[:, 0:n], func=AF.Ln), rd=[Z["t7c"]], wr=[Z["t7c"]])
            k.op(k.act, lambda e: e.activation(out=t7c[:, 0:n], in_=t7c[:, 0:n], func=AF.Exp, scale=-0.5), rd=[Z["t7c"]], wr=[Z["t7c"]])
            k.op(k.dve, lambda e: e.tensor_tensor(out=oT[:, 0:n], in0=oT[:, 0:n], in1=t7c[:, 0:n], op=ALU.mult), rd=[Z["oT"], Z["t7c"]], wr=[Z["oT"]])
            k.op(k.dve, lambda e: e.tensor_scalar(out=oT[:, 0:n], in0=oT[:, 0:n], scalar1=pc(8), scalar2=pc(9), op0=ALU.mult, op1=ALU.add), rd=[Z["oT"], cB], wr=[Z["oT"]])
            k.op(k.dve, lambda e: e.tensor_tensor(out=oT[:, 0:n], in0=oT[:, 0:n], in1=C["bv"][:, 0:n], op=ALU.add), rd=[Z["oT"], CZ["bv"]], wr=[Z["oT"]])
            k.op(k.dve, lambda e: e.tensor_tensor(out=yb[:, 0:n], in0=oT[:, 0:n], in1=C["sg"][:, 0:n], op=ALU.mult), rd=[Z["oT"], CZ["sg"]], wr=[Z["y"]])
            k.dma(k.pool, L["ymix_d"][16 + pi, :, t0:t0 + n], yb[:, 0:n], rd=[Z["y"]])
            if (not smp) and t0 + n == T:
                k.dma(k.pool, L["nrw_pT"][2 * pi:2 * pi + 2].rearrange("e k v -> (e k) v"), S[:, :], rd=[Z["S"]])
            yield

        blocks = []
        for pi in range(RW_PAIRS):
            for bi in range(T // NB):
                blocks.append((pi, bi * NB, NB, False))
            if RW_SAMPLE:
                blocks.append((pi, T, TSM, True))
        nb_ = len(blocks)
        for j in range(nb_ + 2):
            gens = []
            if j < nb_:
                b_ = blocks[j]; gens.append([prepA(b_[0], b_[1], b_[2], b_[3], j % 3, j % 2), 0, 20 if b_[3] else 22])
            if 0 <= j - 1 < nb_:
                b_ = blocks[j - 1]; gens.append([prepB(b_[0], b_[1], b_[2], b_[3], (j - 1) % 3, (j - 1) % 2), 0, 7 if b_[3] else 15])
            if 0 <= j - 2 < nb_:
                b_ = blocks[j - 2]; gens.append([chain(b_[0], b_[1], b_[2], b_[3], (j - 2) % 3), 0, 6 if b_[3] else 5])
            while gens:
                g = min(gens, key=lambda x: (x[1] + 0.5) / x[2])
                try:
                    next(g[0])
                    g[1] += 1
                except StopIteration:
                    gens.remove(g)
        k.barrier()
```
